# Optimizing a Trainium2 kernel written in Bass

```python
import math
import jax
import jax.numpy as jnp
from jax import lax
import numpy as np

D_MODEL = 1024
BATCH = 2
SEQ = 8192
DEPTH = 1
DEC_BATCH = 128
DEC_SEQ = 8
PAST_LEN = 16384
PAGE_SIZE = 128

HEAD_DIM = 64
ATT_HEADS = (D_MODEL // 2) // HEAD_DIM
ATT_KV_HEADS = ATT_HEADS // 4
ATT_GROUP = ATT_HEADS // ATT_KV_HEADS
WINDOW = 128
ATT_BLOCK = WINDOW
ROPE_THETA = 10000.0
GLA_HEADS = 4
GLA_DK = (D_MODEL // 4) // GLA_HEADS
GLA_DV = (D_MODEL // 2) // GLA_HEADS
GLA_RANK = 16
GLA_GATE_TEMP = 16.0
GLA_CHUNK = 16
D_FF = 4 * D_MODEL
ATT_WIDTH = ATT_HEADS * HEAD_DIM
KV_WIDTH = ATT_KV_HEADS * HEAD_DIM
GLA_KW = GLA_HEADS * GLA_DK
GLA_VW = GLA_HEADS * GLA_DV
MIX_WIDTH = ATT_WIDTH + GLA_VW
IN_WIDTH = ATT_WIDTH + 2 * KV_WIDTH + 2 * GLA_KW + 2 * GLA_VW + GLA_RANK
EPS = 1e-6

kernel_name = "hymba_swa_sink_gla_sandwich_step"


def rms_norm(x, g):
    xf = x.astype(jnp.float32)
    y = xf * lax.rsqrt(jnp.mean(xf * xf, axis=-1, keepdims=True) + EPS)
    return (y * g.astype(jnp.float32)).astype(x.dtype)


def rope(x, pos):
    half = HEAD_DIM // 2
    inv = ROPE_THETA ** (-jnp.arange(half, dtype=jnp.float32) / half)
    ang = pos.astype(jnp.float32)[:, None] * inv[None, :]
    cos = jnp.cos(ang)[:, None, :]
    sin = jnp.sin(ang)[:, None, :]
    xf = x.astype(jnp.float32)
    x1, x2 = xf[..., :half], xf[..., half:]
    return jnp.concatenate([x1 * cos - x2 * sin, x2 * cos + x1 * sin], axis=-1).astype(x.dtype)


def sink_softmax(scores, mask, sink):
    scores = jnp.where(mask, scores, -jnp.inf)
    m = jnp.maximum(jnp.max(scores, axis=-1, keepdims=True), sink)
    p = jnp.exp(scores - m)
    return p / (jnp.sum(p, axis=-1, keepdims=True) + jnp.exp(sink - m))


def swa_banded(q, k, v, sinks):
    B, T = q.shape[:2]
    L = ATT_BLOCK
    nb = T // L
    qb = q.reshape(B, nb, L, ATT_KV_HEADS, ATT_GROUP, HEAD_DIM)
    kb = k.reshape(B, nb, L, ATT_KV_HEADS, HEAD_DIM)
    vb = v.reshape(B, nb, L, ATT_KV_HEADS, HEAD_DIM)
    kk = jnp.concatenate([jnp.concatenate([jnp.zeros_like(kb[:, :1]), kb[:, :-1]], 1), kb], 2)
    vv = jnp.concatenate([jnp.concatenate([jnp.zeros_like(vb[:, :1]), vb[:, :-1]], 1), vb], 2)
    scores = jnp.einsum('bnqhgd,bnkhd->bnhgqk', qb, kk,
                        preferred_element_type=jnp.float32) * (HEAD_DIM ** -0.5)
    qi = jnp.arange(L)[:, None] + L
    ki = jnp.arange(2 * L)[None, :]
    rel = qi - ki
    band = (rel >= 0) & (rel < WINDOW)
    blk_ok = (jnp.arange(nb)[:, None] > 0) | (jnp.arange(2 * L)[None, :] >= L)
    mask = (band[None] & blk_ok[:, None, :])[None, :, None, None]
    sink = sinks.astype(jnp.float32).reshape(ATT_KV_HEADS, ATT_GROUP, 1, 1)
    p = sink_softmax(scores, mask, sink)
    out = jnp.einsum('bnhgqk,bnkhd->bnqhgd', p.astype(v.dtype), vv)
    return out.reshape(B, T, ATT_WIDTH)


def swa_window(q, k_all, v_all, sinks):
    B, T = q.shape[:2]
    qg = q.reshape(B, T, ATT_KV_HEADS, ATT_GROUP, HEAD_DIM)
    scores = jnp.einsum('bqhgd,bkhd->bhgqk', qg, k_all,
                        preferred_element_type=jnp.float32) * (HEAD_DIM ** -0.5)
    rel = (jnp.arange(T)[:, None] + WINDOW) - jnp.arange(WINDOW + T)[None, :]
    mask = (rel >= 0) & (rel < WINDOW)
    sink = sinks.astype(jnp.float32).reshape(ATT_KV_HEADS, ATT_GROUP, 1, 1)
    p = sink_softmax(scores, mask, sink)
    out = jnp.einsum('bhgqk,bkhd->bqhgd', p.astype(v_all.dtype), v_all)
    return out.reshape(B, T, ATT_WIDTH)


def gla_chunked(q, k, v, log_a, s0):
    B, T = q.shape[:2]
    C = math.gcd(T, GLA_CHUNK)
    n = T // C
    f32 = jnp.float32
    qc = q.astype(f32).reshape(B, n, C, GLA_HEADS, GLA_DK) * (GLA_DK ** -0.5)
    kc = k.astype(f32).reshape(B, n, C, GLA_HEADS, GLA_DK)
    vc = v.astype(f32).reshape(B, n, C, GLA_HEADS, GLA_DV)
    b = jnp.cumsum(log_a.reshape(B, n, C, GLA_HEADS, GLA_DK), axis=2)
    causal = jnp.tril(jnp.ones((C, C), dtype=bool))[:, :, None, None]
    diff = b[:, :, :, None] - b[:, :, None, :]
    k_rel = jnp.exp(jnp.where(causal, diff, -jnp.inf)) * kc[:, :, None]
    attn = jnp.einsum('bnthk,bntshk->bnhts', qc, k_rel)
    o_intra = jnp.einsum('bnhts,bnshv->bnthv', attn, vc)
    b_last = b[:, :, -1:]
    q_dec = qc * jnp.exp(b)
    k_dec = kc * jnp.exp(b_last - b)
    a_blk = jnp.exp(b_last[:, :, 0])

    def step(S, xs):
        qd, kd, vv, ab = xs
        o = jnp.einsum('bthk,bhkv->bthv', qd, S)
        S = ab[..., None] * S + jnp.einsum('bthk,bthv->bhkv', kd, vv)
        return S, o

    xs = (jnp.moveaxis(q_dec, 1, 0), jnp.moveaxis(k_dec, 1, 0),
          jnp.moveaxis(vc, 1, 0), jnp.moveaxis(a_blk, 1, 0))
    S, o_inter = lax.scan(step, s0, xs)
    o = o_intra + jnp.moveaxis(o_inter, 0, 1)
    return o.reshape(B, T, GLA_HEADS, GLA_DV), S


def hybrid_layer(x, pos, k_win, v_win, s0, w_in, w_gk2, b_gk, g_gla, sinks, w_out,
                 g_mix_pre, g_mix_post, g_ffn_pre, g_ffn_post, w_up, w_down):
    B, T, _ = x.shape
    h = rms_norm(x, g_mix_pre)
    proj = jnp.einsum('btd,de->bte', h, w_in)
    widths = (ATT_WIDTH, KV_WIDTH, KV_WIDTH, GLA_KW, GLA_KW, GLA_VW, GLA_VW, GLA_RANK)
    points, acc = [], 0
    for wdt in widths[:-1]:
        acc += wdt
        points.append(acc)
    q, k, v, gq, gk, gv, gr, glr = jnp.split(proj, points, axis=-1)

    q = rope(q.reshape(B, T, ATT_HEADS, HEAD_DIM), pos)
    k = rope(k.reshape(B, T, ATT_KV_HEADS, HEAD_DIM), pos)
    v = v.reshape(B, T, ATT_KV_HEADS, HEAD_DIM)
    if k_win is None:
        attn = swa_banded(q, k, v, sinks)
        k_all, v_all = k, v
    else:
        k_all = jnp.concatenate([k_win.astype(k.dtype), k], axis=1)
        v_all = jnp.concatenate([v_win.astype(v.dtype), v], axis=1)
        attn = swa_window(q, k_all, v_all, sinks)
    new_k = k_all[:, -WINDOW:]
    new_v = v_all[:, -WINDOW:]

    log_a = jax.nn.log_sigmoid(
        (jnp.einsum('btr,rk->btk', glr, w_gk2) + b_gk).astype(jnp.float32)) / GLA_GATE_TEMP
    if s0 is None:
        s_init = jnp.zeros((B, GLA_HEADS, GLA_DK, GLA_DV), jnp.float32)
    else:
        s_init = s0.astype(jnp.float32)
    o, s_new = gla_chunked(gq.reshape(B, T, GLA_HEADS, GLA_DK), gk.reshape(B, T, GLA_HEADS, GLA_DK),
                           gv.reshape(B, T, GLA_HEADS, GLA_DV),
                           log_a.reshape(B, T, GLA_HEADS, GLA_DK), s_init)
    o = rms_norm(o, g_gla) * jax.nn.silu(gr.astype(jnp.float32)).reshape(B, T, GLA_HEADS, GLA_DV)
    gla_out = o.reshape(B, T, GLA_VW).astype(x.dtype)

    mix = jnp.einsum('bte,ed->btd', jnp.concatenate([attn.astype(x.dtype), gla_out], axis=-1), w_out)
    x = x + rms_norm(mix, g_mix_post)

    h = rms_norm(x, g_ffn_pre)
    u = jax.nn.relu(jnp.einsum('btd,df->btf', h, w_up))
    f = jnp.einsum('btf,fd->btd', u * u, w_down)
    x = x + rms_norm(f, g_ffn_post)
    return x, new_k, new_v, s_new.astype(x.dtype)


def setup_inputs(seed: int = 0) -> dict:
    key = jax.random.key(seed)
    ks = jax.random.split(key, 20)
    nrm = jax.random.normal
    f32 = jnp.float32
    return {
        "x_prompt": nrm(ks[0], (BATCH, SEQ, D_MODEL), f32),
        "x_sample": nrm(ks[1], (DEC_BATCH, DEC_SEQ, D_MODEL), f32),
        "cache_k": nrm(ks[2], (DEPTH, DEC_BATCH, WINDOW, ATT_KV_HEADS, HEAD_DIM), f32),
        "cache_v": nrm(ks[3], (DEPTH, DEC_BATCH, WINDOW, ATT_KV_HEADS, HEAD_DIM), f32),
        "state_gla": nrm(ks[4], (DEPTH, DEC_BATCH, GLA_HEADS, GLA_DK, GLA_DV), f32),
        "w_in": nrm(ks[5], (DEPTH, D_MODEL, IN_WIDTH), f32) * D_MODEL ** -0.5,
        "w_gk2": nrm(ks[6], (DEPTH, GLA_RANK, GLA_KW), f32) * GLA_RANK ** -0.5,
        "b_gk": 0.02 * nrm(ks[7], (DEPTH, GLA_KW), f32),
        "g_gla": 1.0 + 0.05 * nrm(ks[8], (DEPTH, GLA_DV), f32),
        "sinks": 0.5 * nrm(ks[9], (DEPTH, ATT_HEADS), f32),
        "w_out": nrm(ks[10], (DEPTH, MIX_WIDTH, D_MODEL), f32) * MIX_WIDTH ** -0.5,
        "g_mix_pre": 1.0 + 0.05 * nrm(ks[11], (DEPTH, D_MODEL), f32),
        "g_mix_post": 1.0 + 0.05 * nrm(ks[12], (DEPTH, D_MODEL), f32),
        "g_ffn_pre": 1.0 + 0.05 * nrm(ks[13], (DEPTH, D_MODEL), f32),
        "g_ffn_post": 1.0 + 0.05 * nrm(ks[14], (DEPTH, D_MODEL), f32),
        "w_up": nrm(ks[15], (DEPTH, D_MODEL, D_FF), f32) * D_MODEL ** -0.5,
        "w_down": nrm(ks[16], (DEPTH, D_FF, D_MODEL), f32) * D_FF ** -0.5,
    }


def reference(x_prompt, x_sample, cache_k, cache_v, state_gla, w_in, w_gk2, b_gk, g_gla, sinks,
              w_out, g_mix_pre, g_mix_post, g_ffn_pre, g_ffn_post, w_up, w_down):
    pos_p = jnp.arange(x_prompt.shape[1], dtype=jnp.int32)
    pos_s = PAST_LEN + jnp.arange(x_sample.shape[1], dtype=jnp.int32)
    yp, ys = x_prompt, x_sample
    kp, vp, sp, ksm, vsm, ssm = [], [], [], [], [], []
    for l in range(DEPTH):
        lw = (w_in[l], w_gk2[l], b_gk[l], g_gla[l], sinks[l], w_out[l],
              g_mix_pre[l], g_mix_post[l], g_ffn_pre[l], g_ffn_post[l], w_up[l], w_down[l])
        yp, k1, v1, s1 = hybrid_layer(yp, pos_p, None, None, None, *lw)
        ys, k2, v2, s2 = hybrid_layer(ys, pos_s, cache_k[l], cache_v[l], state_gla[l], *lw)
        kp.append(k1); vp.append(v1); sp.append(s1)
        ksm.append(k2); vsm.append(v2); ssm.append(s2)
    return (yp, ys, jnp.stack(kp), jnp.stack(vp), jnp.stack(sp),
            jnp.stack(ksm), jnp.stack(vsm), jnp.stack(ssm))
```

```python
import contextlib
import sys as _sys
import numpy as np
import concourse.bass as bass
import concourse.mybir as mybir
from concourse.bass_utils import run_bass_kernel_spmd

F32 = mybir.dt.float32
BF16 = mybir.dt.bfloat16
AF = mybir.ActivationFunctionType
ALU = mybir.AluOpType
AX = mybir.AxisListType

D = 1024
NCORES = 8
NT = 16
NH = 48
INW = 2320
DFF = 4096
EPS = 1e-6
NEG = -30000.0
CQ, CK, CV, CGQ, CGK, CGLR, CGV, CGR = 0, 512, 640, 768, 1024, 1280, 1296, 1808


class Buf:
    def __init__(self, name, t):
        self.name, self.t = name, t
        self.w = None
        self.r = []
        self.psum = False

    def __getitem__(self, k):
        return self.t[k]


ROT = {"p": 0}


class Rot:
    def __init__(self, bufs):
        self.bufs = list(bufs)

    @property
    def cur(self):
        return self.bufs[ROT["p"] % len(self.bufs)]

    def __getitem__(self, k):
        return self.cur.t[k]

    t = property(lambda s: s.cur.t)
    psum = property(lambda s: s.cur.psum)
    name = property(lambda s: s.cur.name)
    w = property(lambda s: s.cur.w, lambda s, v: setattr(s.cur, "w", v))
    r = property(lambda s: s.cur.r, lambda s, v: setattr(s.cur, "r", v))


class Inst:
    __slots__ = ("eng", "fn", "deps", "ticket", "key", "needs", "rot", "seg", "odeps", "src")

    def __init__(self, eng, fn, key=None):
        self.eng, self.fn, self.key = eng, fn, key
        self.rot = ROT["p"]
        self.seg = None
        self.odeps = []
        self.deps = []
        self.ticket = None
        self.needs = False


class Tracker:
    def __init__(self, nc, es):
        self.nc, self.es = nc, es
        self.engs = {"pe": nc.tensor, "act": nc.scalar, "dve": nc.vector, "pool": nc.gpsimd, "sp": nc.sync}
        self.order = []
        self.pos = 0
        self.esem = None
        self.ksem, self.kcnt, self.ecnt = {}, {}, {e: 0 for e in self.engs}
        self.waited = {e: {} for e in self.engs}
        self.persist = []
        self.allbufs = []
        self.pending_bar = {}
        self.seg = None
        self.groups = {}

    def sb(self, name, shape, dt, scope=None):
        b = Buf(name, (scope or self.es).enter_context(self.nc.sbuf_tensor(name, list(shape), dt)))
        self.allbufs.append(b)
        if scope is None:
            self.persist.append(b)
        return b

    def ps(self, name, shape, dt):
        b = Buf(name, self.es.enter_context(self.nc.psum_tensor(name, list(shape), dt)))
        b.psum = True
        self.persist.append(b)
        return b

    def op(self, eng, fn, r=(), w=(), key=None, extra=()):
        ins = Inst(eng, fn, key)
        f = _sys._getframe(1)
        while f.f_code.co_name in ("pe", "act", "dve", "pool", "dma", "op"):
            f = f.f_back
        ins.src = "%s:%d" % (f.f_code.co_name, f.f_lineno)
        if _CACHE.get("trace_lines"):
            f = _sys._getframe(1)
            while f.f_code.co_name in ("pe", "act", "dve", "pool", "dma", "op"):
                f = f.f_back
            _CACHE.setdefault("lines", []).append((len(self.order), eng, f.f_code.co_name, f.f_lineno))
        deps = list(extra) + list(self.pending_bar.pop(eng, ()))
        for b in r:
            if b.w is not None:
                deps.append(b.w)
            if b.psum:
                deps.extend(x for x in b.r if x.eng != eng)
        for b in w:
            if b.w is not None:
                deps.append(b.w)
            deps.extend(b.r)
        seen = set()
        ins.seg = self.seg
        for d in deps:
            if id(d) in seen or d is ins:
                continue
            seen.add(id(d))
            ins.odeps.append(d)
            if d.key is not None and d.key.startswith("G_") and d.key != key:
                ins.odeps.extend(self.groups.get(d.key, ()))
            if d.key is None and d.eng == "pe" and eng == "pe" and key is None:
                continue
            if key is not None and d.key == key and key.startswith("G_"):
                continue
            ins.deps.append(d)
            d.needs = True
        for b in r:
            b.r.append(ins)
        for b in w:
            b.w = ins
            b.r = []
        if key is not None and key.startswith("G_"):
            self.groups.setdefault(key, []).append(ins)
        self.order.append(ins)
        return ins

    def pe(self, fn, r=(), w=()):
        return self.op("pe", fn, r, w)

    def act(self, fn, r=(), w=()):
        return self.op("act", fn, r, w)

    def dve(self, fn, r=(), w=()):
        return self.op("dve", fn, r, w)

    def pool(self, fn, r=(), w=()):
        return self.op("pool", fn, r, w)

    def dma(self, q, out, in_, r=(), w=(), key=None, extra=()):
        return self.op(q, lambda e: e.dma_start(out=out, in_=in_), r, w, key=key, extra=extra)

    def seal(self):
        bar = []
        last = {}
        for ins in self.order:
            last[ins.eng] = ins
            if ins.key is not None:
                bar.append(ins)
        for ins in last.values():
            if ins.key is None:
                ins.needs = True
                bar.append(ins)
        for b in self.persist + self.allbufs:
            for ins in ([b.w] if b.w is not None else []) + list(b.r):
                ins.needs = True
        return bar

    @staticmethod
    def _zipmerge(F, B):
        inF = {id(x): k for k, x in enumerate(F)}
        out, pf, pb = [], 0, 0
        while pf < len(F) or pb < len(B):
            take_b = pb < len(B) and (pf >= len(F) or pb * len(F) <= pf * len(B))
            if take_b:
                ins = B[pb]
                need = max((inF[id(d)] for d in ins.odeps if id(d) in inF), default=-1)
                while pf <= need:
                    out.append(F[pf]); pf += 1
                out.append(ins); pb += 1
            else:
                out.append(F[pf]); pf += 1
        return out

    COST = {"pe": 0.20, "act": 0.40, "dve": 0.40, "pool": 0.80, "sp": 0.08}
    FCOST = {
        ("pe", "proj"): 0.35, ("pe", "norm_T"): 0.14, ("pe", "main_back"): 0.30, ("pe", "attention_prompt"): 0.15,
        ("pe", "gla_intra_and_out"): 0.15, ("pe", "inter_prompt"): 0.15, ("pe", "gate_common"): 0.50, ("pe", "ablk_prompt"): 0.20,
        ("pe", "state_update_prompt"): 0.45, ("pe", "k_transpose"): 0.14,
        ("act", "norm_T"): 0.75, ("act", "rstd_from_ss"): 0.25, ("act", "main_front"): 0.40, ("act", "rope_kv"): 0.30,
        ("act", "evac_gate"): 0.30, ("act", "gate_out_prep"): 0.60, ("act", "attention_prompt"): 0.45, ("act", "gate_common"): 0.35,
        ("act", "gla_intra_and_out"): 0.35, ("act", "gla_finish"): 0.30, ("act", "post_norm_residual"): 0.75, ("act", "main_back"): 0.60,
        ("act", "hist_front"): 0.40,
        ("dve", "norm_T"): 0.70, ("dve", "rope"): 0.35, ("dve", "attention_prompt"): 0.45, ("dve", "gate_common"): 0.35,
        ("dve", "gla_intra_and_out"): 0.50, ("dve", "gla_finish"): 0.35, ("dve", "state_update_prompt"): 0.33,
        ("dve", "post_norm_residual"): 1.10, ("dve", "evac_gate"): 0.40, ("dve", "main_front"): 0.40,
        ("pool", "rope"): 0.60, ("dve", "post_norm_residual"): 1.10, ("pool", "gate_out_prep"): 1.10, ("pool", "state_update_prompt"): 0.80,
        ("pool", "rope_kv"): 0.40,
    }

    def _schedule(self, todo):
        import heapq
        idx = {id(x): k for k, x in enumerate(todo)}
        n = len(todo)
        ndep = [0] * n
        users = [[] for _ in range(n)]
        for k, ins in enumerate(todo):
            ds = {idx[id(d)] for d in ins.odeps if id(d) in idx}
            ndep[k] = len(ds)
            for j in ds:
                users[j].append(k)
        fin = [0.0] * n
        ready_t = [0.0] * n
        crit = [None] * n
        elast = {e: None for e in self.engs}
        efree = {e: 0.0 for e in self.engs}
        ready = {e: [] for e in self.engs}
        for k in range(n):
            if ndep[k] == 0:
                heapq.heappush(ready[todo[k].eng], k)
        out = []
        LOOK = 24
        while len(out) < n:
            best = None
            for e, hp in ready.items():
                if not hp:
                    continue
                cands = heapq.nsmallest(LOOK, hp)
                for k in cands:
                    st = max(efree[e], ready_t[k])
                    key = (st, k)
                    if best is None or key < best[0]:
                        best = (key, e, k)
            (st, _), e, k = best
            ready[e].remove(k)
            heapq.heapify(ready[e])
            ins = todo[k]
            is_dma = ins.key is not None
            dur = (0.08 if is_dma else self.FCOST.get((e, ins.src.split(":")[0]), self.COST[e]))
            if efree[e] > ready_t[k] and elast[e] is not None:
                crit[k] = ("eng", elast[e])
            elast[e] = k
            efree[e] = st + dur
            fin[k] = st + (3.0 if is_dma else dur)
            out.append(ins)
            for u in users[k]:
                lat = 0.1 if (todo[u].eng == e and not is_dma) else 0.8
                if fin[k] + lat > ready_t[u]:
                    ready_t[u] = fin[k] + lat
                    if crit[u] is None or crit[u][0] != "eng":
                        crit[u] = ("dep", k)
                ndep[u] -= 1
                if ndep[u] == 0:
                    heapq.heappush(ready[todo[u].eng], u)
        if _CACHE.get("crit_seg"):
            tgt = _CACHE["crit_seg"]
            ks = [k for k, ins in enumerate(todo) if ins.seg == tgt]
            k = max(ks, key=lambda q: fin[q]) if ks else None
            lines = {t[0]: t for t in _CACHE.get("lines", [])}
            base = self.pos
            prevdesc = None
            hops = 0
            while k is not None and hops < 4000:
                ins = todo[k]
                li = lines.get(base + self.order[self.pos:].index(ins)) if False else None
                desc = (ins.seg, ins.eng, getattr(ins, "src", None), crit[k][0] if crit[k] else None)
                if desc != prevdesc:
                    print("crit: t=%.1f" % fin[k], desc)
                    prevdesc = desc
                k = crit[k][1] if crit[k] else None
                hops += 1
                if ins.seg is not None and ins.seg[1] < tgt[1] - 1:
                    break
        if _CACHE.get("sched_report"):
            last = {}
            for k, ins in enumerate(todo):
                if ins.seg is not None:
                    last[ins.seg] = max(last.get(ins.seg, 0.0), fin[k])
            prev = 0.0
            for sg in sorted(last, key=lambda x: (x[1], x[0])):
                if sg[0] == "B":
                    print("sched est: seg", sg, "done at %.1f us (+%.1f)" % (last[sg], last[sg] - prev))
                    prev = last[sg]
        return out

    def _reorder(self, todo):
        runs = []
        for ins in todo:
            if runs and runs[-1][0] == ins.seg:
                runs[-1][1].append(ins)
            else:
                runs.append((ins.seg, [ins]))
        out, k = [], 0
        while k < len(runs):
            lab, lst = runs[k]
            if lab is not None and lab[0] == "F" and k + 1 < len(runs) and runs[k + 1][0] is not None and runs[k + 1][0][0] == "B":
                out.extend(self._zipmerge(lst, runs[k + 1][1]))
                k += 2
            else:
                out.extend(lst)
                k += 1
        assert len(out) == len(todo)
        return out

    def emit(self, final=False):
        nc, es = self.nc, self.es
        if self.esem is None:
            self.esem = {e: es.enter_context(nc.semaphore("se_" + e)) for e in self.engs}
        esem, ksem, kcnt, ecnt, waited = self.esem, self.ksem, self.kcnt, self.ecnt, self.waited
        todo = self.order[self.pos:]
        if _CACHE.get('zipmerge'):
            todo = self._reorder(todo)
        elif not _CACHE.get('no_sched'):
            todo = self._schedule(todo)
        if _CACHE.get('maxinst'):
            todo = todo[:_CACHE['maxinst']]
        for ins in todo:
            if ins.key is not None:
                if ins.key not in ksem:
                    ksem[ins.key] = es.enter_context(nc.semaphore("sk_" + ins.key))
                    kcnt[ins.key] = 0
                kcnt[ins.key] += 16
                ins.ticket = kcnt[ins.key]
            elif ins.needs:
                ecnt[ins.eng] += 1
                ins.ticket = ecnt[ins.eng]
        for ins in todo:
            eng = self.engs[ins.eng]
            wl = {}
            for d in ins.deps:
                if d.key is not None:
                    grp = d.key.startswith("G_")
                    s, v = ksem[d.key], (kcnt[d.key] if grp else d.ticket)
                else:
                    s, v = esem[d.eng], d.ticket
                assert v is not None, (ins.eng, d.eng)
                if wl.get(s.name, (None, 0))[1] < v:
                    wl[s.name] = (s, v)
            for nm, (s, v) in wl.items():
                if waited[ins.eng].get(nm, 0) >= v:
                    continue
                waited[ins.eng][nm] = v
                eng.wait_ge(s, v)
            ROT["p"] = ins.rot
            bi = ins.fn(eng)
            if ins.key is not None:
                bi.then_inc(ksem[ins.key], 16)
            elif ins.needs:
                bi.then_inc(esem[ins.eng], 1)
        self.pos = len(self.order)
        if final:
            for k, s in ksem.items():
                nc.sync.wait_ge(s, kcnt[k])
            print("bass program: insts", len(self.order), "sems", len(ksem) + 5, "eng tickets", ecnt)


def build_program():
    nc = bass.Bass("TRN2", target_bir_lowering=False)
    es = contextlib.ExitStack()
    T = Tracker(nc, es)

    def din(name, shape):
        return nc.dram_tensor(name, list(shape), F32, kind="ExternalInput").ap()

    def dout(name, shape):
        return nc.dram_tensor(name, list(shape), F32, kind="ExternalOutput").ap()

    xp = din("xp", [NT * 128, D]); xh = din("xh", [NH * 128, D]); xs = din("xs", [128, D])
    ck = din("ck", [16, 128, 128]); cv = din("cv", [16, 128, 128]); sg = din("sg", [16, 4, 64, 128])
    w_in = din("w_in", [D, INW]); w_out = din("w_out", [D, D]); w_up = din("w_up", [D, DFF]); w_down = din("w_down", [DFF, D])
    wgk = din("wgk", [32, 256])
    cs_p = din("cs_p", [NT * 128, 64]); cs_h = din("cs_h", [128, 64]); cs_s = din("cs_s", [128, 64])
    c_idf = din("c_idf", [128, 128]); c_ltri = din("c_ltri", [128, 128]); c_utri = din("c_utri", [128, 128])
    c_ltri_s = din("c_ltri_s", [128, 128]); c_utri_s = din("c_utri_s", [128, 128])
    c_cm = din("c_cm", [128, 128]); c_cm_s = din("c_cm_s", [128, 128])
    c_mask = din("c_mask", [128, 256]); c_mask0 = din("c_mask0", [128, 256])
    c_gpre = din("c_gpre", [128, 8]); c_gfpre = din("c_gfpre", [128, 8])
    c_gpost = din("c_gpost", [128, D]); c_gfpost = din("c_gfpost", [128, D]); c_ggla = din("c_ggla", [128, 512])
    c_sink = din("c_sink", [128, 8])
    c_masks = din("c_masks", [128, 136]); c_sinks = din("c_sinks", [32, 2]); c_bmask = din("c_bmask", [128, 16 * 128])
    c_rowmask = din("c_rowmask", [128, 16]); c_seqsel = din("c_seqsel", [128, 16])
    scr = nc.dram_tensor("scr_attn", [16, 8, 2, 4, 64], BF16).ap()

    y_p = dout("y_p", [NT * 128, D]); y_s = dout("y_s", [128, D])
    kw_p = dout("kw_p", [128, 128]); vw_p = dout("vw_p", [128, 128]); gl_p = dout("gl_p", [128, 2, 128])
    kw_s = dout("kw_s", [16, 128, 128]); vw_s = dout("vw_s", [16, 128, 128]); gl_s = dout("gl_s", [16, 4, 64, 128])

    h2T = T.sb("h2T", [128, 8, (NT + 1) * 128], BF16)
    xbuf = [T.sb(f"xbuf{i}", [128, D], F32) for i in range(3)]
    st = T.sb("st", [128, 16], F32)
    stb = T.sb("stb", [128, 16], F32)
    tmp = T.sb("tmp", [128, D], F32)
    epsb = T.sb("epsb", [128, 1], F32)
    esA = contextlib.ExitStack()
    _sb = T.sb
    A = lambda name, shape, dt: _sb(name, shape, dt, scope=esA)
    Win = A("Win", [128, 8, INW], BF16)
    Wout = A("Wout", [128, 8, D], BF16)
    idf = A("idf", [128, 128], F32); idb = A("idb", [128, 128], BF16)
    ltri = A("ltri", [128, 128], F32); utri = A("utri", [128, 128], F32)
    ltri_s = A("ltri_s", [128, 128], F32); utri_s = A("utri_s", [128, 128], F32)
    cm = A("cm", [128, 128], F32); cm_s = A("cm_s", [128, 128], F32)
    maskb = A("maskb", [128, 256], BF16); mask0b = A("mask0b", [128, 256], BF16)
    gpre = A("gpre", [128, 8], F32); gfpre = A("gfpre", [128, 8], F32)
    gpost = A("gpost", [128, D], F32); ggla = A("ggla", [128, 512], F32)
    sink = A("sink", [128, 8], F32)
    wgk_sb = A("wgk_sb", [32, 256], F32)
    ones16 = A("ones16", [128, 2], F32)
    glrT = A("glrT", [32, 128], F32)
    S = A("S", [128, 2, 128], F32); Sb = A("Sb", [128, 2, 128], BF16)
    csb = [A(f"csb{i}", [128, 64], F32) for i in range(2)]
    hb = A("hb", [128, D], BF16)
    hT = A("hT", [128, 8, 128], BF16)
    qk = A("qk", [128, 10, 64], F32); qkr = A("qkr", [128, 10, 64], F32); qkb = A("qkb", [128, 10, 64], BF16)
    rt = [A(f"rt{i}", [128, 10, 32], F32) for i in range(2)]
    vf = A("vf", [128, 128], F32)
    vb = [A(f"vb{i}", [128, 128], BF16) for i in range(3)]
    kT = [A(f"kT{i}", [128, 128], BF16) for i in range(3)]
    qT = A("qT", [128, 4, 128], BF16)
    Pm = A("Pm", [128, 8, 256], BF16)
    PT = A("PT", [128, 16, 128], BF16)
    ast = A("ast", [128, 48], F32)
    mix = A("mix", [128, D], BF16)
    mixT = A("mixT", [128, 8, 128], BF16)
    glr_sb = A("glr_sb", [128, 16], F32)
    gk_sb = A("gk_sb", [128, 256], F32); gq_sb = A("gq_sb", [128, 256], F32)
    az = A("az", [128, 256], F32); ez = A("ez", [128, 256], F32); la = A("la", [128, 256], F32)
    eB = A("eB", [128, 256], F32); eNB = A("eNB", [128, 256], F32); eR = A("eR", [128, 256], F32)
    ablk = A("ablk", [128, 2], F32)
    qd = A("qd", [128, 256], BF16); ki = A("ki", [128, 256], BF16); kd = A("kd", [128, 256], BF16)
    qdT = A("qdT", [128, 2, 128], BF16)
    qdTz = A("qdTz", [128, 4, 128], BF16); kiTz = A("kiTz", [128, 4, 128], BF16)
    ATm = A("ATm", [128, 4, 128], BF16)
    gvb = A("gvb", [128, 512], BF16)
    sil = A("sil", [128, 512], F32); t1 = A("t1", [128, 512], F32)
    gst = A("gst", [128, 8], F32)
    esS = contextlib.ExitStack()
    SA = lambda name, shape, dt: _sb(name, shape, dt, scope=esS)
    cvb = SA("cvb", [128, 16, 128], BF16)
    kTcz = SA("kTcz", [128, 2, 16, 128], BF16); kTnz = SA("kTnz", [128, 2, 128], BF16)
    qTs = SA("qTs", [128, 16, 4, 8], BF16)
    vn = SA("vn", [8, 16, 128], BF16)
    masks = SA("masks", [128, 136], BF16); sink_s = SA("sink_s", [32, 2], F32)
    Pn = SA("Pn", [32, 8, 8], BF16); PTn = SA("PTn", [8, 8, 32], BF16)
    attn_s = SA("attn_s", [32, 2, 16, 64], BF16)
    sst = SA("sst", [32, 64], F32)
    bmask = SA("bmask", [128, 16, 128], BF16); rowmask = SA("rowmask", [128, 16], F32); seqsel = SA("seqsel", [128, 16], F32)
    ablk_s = SA("ablk_s", [128, 16, 2], F32)
    qdTm = [SA(f"qdTm{i}", [128, 16, 128], BF16) for i in range(2)]
    S0F = [SA(f"S0f{i}", [128, 2, 2, 128], F32) for i in range(2)]
    S0B = [SA(f"S0blk{i}", [128, 2, 2, 256], BF16) for i in range(2)]
    ckb_v = Pm.t[:].rearrange("p a (c k) -> p (a c) k", c=2)
    kdm_v = PT.t[:, 8:16, :].rearrange("p a k -> p (a k)").rearrange("p (b c) -> p b c", b=4)

    PU = [T.ps(f"PU{i}", [128, 1024], F32) for i in range(3)]
    PTb = [T.ps(f"PTb{i}", [128, 1024], BF16) for i in range(2)]
    cnt = {"u": 0, "t": 0}

    def pu(avoid=None, small=False):
        if T.seg is not None and T.seg[0] == "F":
            return PU[0]
        if T.seg is not None and T.seg[0] == "B":
            return PU[1 + T.seg[1] % 2]
        pool_ = (PU + cnt.get("extra_small", [])) if small else PU
        while True:
            cnt["u"] += 1
            u = pool_[cnt["u"] % len(pool_)]
            if u is not avoid:
                return u

    def ptb():
        if cnt.get("force_t") is not None:
            return PTb[cnt["force_t"]]
        if T.seg is not None and T.seg[0] == "F":
            return PTb[0]
        if T.seg is not None and T.seg[0] == "B":
            return PTb[1]
        cnt["t"] += 1
        return PTb[cnt["t"] % 2]

    for (sbt, dr) in ((idf, c_idf), (ltri, c_ltri), (utri, c_utri), (ltri_s, c_ltri_s), (utri_s, c_utri_s), (cm, c_cm),
                      (cm_s, c_cm_s), (gpre, c_gpre), (gfpre, c_gfpre), (gpost, c_gpost),
                      (ggla, c_ggla), (sink, c_sink), (wgk_sb, wgk)):
        T.dma("sp", sbt[:], dr, w=[sbt], key="G_const")
    for (sbt, dr) in ((sink_s, c_sinks), (rowmask, c_rowmask), (seqsel, c_seqsel)):
        T.dma("sp", sbt[:], dr, w=[sbt], key="G_const")
    for (sbt, dr) in ((idb, c_idf), (maskb, c_mask), (mask0b, c_mask0), (masks, c_masks)):
        T.dma("pool", sbt[:], dr, w=[sbt], key="G_constb")
    T.dma("pool", bmask[:].rearrange("p a t -> p (a t)"), c_bmask, w=[bmask], key="G_constb")
    T.dma("pool", ckb_v, ck.rearrange("b j c -> j b c"), w=[Pm], key="G_constb")
    T.dma("pool", cvb[:], cv.rearrange("b j c -> j b c"), w=[cvb], key="G_constb")
    T.dve(lambda e: e.memset(ones16[:], 1.0 / 16.0), w=[ones16])
    T.dve(lambda e: e.memset(epsb[:], EPS), w=[epsb])
    T.dve(lambda e: e.memset(glrT[:], 1.0), w=[glrT])
    T.dve(lambda e: e.memset(S[:], 0.0), w=[S])
    T.dve(lambda e: e.memset(Sb[:], 0.0), w=[Sb])
    T.dve(lambda e: e.memset(qdTz[:], 0.0), w=[qdTz])
    T.dve(lambda e: e.memset(kiTz[:], 0.0), w=[kiTz])
    T.dve(lambda e: e.memset(kTcz[:], 0.0), w=[kTcz])
    T.dve(lambda e: e.memset(kTnz[:], 0.0), w=[kTnz])
    for sb_ in S0B:
        T.dve(lambda e, sb_=sb_: e.memset(sb_[:], 0.0), w=[sb_])
    for k in range(8):
        rows = slice(k * 128, (k + 1) * 128)
        T.dma("pool", Win[:, k, 0:1280], w_in[rows, 0:1280], w=[Win], key="G_win")
        T.dma("pool", Win[:, k, 1280:1296], w_in[rows, 2304:2320], w=[Win], key="G_win")
        T.dma("pool", Win[:, k, 1296:2320], w_in[rows, 1280:2304], w=[Win], key="G_win")
    for k in range(8):
        T.dma("pool", Wout[:, k, :], w_out[k * 128:(k + 1) * 128, :], w=[Wout], key="G_wout")

    T.dma("sp", kw_s[:, 0:120, :], ck[:, 8:128, :], key="o_kws")
    T.dma("sp", vw_s[:, 0:120, :], cv[:, 8:128, :], key="o_vws")

    def rstd_from_ss(ss_ap, n, out_ap, rbufs, wbufs):
        T.act(lambda e: e.activation(out=out_ap, in_=ss_ap, func=AF.Ln, scale=1.0 / n, bias=epsb[:, 0:1]), r=list(rbufs) + [epsb], w=wbufs)
        T.act(lambda e: e.activation(out=out_ap, in_=out_ap, func=AF.Exp, scale=-0.5), r=wbufs, w=wbufs)

    def front(x_dram, slot, gcol, dst, dst_ap):
        xt = xbuf[slot]
        T.dma("sp", xt[:], x_dram, w=[xt], key=f"x{slot}")
        norm_T(xt, gcol, dst, dst_ap)

    def norm_T(xt, gcol, dst, dst_ap, tail=False):
        sx = stb if tail else st
        c0 = 4 if tail else 0
        if tail and isinstance(hb, Rot):
            hx = hb.bufs[(ROT["p"] + 1) % len(hb.bufs)]
        else:
            hx = hb.cur if isinstance(hb, Rot) else hb
        if isinstance(sx, Rot):
            sx = sx.cur
        T.act(lambda e: e.activation(out=hx[:], in_=xt[:], func=AF.Square, accum_out=sx[:, c0:c0 + 1]), r=[xt], w=[sx, hx])
        rstd_from_ss(sx[:, c0:c0 + 1], D, sx[:, c0 + 1:c0 + 2], [sx], [sx])
        T.dve(lambda e: e.tensor_scalar(out=hx[:], in0=xt[:], scalar1=sx[:, c0 + 1:c0 + 2], scalar2=None, op0=ALU.mult), r=[xt, sx], w=[hx])
        pt = ptb()
        for k in range(8):
            T.pe(lambda e, k=k: e.transpose(out=pt[:, k * 128:(k + 1) * 128], in_=hx[:, k * 128:(k + 1) * 128], identity=idb[:]),
                 r=[hx, idb], w=[pt])
        T.dve(lambda e: e.tensor_tensor(out=dst_ap, in0=pt[:].rearrange("p (k t) -> p k t", k=8),
                                        in1=gcol[:].unsqueeze(2).to_broadcast([128, 8, 128]), op=ALU.mult),
              r=[pt, gcol], w=[dst])

    def proj(ps_ap, c0, n, ps):
        for k in range(8):
            T.pe(lambda e, k=k: e.matmul(ps_ap, lhsT=hT[:, k, :], rhs=Win[:, k, c0:c0 + n], start=(k == 0), stop=(k == 7)),
                 r=[hT, Win], w=[ps])

    def rope_kv(psB, cst, kslot, write_kT=True):
        T.act(lambda e: e.copy(out=qk[:, 8:10, :], in_=psB[:, 0:128].rearrange("p (h d) -> p h d", h=2)), r=[psB], w=[qk])
        T.act(lambda e: e.copy(out=vf[:], in_=psB[:, 128:256]), r=[psB], w=[vf])
        T.pool(lambda e: e.tensor_copy(out=vb[kslot][:], in_=vf[:]), r=[vf], w=[vb[kslot]])

    def rope(cst, h0, h1):
        n = h1 - h0
        cos = cst[:, 0:32].unsqueeze(1).to_broadcast([128, n, 32])
        sin = cst[:, 32:64].unsqueeze(1).to_broadcast([128, n, 32])
        x1, x2 = qk[:, h0:h1, 0:32], qk[:, h0:h1, 32:64]
        T.dve(lambda e: e.tensor_tensor(out=rt[0][:, h0:h1, :], in0=x1, in1=cos, op=ALU.mult), r=[qk, cst], w=[rt[0]])
        T.pool(lambda e: e.tensor_tensor(out=rt[1][:, h0:h1, :], in0=x2, in1=sin, op=ALU.mult), r=[qk, cst], w=[rt[1]])
        T.dve(lambda e: e.tensor_tensor(out=qkr[:, h0:h1, 0:32], in0=rt[0][:, h0:h1, :], in1=rt[1][:, h0:h1, :], op=ALU.subtract),
              r=[rt[0], rt[1]], w=[qkr])
        T.dve(lambda e: e.tensor_tensor(out=rt[0][:, h0:h1, :], in0=x2, in1=cos, op=ALU.mult), r=[qk, cst], w=[rt[0]])
        T.pool(lambda e: e.tensor_tensor(out=rt[1][:, h0:h1, :], in0=x1, in1=sin, op=ALU.mult), r=[qk, cst], w=[rt[1]])
        T.dve(lambda e: e.tensor_tensor(out=qkr[:, h0:h1, 32:64], in0=rt[0][:, h0:h1, :], in1=rt[1][:, h0:h1, :], op=ALU.add),
              r=[rt[0], rt[1]], w=[qkr])
        if h0 == 0:
            for g in range(2):
                T.dve(lambda e, g=g: e.tensor_copy(out=qkb[:, g:8:2, :], in_=qkr[:, 4 * g:4 * g + 4, :]), r=[qkr], w=[qkb])
        T.dve(lambda e: e.tensor_copy(out=qkb[:, 8:10, :], in_=qkr[:, 8:10, :]), r=[qkr], w=[qkb])

    def k_transpose(kslot):
        pt = ptb()
        T.pe(lambda e: e.transpose(out=pt[:, 0:128], in_=qkb[:, 8:10, :].rearrange("p h d -> p (h d)"), identity=idb[:]),
             r=[qkb, idb], w=[pt])
        T.act(lambda e: e.copy(out=kT[kslot][:], in_=pt[:, 0:128]), r=[pt], w=[kT[kslot]])

    def evac_gate(psC):
        T.act(lambda e: e.copy(out=glr_sb[:], in_=psC[:, 256:272]), r=[psC], w=[glr_sb])
        T.dve(lambda e: e.tensor_copy(out=gk_sb[:], in_=psC[:, 0:256]), r=[psC], w=[gk_sb])

    def gate_common(ltm, utm, full):
        px = pu()
        T.pe(lambda e: e.matmul(px[0:16, 0:128], lhsT=glr_sb[:], rhs=idf[:], start=True, stop=True), r=[glr_sb, idf], w=[px])
        T.act(lambda e: e.copy(out=glrT[0:16, :], in_=px[0:16, 0:128]), r=[px], w=[glrT])
        T.pe(lambda e: e.matmul(px[:, 512:768], lhsT=glrT[:], rhs=wgk_sb[:], start=True, stop=True), r=[glrT, wgk_sb], w=[px])
        z = px[:, 512:768]
        T.act(lambda e: e.activation(out=az[:], in_=z, func=AF.Abs), r=[px], w=[az])
        T.act(lambda e: e.activation(out=ez[:], in_=az[:], func=AF.Exp, scale=-1.0), r=[az], w=[ez])
        T.act(lambda e: e.activation(out=ez[:], in_=ez[:], func=AF.Ln, bias=1.0), r=[ez], w=[ez])
        T.dve(lambda e: e.tensor_single_scalar(out=az[:], in_=z, scalar=0.0, op=ALU.min), r=[px], w=[az])
        T.dve(lambda e: e.tensor_tensor(out=la[:], in0=az[:], in1=ez[:], op=ALU.subtract), r=[az, ez], w=[la])
        pb = pu()
        if full:
            T.pe(lambda e: e.matmul(pb[:, 0:256], lhsT=ltm[:], rhs=la[:], start=True, stop=True), r=[ltm, la], w=[pb])
        T.pe(lambda e: e.matmul(pb[:, 256:512], lhsT=utm[:], rhs=la[:], start=True, stop=True), r=[utm, la], w=[pb])
        if full:
            T.act(lambda e: e.activation(out=eB[:], in_=pb[:, 0:256], func=AF.Exp), r=[pb], w=[eB])
            T.act(lambda e: e.activation(out=eNB[:], in_=pb[:, 0:256], func=AF.Exp, scale=-1.0), r=[pb], w=[eNB])
        T.act(lambda e: e.activation(out=eR[:], in_=pb[:, 256:512], func=AF.Exp), r=[pb], w=[eR])
        T.dve(lambda e: e.tensor_tensor(out=kd[:], in0=gk_sb[:], in1=eR[:], op=ALU.mult), r=[gk_sb, eR], w=[kd])

    def ablk_prompt():
        pa = pu()
        for p in range(2):
            T.pe(lambda e, p=p: e.matmul(pa[:, 2 * p:2 * p + 2], lhsT=la[:, p * 128:(p + 1) * 128], rhs=ones16[:], start=True, stop=True),
                 r=[la, ones16], w=[pa])
        T.act(lambda e: e.activation(out=ablk[:], in_=pa[:, 0:4:2], func=AF.Exp), r=[pa], w=[ablk])

    def state_update_prompt():
        for p in range(2):
            pss = pu()
            T.pe(lambda e, p=p, pss=pss: e.matmul(pss[:, 0:256], lhsT=kd[:, p * 128:(p + 1) * 128], rhs=gvb[:, p * 256:(p + 1) * 256],
                                         start=True, stop=True), r=[kd, gvb], w=[pss])
            for hp in range(2):
                rows = slice(hp * 64, (hp + 1) * 64)
                T.dve(lambda e, p=p, rows=rows, hp=hp, pss=pss: e.scalar_tensor_tensor(
                    out=S[rows, p, :], in0=S[rows, p, :], scalar=ablk[rows, p:p + 1], in1=pss[rows, hp * 128:(hp + 1) * 128],
                    op0=ALU.mult, op1=ALU.add), r=[S, ablk, pss], w=[S])
        T.pool(lambda e: e.tensor_copy(out=Sb[:], in_=S[:]), r=[S], w=[Sb])

    def hist_front(i):
        last = (i == NH - 1)
        front(xh[i * 128:(i + 1) * 128, :], i % 2, gpre, hT, hT[:])
        psC = pu()
        proj(psC[:, 0:272], CGK, 272, psC)
        evac_gate(psC)
        psD = pu()
        proj(psD[:, 0:512], CGV, 512, psD)
        T.act(lambda e, psD=psD: e.copy(out=gvb[:], in_=psD[:, 0:512]), r=[psD], w=[gvb])
        if last:
            T.dma("sp", csb[1][:], cs_h, w=[csb[1]], key="cs1")
            psB = pu()
            proj(psB[:, 0:256], CK, 256, psB)
            rope_kv(psB, csb[1], 2)
            rope(csb[1], 8, 10)
            k_transpose(2)
        gate_common(ltri, utri, full=False)
        ablk_prompt()

    def hist_back(i):
        state_update_prompt()

    def gla_intra_and_out(cmask, inter_fn):
        T.dve(lambda e: e.scalar_tensor_tensor(out=qd[:], in0=gq_sb[:], scalar=0.125, in1=eB[:], op0=ALU.mult, op1=ALU.mult),
              r=[gq_sb, eB], w=[qd])
        T.dve(lambda e: e.tensor_tensor(out=ki[:], in0=gk_sb[:], in1=eNB[:], op=ALU.mult), r=[gk_sb, eNB], w=[ki])
        pt = ptb()
        for p in range(2):
            T.pe(lambda e, p=p: e.transpose(out=pt[:, p * 128:(p + 1) * 128], in_=qd[:, p * 128:(p + 1) * 128], identity=idb[:]),
                 r=[qd, idb], w=[pt])
            T.pe(lambda e, p=p: e.transpose(out=pt[:, 256 + p * 128:256 + (p + 1) * 128], in_=ki[:, p * 128:(p + 1) * 128], identity=idb[:]),
                 r=[ki, idb], w=[pt])
        T.act(lambda e: e.copy(out=qdT[:].rearrange("p a t -> p (a t)"), in_=pt[:, 0:256]), r=[pt], w=[qdT])
        for hp in range(2):
            rows = slice(hp * 64, (hp + 1) * 64)
            T.act(lambda e, rows=rows, hp=hp: e.copy(out=qdTz[rows, hp:4:2, :], in_=pt[rows, 0:256].rearrange("p (a t) -> p a t", a=2)),
                  r=[pt], w=[qdTz])
            T.act(lambda e, rows=rows, hp=hp: e.copy(out=kiTz[rows, hp:4:2, :], in_=pt[rows, 256:512].rearrange("p (a t) -> p a t", a=2)),
                  r=[pt], w=[kiTz])
        pat = pu()
        for h in range(4):
            p = h // 2
            T.pe(lambda e, h=h, p=p: e.matmul(pat[:, h * 128:(h + 1) * 128], lhsT=kiTz[:, h, :], rhs=qdT[:, p, :],
                                             start=True, stop=True), r=[kiTz, qdT], w=[pat])
        T.dve(lambda e: e.tensor_tensor(out=ATm[:], in0=pat[:, 0:512].rearrange("p (h t) -> p h t", h=4),
                                        in1=cmask[:].unsqueeze(1).to_broadcast([128, 4, 128]), op=ALU.mult), r=[pat, cmask], w=[ATm])
        po = pu()
        if inter_fn is None:
            sample_inter_and_state(po)
            for h in range(4):
                c0 = (h // 2) * 512 + (h % 2) * 128
                T.pe(lambda e, h=h, c0=c0: e.matmul(po[:, c0:c0 + 128], lhsT=ATm[:, h, :], rhs=gvb[:, h * 128:(h + 1) * 128],
                                                   start=False, stop=(h % 2 == 1)), r=[ATm, gvb], w=[po])
            return po
        for h in range(4):
            T.pe(lambda e, h=h: e.matmul(po[:, h * 128:(h + 1) * 128], lhsT=ATm[:, h, :], rhs=gvb[:, h * 128:(h + 1) * 128],
                                         start=True, stop=False), r=[ATm, gvb], w=[po])
            inter_fn(po, h)
        return po

    def sample_inter_and_state(po):
        pa = pu(po)
        for p in range(2):
            T.pe(lambda e, p=p: e.matmul(pa[:, p * 16:(p + 1) * 16], lhsT=la[:, p * 128:(p + 1) * 128], rhs=seqsel[:], start=True, stop=True),
                 r=[la, seqsel], w=[pa])
        T.act(lambda e: e.activation(out=ablk_s[:].rearrange("q b p -> q p b"), in_=pa[:, 0:32].rearrange("q (p b) -> q p b", p=2),
                                     func=AF.Exp), r=[pa], w=[ablk_s])
        for p in range(2):
            T.dve(lambda e, p=p: e.tensor_tensor(out=qdTm[p][:], in0=qdT[:, p, :].unsqueeze(1).to_broadcast([128, 16, 128]),
                                                 in1=bmask[:], op=ALU.mult), r=[qdT, bmask], w=[qdTm[p]])
        sg_v = sg.rearrange("b (p hp) k v -> hp k b p v", hp=2)
        gl_v = gl_s.rearrange("b (p hp) k v -> hp k b p v", hp=2)
        for o in range(8):
            sf, sb_ = S0F[o % 2], S0B[o % 2]
            ko = (o % 2) * 2
            for hp in range(2):
                T.dma("sp", sf[hp * 64:(hp + 1) * 64, :, :, :], sg_v[hp][:, o * 2:(o + 1) * 2], w=[sf], key=f"s0_{hp}_{o % 2}")
            for hp in range(2):
                rows = slice(hp * 64, (hp + 1) * 64)
                T.pool(lambda e, rows=rows, hp=hp, sf=sf, sb_=sb_: e.tensor_copy(
                    out=sb_[rows, :, :, hp * 128:(hp + 1) * 128].rearrange("k b p v -> k (b p) v"),
                    in_=sf[rows, :, :, :].rearrange("k b p v -> k (b p) v")), r=[sf], w=[sb_])
            for bb in range(2):
                b = o * 2 + bb
                for p in range(2):
                    T.pe(lambda e, b=b, bb=bb, p=p, sb_=sb_: e.matmul(po[:, p * 512:p * 512 + 256], lhsT=qdTm[p][:, b, :], rhs=sb_[:, bb, p, :],
                                                                     start=(b == 0), stop=False), r=[qdTm[p], sb_], w=[po])
            for bb in range(2):
                b = o * 2 + bb
                T.dve(lambda e, b=b, bb=bb, ko=ko: e.tensor_scalar(out=kdm_v[:, ko + bb, :], in0=kd[:], scalar1=rowmask[:, b:b + 1], scalar2=None,
                                                                   op0=ALU.mult), r=[kd, rowmask], w=[PT])
            T.dve(lambda e, o=o, sf=sf: e.tensor_tensor(out=sf[:].rearrange("k b p v -> k (b p) v"), in0=sf[:].rearrange("k b p v -> k (b p) v"),
                                                        in1=ablk_s[:, o * 2:(o + 1) * 2, :].rearrange("k b p -> k (b p)").unsqueeze(2).to_broadcast([128, 4, 128]),
                                                        op=ALU.mult), r=[sf, ablk_s], w=[sf])
            pss = pu(po)
            for bb in range(2):
                for p in range(2):
                    c0 = (bb * 2 + p) * 256
                    T.pe(lambda e, bb=bb, p=p, c0=c0, pss=pss, ko=ko: e.matmul(pss[:, c0:c0 + 256], lhsT=kdm_v[:, ko + bb, p * 128:(p + 1) * 128],
                                                                              rhs=gvb[:, p * 256:(p + 1) * 256], start=True, stop=True),
                         r=[PT, gvb], w=[pss])
            for hp in range(2):
                rows = slice(hp * 64, (hp + 1) * 64)
                T.dve(lambda e, rows=rows, hp=hp, pss=pss, sf=sf: e.tensor_tensor(
                    out=sf[rows, :, :, :].rearrange("k b p v -> k (b p) v"),
                    in0=sf[rows, :, :, :].rearrange("k b p v -> k (b p) v"),
                    in1=pss[rows, :].rearrange("k (c v) -> k c v", c=4)[:, :, hp * 128:(hp + 1) * 128], op=ALU.add),
                    r=[sf, pss], w=[sf])
            for hp in range(2):
                T.dma("sp", gl_v[hp][:, o * 2:(o + 1) * 2], sf[hp * 64:(hp + 1) * 64, :, :, :], r=[sf], key=f"s0o_{hp}_{o % 2}")

    def gate_out_prep(psE):
        T.act(lambda e: e.activation(out=sil[:], in_=psE[:, 0:512], func=AF.Silu), r=[psE], w=[sil])
        T.pool(lambda e: e.tensor_tensor(out=t1[:], in0=sil[:], in1=ggla[:], op=ALU.mult), r=[sil, ggla], w=[t1])

    def gla_finish(po, banked=False):
        hc = (lambda h: (h // 2) * 512 + (h % 2) * 128) if banked else (lambda h: h * 128)
        for h in range(4):
            T.act(lambda e, h=h: e.activation(out=mix[:, 512 + h * 128:512 + (h + 1) * 128], in_=po[:, hc(h):hc(h) + 128], func=AF.Square,
                                              accum_out=gst[:, h:h + 1]), r=[po], w=[gst, mix])
        rstd_from_ss(gst[:, 0:4], 128, gst[:, 4:8], [gst], [gst])
        for h in range(4):
            T.dve(lambda e, h=h: e.scalar_tensor_tensor(out=mix[:, 512 + h * 128:512 + (h + 1) * 128], in0=po[:, hc(h):hc(h) + 128],
                                                        scalar=gst[:, 4 + h:5 + h], in1=t1[:, h * 128:(h + 1) * 128],
                                                        op0=ALU.mult, op1=ALU.mult), r=[po, gst, t1], w=[mix])

    def attention_prompt(cur, prev, mk):
        ptq = ptb()
        for i in range(4):
            T.pe(lambda e, i=i: e.transpose(out=ptq[:, i * 128:(i + 1) * 128], in_=qkb[:, 2 * i:2 * i + 2, :].rearrange("p h d -> p (h d)"), identity=idb[:]),
                 r=[qkb, idb], w=[ptq])
        T.act(lambda e: e.copy(out=qT[:].rearrange("p a t -> p (a t)"), in_=ptq[:, 0:512]), r=[ptq], w=[qT])
        for g in range(2):
            pss = pu()
            rows = slice(g * 64, (g + 1) * 64)
            for i in range(4):
                for blk, kslot in ((0, prev), (1, cur)):
                    T.pe(lambda e, i=i, rows=rows, pss=pss, blk=blk, kslot=kslot: e.matmul(
                        pss[:, i * 256 + blk * 128:i * 256 + (blk + 1) * 128], lhsT=qT[rows, i, :], rhs=kT[kslot][rows, :],
                        start=True, stop=False), r=[qT, kT[kslot]], w=[pss])
                    T.pe(lambda e, i=i, pss=pss, blk=blk: e.matmul(
                        pss[:, i * 256 + blk * 128:i * 256 + (blk + 1) * 128], lhsT=idb[:], rhs=mk[:, blk * 128:(blk + 1) * 128],
                        start=False, stop=True), r=[idb, mk], w=[pss])
            T.dve(lambda e, g=g, pss=pss: e.tensor_reduce(out=ast[:, g * 4:(g + 1) * 4], in_=pss[:].rearrange("p (i k) -> p i k", i=4),
                                                         axis=AX.X, op=ALU.max), r=[pss], w=[ast])
            T.dve(lambda e, g=g: e.scalar_tensor_tensor(out=ast[:, 8 + g * 4:8 + (g + 1) * 4], in0=ast[:, g * 4:(g + 1) * 4], scalar=0.125,
                                                        in1=sink[:, g * 4:(g + 1) * 4], op0=ALU.mult, op1=ALU.max), r=[ast, sink], w=[ast])
            T.dve(lambda e, g=g: e.tensor_scalar(out=ast[:, 16 + g * 4:16 + (g + 1) * 4], in0=ast[:, 8 + g * 4:8 + (g + 1) * 4],
                                                 scalar1=-1.0, scalar2=None, op0=ALU.mult), r=[ast], w=[ast])
            for i in range(4):
                h = g * 4 + i
                T.act(lambda e, i=i, h=h, pss=pss: e.activation(out=Pm[:, h, :], in_=pss[:, i * 256:(i + 1) * 256], func=AF.Exp, scale=0.125,
                                                                bias=ast[:, 16 + h:17 + h], accum_out=ast[:, 24 + h:25 + h]),
                      r=[pss, ast], w=[Pm, ast])
        T.dve(lambda e: e.tensor_tensor(out=ast[:, 32:40], in0=sink[:], in1=ast[:, 8:16], op=ALU.subtract), r=[sink, ast], w=[ast])
        T.act(lambda e: e.activation(out=ast[:, 32:40], in_=ast[:, 32:40], func=AF.Exp), r=[ast], w=[ast])
        T.dve(lambda e: e.tensor_tensor(out=ast[:, 40:48], in0=ast[:, 32:40], in1=ast[:, 24:32], op=ALU.add), r=[ast], w=[ast])
        T.dve(lambda e: e.reciprocal(out=ast[:, 40:48], in_=ast[:, 40:48]), r=[ast], w=[ast])
        for half in range(2):
            pt = ptb()
            for j in range(8):
                idx = half * 8 + j
                h, blk = idx // 2, idx % 2
                T.pe(lambda e, j=j, h=h, blk=blk, pt=pt: e.transpose(out=pt[:, j * 128:(j + 1) * 128], in_=Pm[:, h, blk * 128:(blk + 1) * 128],
                                                                    identity=idb[:]), r=[Pm, idb], w=[pt])
            if half == 0:
                T.act(lambda e, pt=pt: e.copy(out=PT[:, 0:8, :].rearrange("p a t -> p (a t)"), in_=pt[:]), r=[pt], w=[PT])
            else:
                T.dve(lambda e, pt=pt: e.tensor_copy(out=PT[:, 8:16, :].rearrange("p a t -> p (a t)"), in_=pt[:]), r=[pt], w=[PT])
        po = pu()
        for h in range(8):
            g = h // 4
            T.pe(lambda e, h=h, g=g: e.matmul(po[:, h * 64:(h + 1) * 64], lhsT=PT[:, 2 * h, :], rhs=vb[prev][:, g * 64:(g + 1) * 64],
                                              start=True, stop=False), r=[PT, vb[prev]], w=[po])
            T.pe(lambda e, h=h, g=g: e.matmul(po[:, h * 64:(h + 1) * 64], lhsT=PT[:, 2 * h + 1, :], rhs=vb[cur][:, g * 64:(g + 1) * 64],
                                              start=False, stop=True), r=[PT, vb[cur]], w=[po])
        T.dve(lambda e: e.tensor_tensor(out=mix[:, 0:512].rearrange("p (h d) -> p h d", h=8), in0=po[:, 0:512].rearrange("p (h d) -> p h d", h=8),
                                        in1=ast[:, 40:48].unsqueeze(2).to_broadcast([128, 8, 64]), op=ALU.mult), r=[po, ast], w=[mix])

    def sample_cache_prep():
        for half in range(2):
            pt = ptb()
            for j in range(8):
                b = half * 8 + j
                T.pe(lambda e, j=j, b=b, pt=pt: e.transpose(out=pt[:, j * 128:(j + 1) * 128], in_=ckb_v[:, b, :], identity=idb[:]),
                     r=[Pm, idb], w=[pt])
            for g in range(2):
                rows = slice(g * 64, (g + 1) * 64)
                T.act(lambda e, g=g, rows=rows, half=half, pt=pt: e.copy(out=kTcz[rows, g, half * 8:(half + 1) * 8, :],
                                                                         in_=pt[rows, :].rearrange("p (b j) -> p b j", b=8)), r=[pt], w=[kTcz])

    vwbuf = Buf("vw_dram", None)

    def attention_sample(cur):
        ptq = ptb()
        for i in range(4):
            T.pe(lambda e, i=i: e.transpose(out=ptq[:, i * 128:(i + 1) * 128], in_=qkb[:, 2 * i:2 * i + 2, :].rearrange("p h d -> p (h d)"), identity=idb[:]),
                 r=[qkb, idb], w=[ptq])
        for i in range(4):
            T.act(lambda e, i=i: e.copy(out=qTs[:, :, i, :], in_=ptq[:, i * 128:(i + 1) * 128].rearrange("p (b t) -> p b t", b=16)),
                  r=[ptq], w=[qTs])
        for g in range(2):
            rows = slice(g * 64, (g + 1) * 64)
            T.act(lambda e, g=g, rows=rows: e.copy(out=kTnz[rows, g, :], in_=kT[cur][rows, :]), r=[kT[cur]], w=[kTnz])
        for b in range(16):
            T.dma("sp", kw_s[b, 120:128, :], qkr[b * 8:(b + 1) * 8, 8:10, :].rearrange("p h d -> p (h d)"), r=[qkr], key="o_kwsn")
            T.dma("sp", vw_s[b, 120:128, :], vf[b * 8:(b + 1) * 8, :], r=[vf], w=[vwbuf], key="G_vwsn")
        T.dma("pool", vn[:], vw_s[:, 120:128, :].rearrange("b t c -> t b c"), r=[vwbuf], w=[vn], key="vnload")
        for g in range(2):
            for hb in range(2):
                Sc = pu()
                Sx = pu()
                for bl in range(8):
                    b = hb * 8 + bl
                    T.pe(lambda e, b=b, bl=bl, g=g, Sc=Sc: e.matmul(Sc[0:32, bl * 128:(bl + 1) * 128], lhsT=qTs[:, b, :, :].rearrange("p i t -> p (i t)"),
                                                                   rhs=kTcz[:, g, b, :], start=True, stop=False), r=[qTs, kTcz], w=[Sc])
                    T.pe(lambda e, bl=bl, Sc=Sc: e.matmul(Sc[0:32, bl * 128:(bl + 1) * 128], lhsT=idb[:, 0:32], rhs=masks[:, 0:128],
                                                         start=False, stop=True), r=[idb, masks], w=[Sc])
                for bl in range(8):
                    b = hb * 8 + bl
                    T.pe(lambda e, b=b, bl=bl, g=g, Sx=Sx: e.matmul(Sx[0:32, bl * 8:(bl + 1) * 8], lhsT=qTs[:, b, :, :].rearrange("p i t -> p (i t)"),
                                                                   rhs=kTnz[:, g, b * 8:(b + 1) * 8], start=True, stop=False), r=[qTs, kTnz], w=[Sx])
                    T.pe(lambda e, bl=bl, Sx=Sx: e.matmul(Sx[0:32, bl * 8:(bl + 1) * 8], lhsT=idb[:, 0:32], rhs=masks[:, 128:136],
                                                         start=False, stop=True), r=[idb, masks], w=[Sx])
                T.dve(lambda e, Sc=Sc: e.tensor_reduce(out=sst[:, 0:8], in_=Sc[0:32, :].rearrange("p (b k) -> p b k", b=8), axis=AX.X, op=ALU.max),
                      r=[Sc], w=[sst])
                T.dve(lambda e, Sx=Sx: e.tensor_reduce(out=sst[:, 8:16], in_=Sx[0:32, 0:64].rearrange("p (b k) -> p b k", b=8), axis=AX.X, op=ALU.max),
                      r=[Sx], w=[sst])
                T.dve(lambda e: e.tensor_tensor(out=sst[:, 0:8], in0=sst[:, 0:8], in1=sst[:, 8:16], op=ALU.max), r=[sst], w=[sst])
                T.dve(lambda e, g=g: e.scalar_tensor_tensor(out=sst[:, 16:24], in0=sst[:, 0:8], scalar=0.125, in1=sink_s[:, g:g + 1].to_broadcast([32, 8]),
                                                            op0=ALU.mult, op1=ALU.max), r=[sst, sink_s], w=[sst])
                T.dve(lambda e: e.tensor_scalar(out=sst[:, 24:32], in0=sst[:, 16:24], scalar1=-1.0, scalar2=None, op0=ALU.mult), r=[sst], w=[sst])
                for bl in range(8):
                    T.act(lambda e, bl=bl, Sc=Sc: e.activation(out=Pm[0:32, bl // 2, (bl % 2) * 128:(bl % 2 + 1) * 128], in_=Sc[0:32, bl * 128:(bl + 1) * 128],
                                                               func=AF.Exp, scale=0.125, bias=sst[:, 24 + bl:25 + bl], accum_out=sst[:, 32 + bl:33 + bl]),
                          r=[Sc, sst], w=[Pm, sst])
                for bl in range(8):
                    T.act(lambda e, bl=bl, Sx=Sx: e.activation(out=Pn[:, bl, :], in_=Sx[0:32, bl * 8:(bl + 1) * 8],
                                                               func=AF.Exp, scale=0.125, bias=sst[:, 24 + bl:25 + bl], accum_out=sst[:, 40 + bl:41 + bl]),
                          r=[Sx, sst], w=[Pn, sst])
                T.dve(lambda e, g=g: e.tensor_tensor(out=sst[:, 48:56], in0=sink_s[:, g:g + 1].to_broadcast([32, 8]), in1=sst[:, 16:24], op=ALU.subtract),
                      r=[sst, sink_s], w=[sst])
                T.act(lambda e: e.activation(out=sst[:, 48:56], in_=sst[:, 48:56], func=AF.Exp), r=[sst], w=[sst])
                T.dve(lambda e: e.tensor_tensor(out=sst[:, 56:64], in0=sst[:, 32:40], in1=sst[:, 40:48], op=ALU.add), r=[sst], w=[sst])
                T.dve(lambda e: e.tensor_tensor(out=sst[:, 56:64], in0=sst[:, 56:64], in1=sst[:, 48:56], op=ALU.add), r=[sst], w=[sst])
                T.dve(lambda e: e.reciprocal(out=sst[:, 56:64], in_=sst[:, 56:64]), r=[sst], w=[sst])
                Pc = Pm[0:32, 0:4, :].rearrange("p a (c k) -> p (a c) k", c=2)
                T.dve(lambda e, Pc=Pc: e.tensor_tensor(out=Pc, in0=Pc, in1=sst[:, 56:64].unsqueeze(2).to_broadcast([32, 8, 128]), op=ALU.mult),
                      r=[Pm, sst], w=[Pm])
                T.dve(lambda e: e.tensor_tensor(out=Pn[:], in0=Pn[:], in1=sst[:, 56:64].unsqueeze(2).to_broadcast([32, 8, 8]), op=ALU.mult),
                      r=[Pn, sst], w=[Pn])
                ptc = ptb()
                for bl in range(8):
                    T.pe(lambda e, bl=bl, ptc=ptc: e.transpose(out=ptc[:, bl * 32:(bl + 1) * 32], in_=Pm[0:32, bl // 2, (bl % 2) * 128:(bl % 2 + 1) * 128],
                                                              identity=idb[0:32, 0:32]), r=[Pm, idb], w=[ptc])
                for bl in range(8):
                    T.pe(lambda e, bl=bl, ptc=ptc: e.transpose(out=ptc[0:8, 256 + bl * 32:256 + (bl + 1) * 32], in_=Pn[:, bl, :],
                                                              identity=idb[0:32, 0:32]), r=[Pn, idb], w=[ptc])
                T.act(lambda e, ptc=ptc: e.copy(out=PT[:, 0:2, :].rearrange("p a (c m) -> p (a c) m", c=4), in_=ptc[:, 0:256].rearrange("p (b m) -> p b m", b=8)),
                      r=[ptc], w=[PT])
                T.act(lambda e, ptc=ptc: e.copy(out=PTn[:], in_=ptc[0:8, 256:512].rearrange("p (b m) -> p b m", b=8)), r=[ptc], w=[PTn])
                po = pu()
                for bl in range(8):
                    b = hb * 8 + bl
                    T.pe(lambda e, b=b, bl=bl, g=g, po=po: e.matmul(po[0:32, bl * 64:(bl + 1) * 64], lhsT=PT[:, bl // 4, (bl % 4) * 32:(bl % 4 + 1) * 32],
                                                                   rhs=cvb[:, b, g * 64:(g + 1) * 64], start=True, stop=False), r=[PT, cvb], w=[po])
                    T.pe(lambda e, b=b, bl=bl, g=g, po=po: e.matmul(po[0:32, bl * 64:(bl + 1) * 64], lhsT=PTn[:, bl, :],
                                                                   rhs=vn[:, b, g * 64:(g + 1) * 64], start=False, stop=True), r=[PTn, vn], w=[po])
                T.act(lambda e, g=g, hb=hb, po=po: e.copy(out=attn_s[:, g, hb * 8:(hb + 1) * 8, :], in_=po[0:32, 0:512].rearrange("p (b d) -> p b d", b=8)),
                      r=[po], w=[attn_s])
        scrbuf = Buf("scr_dram", None)
        for i in range(4):
            T.dma("sp", scr[:, :, :, i, :].rearrange("b t g d -> t g b d"), attn_s[i * 8:(i + 1) * 8, :, :, :], r=[attn_s], w=[scrbuf], key="G_scrw")
        T.dma("sp", mix[:, 0:512], scr.rearrange("b t g i d -> (b t) (g i d)"), r=[scrbuf], w=[mix], key="scrr")

    def post_norm_residual(ps, xt, grep):
        sx = stb.cur if isinstance(stb, Rot) else stb
        T.act(lambda e: e.activation(out=tmp[:], in_=ps[:], func=AF.Square, accum_out=sx[:, 2:3]), r=[ps], w=[sx, tmp])
        rstd_from_ss(sx[:, 2:3], D, sx[:, 3:4], [sx], [sx])
        T.dve(lambda e: e.scalar_tensor_tensor(out=tmp[:], in0=ps[:], scalar=sx[:, 3:4], in1=grep[:], op0=ALU.mult, op1=ALU.mult),
              r=[ps, sx, grep], w=[tmp])
        T.dve(lambda e: e.tensor_tensor(out=xt[:], in0=tmp[:], in1=xt[:], op=ALU.add), r=[tmp, xt], w=[xt])

    def inter_prompt(po, h):
        p = h // 2
        T.pe(lambda e: e.matmul(po[:, h * 128:(h + 1) * 128], lhsT=qdTz[:, h, :], rhs=Sb[:, p, :], start=False, stop=True),
             r=[qdTz, Sb], w=[po])

    ybuf = [Buf(f"ydram{i}", None) for i in range(NT + 1)]
    T.persist.extend(ybuf)

    def main_front(i, is_sample):
        slot = i % 3
        xsrc = xs if is_sample else xp[i * 128:(i + 1) * 128, :]
        cssrc = cs_s if is_sample else cs_p[i * 128:(i + 1) * 128, :]
        cst = csb[i % 2]
        T.dma("sp", cst[:], cssrc, w=[cst], key=f"cs{i % 2}")
        front(xsrc, slot, gpre, hT, hT[:])
        cur = i % 3
        psA = pu()
        proj(psA[:, 0:512], CQ, 512, psA)
        T.act(lambda e: e.copy(out=qk[:, 0:8, :], in_=psA[:, 0:512].rearrange("p (h d) -> p h d", h=8)), r=[psA], w=[qk])
        psB = pu()
        proj(psB[:, 0:512], CK, 512, psB)
        rope_kv(psB, cst, cur)
        T.dve(lambda e: e.tensor_copy(out=gq_sb[:], in_=psB[:, 256:512]), r=[psB], w=[gq_sb])
        rope(cst, 0, 10)
        k_transpose(cur)
        psC = pu()
        proj(psC[:, 0:272], CGK, 272, psC)
        proj(psC[:, 512:1024], CGV, 512, psC)
        T.act(lambda e: e.copy(out=gvb[:], in_=psC[:, 512:1024]), r=[psC], w=[gvb])
        evac_gate(psC)
        psE = pu()
        proj(psE[:, 0:512], CGR, 512, psE)
        gate_out_prep(psE)
        if is_sample:
            gate_common(ltri_s, utri_s, full=True)
        else:
            gate_common(ltri, utri, full=True)
            ablk_prompt()

    def main_back(i, is_sample):
        slot = i % 3
        xt = xbuf[slot]
        cur, prev = i % 3, (i - 1) % 3
        if not is_sample:
            attention_prompt(cur, prev, mask0b if i == 0 else maskb)
            po = gla_intra_and_out(cm, inter_prompt)
            gla_finish(po)
            state_update_prompt()
        else:
            attention_sample(cur)
            po = gla_intra_and_out(cm_s, None)
            gla_finish(po, banked=True)
        if (not is_sample) and i == NT - 1:
            T.dma("sp", kw_p, qkr[:, 8:10, :].rearrange("p h d -> p (h d)"), r=[qkr], key="o_kw")
            T.dma("sp", vw_p, vf[:], r=[vf], key="o_vw")
            T.dma("sp", gl_p, S[:], r=[S], key="o_gl")
        if T.seg is not None and T.seg[0] == "B":
            cnt["force_t"] = 0
        pt = ptb()
        for k in range(8):
            T.pe(lambda e, k=k: e.transpose(out=pt[:, k * 128:(k + 1) * 128], in_=mix[:, k * 128:(k + 1) * 128], identity=idb[:]),
                 r=[mix, idb], w=[pt])
        T.act(lambda e: e.copy(out=mixT[:].rearrange("p a t -> p (a t)"), in_=pt[:]), r=[pt], w=[mixT])
        pm = pu()
        for n in range(2):
            for k in range(8):
                T.pe(lambda e, n=n, k=k: e.matmul(pm[:, n * 512:(n + 1) * 512], lhsT=mixT[:, k, :], rhs=Wout[:, k, n * 512:(n + 1) * 512],
                                                  start=(k == 0), stop=(k == 7)), r=[mixT, Wout], w=[pm])
        post_norm_residual(pm, xt, gpost)
        ydst = y_s if is_sample else y_p[i * 128:(i + 1) * 128, :]
        T.dma("sp", ydst, xt[:], r=[xt], w=[ybuf[i]], key=f"x1o{slot}")
        norm_T(xt, gfpre, h2T, h2T[:, :, i * 128:(i + 1) * 128], tail=True)
        cnt["force_t"] = None

    ROT["p"] = 0
    if not _CACHE.get('no_sample'):
        T.seg = ("S", 0)
        sample_cache_prep()
        main_front(NT, True)
        main_back(NT, True)
        T.seg = None
        bar0 = T.seal()
        T.emit()
        T.pending_bar = {e: bar0 for e in T.engs}
    esS.close()
    esA2 = contextlib.ExitStack()
    A2 = lambda name, shape, dt: _sb(name + "_b", shape, dt, scope=esA2)
    def dup(bf):
        return Rot([bf, A2(bf.name, list(bf.t.shape), bf.t.dtype)])
    hb = dup(hb); hT = dup(hT); st = dup(st); stb = dup(stb); qk = dup(qk); qkr = dup(qkr); qkb = dup(qkb); rt = [dup(x) for x in rt]
    vf = dup(vf); qT = dup(qT); ast = dup(ast); mix = dup(mix); mixT = dup(mixT)
    glr_sb = dup(glr_sb); gk_sb = dup(gk_sb); gq_sb = dup(gq_sb); glrT = dup(glrT)
    az = dup(az); ez = dup(ez); la = dup(la); eB = dup(eB); eNB = dup(eNB); eR = dup(eR); ablk = dup(ablk)
    qd = dup(qd); ki = dup(ki); kd = dup(kd); qdT = dup(qdT); qdTz = dup(qdTz); kiTz = dup(kiTz); ATm = dup(ATm)
    gvb = dup(gvb); sil = dup(sil); t1 = dup(t1); gst = dup(gst)
    ROT["p"] = 1
    T.dve(lambda e: e.memset(glrT[:], 1.0), w=[glrT])
    T.dve(lambda e: e.memset(qdTz[:], 0.0), w=[qdTz])
    T.dve(lambda e: e.memset(kiTz[:], 0.0), w=[kiTz])
    def stage(kind, n, par, fn, *args):
        T.seg = (kind, n)
        ROT["p"] = par
        fn(*args)
        T.seg = None

    stage("F", 0, 0, hist_front, 0)
    for i in range(NH):
        if i + 1 < NH:
            stage("F", i + 1, i + 1, hist_front, i + 1)
        else:
            stage("F", NH, NH, main_front, 0, False)
        stage("B", i, i, hist_back, i)
    for i in range(NT):
        if i + 1 < NT:
            stage("F", NH + i + 1, NH + i + 1, main_front, i + 1, False)
        stage("B", NH + i, NH + i, main_back, i, False)
    ROT["p"] = 0
    bar = T.seal()
    T.emit()
    if _CACHE.get('stop_after_A'):
        T.emit(final=True)
        return nc, es
    esA2.close()
    esA.close()
    st = st.bufs[0]
    stb = stb.bufs[0]
    ROT["p"] = 0

    class _F32View:
        def __init__(self, t):
            self.t = t
        def __getitem__(self, k):
            return self.t[:].bitcast(F32)[k]
    cnt["extra_small"] = []
    for pb_ in PTb:
        vb_ = Buf(pb_.name + "_f32", _F32View(pb_.t))
        vb_.psum = True
        T.persist.append(vb_)
        cnt["extra_small"].append(vb_)

    Wup = [T.sb(f"Wup{j}", [128, 8, 512], BF16) for j in range(8)]
    Wdn = [T.sb(f"Wdn{j}", [128, 4, D], BF16) for j in range(8)]
    u2T = T.sb("u2T", [128, 32, 256], BF16)
    ur = [T.sb(f"ur{i}", [128, 512], F32) for i in range(2)]
    gfpost = T.sb("gfpost", [128, D], F32)
    T.dma("sp", gfpost[:], c_gfpost, w=[gfpost], key="G_gfp", extra=bar)
    for j in range(8):
        T.dma("pool", Wup[j][:], w_up[:, j * 512:(j + 1) * 512].rearrange("(k p) f -> p k f", p=128), w=[Wup[j]], key=f"wup{j}",
              extra=(bar if j == 0 else ()))
    for j in range(8):
        T.dma("pool", Wdn[j][:], w_down[j * 512:(j + 1) * 512, :].rearrange("(c p) n -> p c n", p=128), w=[Wdn[j]], key=f"wdn{j}")
    ntiles = NT + 1
    t0 = 0
    nev = 0
    while t0 < ntiles:
        nt = min(2, ntiles - t0)
        ntok = nt * 128
        for c2 in range(16):
            pu_ = pu(small=True)
            for cc in range(2):
                c = c2 * 2 + cc
                for k in range(8):
                    T.pe(lambda e, c=c, cc=cc, k=k, pu_=pu_, t0=t0, ntok=ntok: e.matmul(pu_[:, cc * 256:cc * 256 + ntok], lhsT=Wup[c // 4][:, k, (c % 4) * 128:(c % 4 + 1) * 128],
                                                                     rhs=h2T[:, k, t0 * 128:t0 * 128 + ntok], start=(k == 0), stop=(k == 7)),
                         r=[Wup[c // 4], h2T], w=[pu_])
            urb = ur[nev % 2]
            nev += 1
            src = pu_[:, 0:512].rearrange("p (c t) -> p c t", c=2)[:, :, 0:ntok]
            dstv = urb[:].rearrange("p (c t) -> p c t", c=2)[:, :, 0:ntok]
            T.act(lambda e, src=src, dstv=dstv: e.activation(out=dstv, in_=src, func=AF.Relu), r=[pu_], w=[urb])
            T.dve(lambda e, dstv=dstv, c2=c2, ntok=ntok: e.tensor_tensor(out=u2T[:, c2 * 2:c2 * 2 + 2, 0:ntok], in0=dstv, in1=dstv, op=ALU.mult),
                   r=[urb], w=[u2T])
        for tl in range(nt):
            ti = t0 + tl
            slot = ti % 2
            xt = xbuf[slot]
            ysrc = y_s if ti == NT else y_p[ti * 128:(ti + 1) * 128, :]
            T.dma("sp", xt[:], ysrc, r=[ybuf[ti]], w=[xt], key=f"x{slot}")
            pf = pu()
            for n in range(2):
                for c in range(32):
                    T.pe(lambda e, n=n, c=c, tl=tl, pf=pf: e.matmul(pf[:, n * 512:(n + 1) * 512], lhsT=u2T[:, c, tl * 128:(tl + 1) * 128],
                                                                   rhs=Wdn[c // 4][:, c % 4, n * 512:(n + 1) * 512], start=(c == 0), stop=(c == 31)),
                         r=[u2T, Wdn[c // 4]], w=[pf])
            post_norm_residual(pf, xt, gfpost)
            T.dma("sp", ysrc, xt[:], r=[xt], w=[ybuf[ti]], key=f"yo{slot}")
        t0 += nt

    T.emit(final=True)
    return nc, es


_CACHE = {}


def _consts():
    idf = np.eye(128, dtype=np.float32)
    s = np.arange(128)
    ltri = (s[:, None] <= s[None, :]).astype(np.float32) / 16.0
    utri = (s[:, None] > s[None, :]).astype(np.float32) / 16.0
    same = (s[:, None] // 8) == (s[None, :] // 8)
    ltri_s = ltri * same
    utri_s = utri * same
    cm = (s[:, None] <= s[None, :]).astype(np.float32)
    cm_s = cm * same
    q = np.arange(128)[:, None]
    kk = np.arange(256)[None, :]
    rel = (q + 128) - kk
    mask = np.where((rel >= 0) & (rel < 128), 0.0, NEG).astype(np.float32)
    mask0 = mask.copy()
    mask0[:, :128] = NEG
    r = np.arange(128)
    t_of = (r % 8)[:, None]
    masks = np.zeros((128, 136), np.float32)
    jj = np.arange(128)[None, :]
    masks[:32, :128] = np.where(jj >= t_of[:32] + 1, 0.0, NEG)
    masks[:32, 128:] = np.where(np.arange(8)[None, :] <= t_of[:32], 0.0, NEG)
    rowmask = ((r[:, None] // 8) == np.arange(16)[None, :]).astype(np.float32)
    bm = ((np.arange(128)[None, :] // 8) == np.arange(16)[:, None]).astype(np.float32).reshape(1, 2048)
    extra = dict(c_masks=masks, c_rowmask=rowmask, c_seqsel=rowmask / 16.0, c_bmask=np.ascontiguousarray(np.broadcast_to(bm, (128, 2048))))
    return dict(**extra, c_idf=idf, c_ltri=ltri, c_utri=utri, c_ltri_s=ltri_s.astype(np.float32), c_utri_s=utri_s.astype(np.float32),
                c_cm=cm, c_cm_s=cm_s.astype(np.float32), c_mask=mask, c_mask0=mask0)


def _rope_table(pos):
    half = 32
    inv = (10000.0 ** (-np.arange(half, dtype=np.float32) / half)).astype(np.float32)
    ang = pos.astype(np.float32)[:, None] * inv[None, :]
    return np.concatenate([np.cos(ang), np.sin(ang)], axis=1).astype(np.float32)


def kernel(x_prompt, x_sample, cache_k, cache_v, state_gla, w_in, w_gk2, b_gk, g_gla, sinks,
           w_out, g_mix_pre, g_mix_post, g_ffn_pre, g_ffn_post, w_up, w_down):
    f = lambda a: np.ascontiguousarray(np.asarray(a, dtype=np.float32))
    x_prompt, x_sample, cache_k, cache_v, state_gla = map(f, (x_prompt, x_sample, cache_k, cache_v, state_gla))
    if "nc" not in _CACHE:
        _CACHE["nc"] = build_program()
    nc, _es = _CACHE["nc"]
    consts = _consts()
    wgk = np.zeros((32, 256), np.float32)
    wgk[0:16] = f(w_gk2)[0]
    wgk[16] = f(b_gk)[0]
    rep = lambda v, n=128: np.ascontiguousarray(np.broadcast_to(f(v).reshape(1, -1), (n, f(v).size)))
    shared = dict(
        w_in=f(w_in)[0], w_out=f(w_out)[0], w_up=f(w_up)[0], w_down=f(w_down)[0], wgk=wgk,
        c_gpre=np.ascontiguousarray(f(g_mix_pre)[0].reshape(8, 128).T), c_gfpre=np.ascontiguousarray(f(g_ffn_pre)[0].reshape(8, 128).T),
        c_gpost=rep(g_mix_post[0]), c_gfpost=rep(g_ffn_post[0]), c_ggla=rep(np.tile(f(g_gla)[0], 4)), c_sink=rep(sinks[0]),
        cs_s=_rope_table(16384 + (np.arange(128) % 8)),
        c_sinks=np.ascontiguousarray(f(sinks)[0].reshape(2, 4).T[:, None, :].repeat(8, axis=1).reshape(32, 2)), **consts)
    in_maps = []
    for c in range(NCORES):
        b, j = c // 4, c % 4
        start = j * NT * 128
        xh = np.zeros((NH * 128, D), np.float32)
        if start > 0:
            hist = x_prompt[b, max(0, start - NH * 128):start]
            xh[NH * 128 - hist.shape[0]:] = hist
        m = dict(shared)
        m.update(
            xp=x_prompt[b, start:start + NT * 128], xh=xh, xs=x_sample[16 * c:16 * (c + 1)].reshape(128, D),
            ck=cache_k[0, 16 * c:16 * (c + 1)].reshape(16, 128, 128), cv=cache_v[0, 16 * c:16 * (c + 1)].reshape(16, 128, 128),
            sg=state_gla[0, 16 * c:16 * (c + 1)],
            cs_p=_rope_table(start + np.arange(NT * 128)), cs_h=_rope_table(np.maximum(start - 128 + np.arange(128), 0)),
        )
        if j != 0:
            m["c_mask0"] = consts["c_mask"]
        in_maps.append({k: np.ascontiguousarray(v, dtype=np.float32) for k, v in m.items()})
    if _CACHE.get("prep_only"):
        return nc, in_maps
    res = run_bass_kernel_spmd(nc, in_maps, core_ids=list(range(NCORES)))
    R = res.results
    if NT != 16:
        return R
    y_prompt = np.stack([np.concatenate([R[b * 4 + j]["y_p"] for j in range(4)], axis=0) for b in range(2)])
    y_sample = np.concatenate([R[c]["y_s"] for c in range(NCORES)], axis=0).reshape(128, 8, D)
    kwp = np.stack([R[b * 4 + 3]["kw_p"].reshape(128, 2, 64) for b in range(2)])[None]
    vwp = np.stack([R[b * 4 + 3]["vw_p"].reshape(128, 2, 64) for b in range(2)])[None]
    def gl(a):
        a = a.reshape(2, 64, 2, 128)
        return np.ascontiguousarray(a.transpose(2, 0, 1, 3)).reshape(4, 64, 128)
    glp = np.stack([gl(R[b * 4 + 3]["gl_p"]) for b in range(2)])[None]
    kws = np.concatenate([R[c]["kw_s"].reshape(16, 128, 2, 64) for c in range(NCORES)], axis=0)[None]
    vws = np.concatenate([R[c]["vw_s"].reshape(16, 128, 2, 64) for c in range(NCORES)], axis=0)[None]
    gls = np.concatenate([R[c]["gl_s"] for c in range(NCORES)], axis=0)[None]
    return (y_prompt.astype(np.float32), y_sample.astype(np.float32), kwp.astype(np.float32), vwp.astype(np.float32),
            glp.astype(np.float32), kws.astype(np.float32), vws.astype(np.float32), gls.astype(np.float32))
```

```python
import contextlib
import sys as _sys
import numpy as np
import concourse.bass as bass
import concourse.mybir as mybir
from concourse.bass_utils import run_bass_kernel_spmd

F32 = mybir.dt.float32
BF16 = mybir.dt.bfloat16
AF = mybir.ActivationFunctionType
ALU = mybir.AluOpType
AX = mybir.AxisListType

D = 1024
NCORES = 8
NT = 16
NH = 48
INW = 2320
DFF = 4096
EPS = 1e-6
NEG = -30000.0
CQ, CK, CV, CGQ, CGK, CGLR, CGV, CGR = 0, 512, 640, 768, 1024, 1280, 1296, 1808


class Buf:
    def __init__(self, name, t):
        self.name, self.t = name, t
        self.w = None
        self.r = []
        self.psum = False

    def __getitem__(self, k):
        return self.t[k]


ROT = {"p": 0}


class Rot:
    def __init__(self, bufs):
        self.bufs = list(bufs)

    @property
    def cur(self):
        return self.bufs[ROT["p"] % len(self.bufs)]

    def __getitem__(self, k):
        return self.cur.t[k]

    t = property(lambda s: s.cur.t)
    psum = property(lambda s: s.cur.psum)
    name = property(lambda s: s.cur.name)
    w = property(lambda s: s.cur.w, lambda s, v: setattr(s.cur, "w", v))
    r = property(lambda s: s.cur.r, lambda s, v: setattr(s.cur, "r", v))


class Inst:
    __slots__ = ("eng", "fn", "deps", "ticket", "key", "needs", "rot", "seg", "odeps", "src")

    def __init__(self, eng, fn, key=None):
        self.eng, self.fn, self.key = eng, fn, key
        self.rot = ROT["p"]
        self.seg = None
        self.odeps = []
        self.deps = []
        self.ticket = None
        self.needs = False


class Tracker:
    def __init__(self, nc, es):
        self.nc, self.es = nc, es
        self.engs = {"pe": nc.tensor, "act": nc.scalar, "dve": nc.vector, "pool": nc.gpsimd, "sp": nc.sync}
        self.order = []
        self.pos = 0
        self.esem = None
        self.ksem, self.kcnt, self.ecnt = {}, {}, {e: 0 for e in self.engs}
        self.waited = {e: {} for e in self.engs}
        self.persist = []
        self.allbufs = []
        self.pending_bar = {}
        self.seg = None
        self.groups = {}

    def sb(self, name, shape, dt, scope=None):
        b = Buf(name, (scope or self.es).enter_context(self.nc.sbuf_tensor(name, list(shape), dt)))
        self.allbufs.append(b)
        if scope is None:
            self.persist.append(b)
        return b

    def ps(self, name, shape, dt):
        b = Buf(name, self.es.enter_context(self.nc.psum_tensor(name, list(shape), dt)))
        b.psum = True
        self.persist.append(b)
        return b

    def op(self, eng, fn, r=(), w=(), key=None, extra=()):
        ins = Inst(eng, fn, key)
        f = _sys._getframe(1)
        while f.f_code.co_name in ("pe", "act", "dve", "pool", "dma", "op"):
            f = f.f_back
        ins.src = "%s:%d" % (f.f_code.co_name, f.f_lineno)
        if _CACHE.get("trace_lines"):
            f = _sys._getframe(1)
            while f.f_code.co_name in ("pe", "act", "dve", "pool", "dma", "op"):
                f = f.f_back
            _CACHE.setdefault("lines", []).append((len(self.order), eng, f.f_code.co_name, f.f_lineno))
        deps = list(extra) + list(self.pending_bar.pop(eng, ()))
        for b in r:
            if b.w is not None:
                deps.append(b.w)
            if b.psum:
                deps.extend(x for x in b.r if x.eng != eng)
        for b in w:
            if b.w is not None:
                deps.append(b.w)
            deps.extend(b.r)
        seen = set()
        ins.seg = self.seg
        for d in deps:
            if id(d) in seen or d is ins:
                continue
            seen.add(id(d))
            ins.odeps.append(d)
            if d.key is not None and d.key.startswith("G_") and d.key != key:
                ins.odeps.extend(self.groups.get(d.key, ()))
            if d.key is None and d.eng == "pe" and eng == "pe" and key is None:
                continue
            if key is not None and d.key == key and key.startswith("G_"):
                continue
            ins.deps.append(d)
            d.needs = True
        for b in r:
            b.r.append(ins)
        for b in w:
            b.w = ins
            b.r = []
        if key is not None and key.startswith("G_"):
            self.groups.setdefault(key, []).append(ins)
        self.order.append(ins)
        return ins

    def pe(self, fn, r=(), w=()):
        return self.op("pe", fn, r, w)

    def act(self, fn, r=(), w=()):
        return self.op("act", fn, r, w)

    def dve(self, fn, r=(), w=()):
        return self.op("dve", fn, r, w)

    def pool(self, fn, r=(), w=()):
        return self.op("pool", fn, r, w)

    def dma(self, q, out, in_, r=(), w=(), key=None, extra=()):
        return self.op(q, lambda e: e.dma_start(out=out, in_=in_), r, w, key=key, extra=extra)

    def seal(self):
        bar = []
        last = {}
        for ins in self.order:
            last[ins.eng] = ins
            if ins.key is not None:
                bar.append(ins)
        for ins in last.values():
            if ins.key is None:
                ins.needs = True
                bar.append(ins)
        for b in self.persist + self.allbufs:
            for ins in ([b.w] if b.w is not None else []) + list(b.r):
                ins.needs = True
        return bar

    @staticmethod
    def _zipmerge(F, B):
        inF = {id(x): k for k, x in enumerate(F)}
        out, pf, pb = [], 0, 0
        while pf < len(F) or pb < len(B):
            take_b = pb < len(B) and (pf >= len(F) or pb * len(F) <= pf * len(B))
            if take_b:
                ins = B[pb]
                need = max((inF[id(d)] for d in ins.odeps if id(d) in inF), default=-1)
                while pf <= need:
                    out.append(F[pf]); pf += 1
                out.append(ins); pb += 1
            else:
                out.append(F[pf]); pf += 1
        return out

    COST = {"pe": 0.20, "act": 0.40, "dve": 0.40, "pool": 0.80, "sp": 0.08}
    FCOST = {
        ("pe", "proj"): 0.35, ("pe", "norm_T"): 0.14, ("pe", "main_back"): 0.30, ("pe", "attention_prompt"): 0.15,
        ("pe", "gla_intra_and_out"): 0.15, ("pe", "inter_prompt"): 0.15, ("pe", "gate_common"): 0.50, ("pe", "ablk_prompt"): 0.20,
        ("pe", "state_update_prompt"): 0.45, ("pe", "k_transpose"): 0.14,
        ("act", "norm_T"): 0.75, ("act", "rstd_from_ss"): 0.25, ("act", "main_front"): 0.40, ("act", "rope_kv"): 0.30,
        ("act", "evac_gate"): 0.30, ("act", "gate_out_prep"): 0.60, ("act", "attention_prompt"): 0.45, ("act", "gate_common"): 0.35,
        ("act", "gla_intra_and_out"): 0.35, ("act", "gla_finish"): 0.30, ("act", "post_norm_residual"): 0.75, ("act", "main_back"): 0.60,
        ("act", "hist_front"): 0.40,
        ("dve", "norm_T"): 0.70, ("dve", "rope"): 0.35, ("dve", "attention_prompt"): 0.45, ("dve", "gate_common"): 0.35,
        ("dve", "gla_intra_and_out"): 0.50, ("dve", "gla_finish"): 0.35, ("dve", "state_update_prompt"): 0.33,
        ("dve", "post_norm_residual"): 1.10, ("dve", "evac_gate"): 0.40, ("dve", "main_front"): 0.40,
        ("pool", "rope"): 0.60, ("dve", "post_norm_residual"): 1.10, ("pool", "gate_out_prep"): 1.10, ("pool", "state_update_prompt"): 0.80,
        ("pool", "rope_kv"): 0.40,
    }

    def _schedule(self, todo):
        import heapq
        idx = {id(x): k for k, x in enumerate(todo)}
        n = len(todo)
        ndep = [0] * n
        users = [[] for _ in range(n)]
        for k, ins in enumerate(todo):
            ds = {idx[id(d)] for d in ins.odeps if id(d) in idx}
            ndep[k] = len(ds)
            for j in ds:
                users[j].append(k)
        fin = [0.0] * n
        ready_t = [0.0] * n
        crit = [None] * n
        elast = {e: None for e in self.engs}
        efree = {e: 0.0 for e in self.engs}
        ready = {e: [] for e in self.engs}
        for k in range(n):
            if ndep[k] == 0:
                heapq.heappush(ready[todo[k].eng], k)
        out = []
        LOOK = 24
        while len(out) < n:
            best = None
            for e, hp in ready.items():
                if not hp:
                    continue
                cands = heapq.nsmallest(LOOK, hp)
                for k in cands:
                    st = max(efree[e], ready_t[k])
                    key = (st, k)
                    if best is None or key < best[0]:
                        best = (key, e, k)
            (st, _), e, k = best
            ready[e].remove(k)
            heapq.heapify(ready[e])
            ins = todo[k]
            is_dma = ins.key is not None
            dur = (0.08 if is_dma else self.FCOST.get((e, ins.src.split(":")[0]), self.COST[e]))
            if efree[e] > ready_t[k] and elast[e] is not None:
                crit[k] = ("eng", elast[e])
            elast[e] = k
            efree[e] = st + dur
            fin[k] = st + (3.0 if is_dma else dur)
            out.append(ins)
            for u in users[k]:
                lat = 0.1 if (todo[u].eng == e and not is_dma) else 0.8
                if fin[k] + lat > ready_t[u]:
                    ready_t[u] = fin[k] + lat
                    if crit[u] is None or crit[u][0] != "eng":
                        crit[u] = ("dep", k)
                ndep[u] -= 1
                if ndep[u] == 0:
                    heapq.heappush(ready[todo[u].eng], u)
        if _CACHE.get("crit_seg"):
            tgt = _CACHE["crit_seg"]
            ks = [k for k, ins in enumerate(todo) if ins.seg == tgt]
            k = max(ks, key=lambda q: fin[q]) if ks else None
            lines = {t[0]: t for t in _CACHE.get("lines", [])}
            base = self.pos
            prevdesc = None
            hops = 0
            while k is not None and hops < 4000:
                ins = todo[k]
                li = lines.get(base + self.order[self.pos:].index(ins)) if False else None
                desc = (ins.seg, ins.eng, getattr(ins, "src", None), crit[k][0] if crit[k] else None)
                if desc != prevdesc:
                    print("crit: t=%.1f" % fin[k], desc)
                    prevdesc = desc
                k = crit[k][1] if crit[k] else None
                hops += 1
                if ins.seg is not None and ins.seg[1] < tgt[1] - 1:
                    break
        if _CACHE.get("sched_report"):
            last = {}
            for k, ins in enumerate(todo):
                if ins.seg is not None:
                    last[ins.seg] = max(last.get(ins.seg, 0.0), fin[k])
            prev = 0.0
            for sg in sorted(last, key=lambda x: (x[1], x[0])):
                if sg[0] == "B":
                    print("sched est: seg", sg, "done at %.1f us (+%.1f)" % (last[sg], last[sg] - prev))
                    prev = last[sg]
        return out

    def _reorder(self, todo):
        runs = []
        for ins in todo:
            if runs and runs[-1][0] == ins.seg:
                runs[-1][1].append(ins)
            else:
                runs.append((ins.seg, [ins]))
        out, k = [], 0
        while k < len(runs):
            lab, lst = runs[k]
            if lab is not None and lab[0] == "F" and k + 1 < len(runs) and runs[k + 1][0] is not None and runs[k + 1][0][0] == "B":
                out.extend(self._zipmerge(lst, runs[k + 1][1]))
                k += 2
            else:
                out.extend(lst)
                k += 1
        assert len(out) == len(todo)
        return out

    def emit(self, final=False):
        nc, es = self.nc, self.es
        if self.esem is None:
            self.esem = {e: es.enter_context(nc.semaphore("se_" + e)) for e in self.engs}
        esem, ksem, kcnt, ecnt, waited = self.esem, self.ksem, self.kcnt, self.ecnt, self.waited
        todo = self.order[self.pos:]
        if _CACHE.get('zipmerge'):
            todo = self._reorder(todo)
        elif not _CACHE.get('no_sched'):
            todo = self._schedule(todo)
        if _CACHE.get('maxinst'):
            todo = todo[:_CACHE['maxinst']]
        for ins in todo:
            if ins.key is not None:
                if ins.key not in ksem:
                    ksem[ins.key] = es.enter_context(nc.semaphore("sk_" + ins.key))
                    kcnt[ins.key] = 0
                kcnt[ins.key] += 16
                ins.ticket = kcnt[ins.key]
            elif ins.needs:
                ecnt[ins.eng] += 1
                ins.ticket = ecnt[ins.eng]
        for ins in todo:
            eng = self.engs[ins.eng]
            wl = {}
            for d in ins.deps:
                if d.key is not None:
                    grp = d.key.startswith("G_")
                    s, v = ksem[d.key], (kcnt[d.key] if grp else d.ticket)
                else:
                    s, v = esem[d.eng], d.ticket
                assert v is not None, (ins.eng, d.eng)
                if wl.get(s.name, (None, 0))[1] < v:
                    wl[s.name] = (s, v)
            for nm, (s, v) in wl.items():
                if waited[ins.eng].get(nm, 0) >= v:
                    continue
                waited[ins.eng][nm] = v
                eng.wait_ge(s, v)
            ROT["p"] = ins.rot
            bi = ins.fn(eng)
            if ins.key is not None:
                bi.then_inc(ksem[ins.key], 16)
            elif ins.needs:
                bi.then_inc(esem[ins.eng], 1)
        self.pos = len(self.order)
        if final:
            for k, s in ksem.items():
                nc.sync.wait_ge(s, kcnt[k])
            print("bass program: insts", len(self.order), "sems", len(ksem) + 5, "eng tickets", ecnt)


def build_program():
    nc = bass.Bass("TRN2", target_bir_lowering=False)
    es = contextlib.ExitStack()
    T = Tracker(nc, es)

    def din(name, shape):
        return nc.dram_tensor(name, list(shape), F32, kind="ExternalInput").ap()

    def dout(name, shape):
        return nc.dram_tensor(name, list(shape), F32, kind="ExternalOutput").ap()

    xp = din("xp", [NT * 128, D]); xh = din("xh", [NH * 128, D]); xs = din("xs", [128, D])
    ck = din("ck", [16, 128, 128]); cv = din("cv", [16, 128, 128]); sg = din("sg", [16, 4, 64, 128])
    w_in = din("w_in", [D, INW]); w_out = din("w_out", [D, D]); w_up = din("w_up", [D, DFF]); w_down = din("w_down", [DFF, D])
    wgk = din("wgk", [32, 256])
    cs_p = din("cs_p", [NT * 128, 64]); cs_h = din("cs_h", [128, 64]); cs_s = din("cs_s", [128, 64])
    c_idf = din("c_idf", [128, 128]); c_ltri = din("c_ltri", [128, 128]); c_utri = din("c_utri", [128, 128])
    c_ltri_s = din("c_ltri_s", [128, 128]); c_utri_s = din("c_utri_s", [128, 128])
    c_cm = din("c_cm", [128, 128]); c_cm_s = din("c_cm_s", [128, 128])
    c_mask = din("c_mask", [128, 256]); c_mask0 = din("c_mask0", [128, 256])
    c_gpre = din("c_gpre", [128, 8]); c_gfpre = din("c_gfpre", [128, 8])
    c_gpost = din("c_gpost", [128, D]); c_gfpost = din("c_gfpost", [128, D]); c_ggla = din("c_ggla", [128, 512])
    c_sink = din("c_sink", [128, 8])
    c_masks = din("c_masks", [128, 136]); c_sinks = din("c_sinks", [32, 2]); c_bmask = din("c_bmask", [128, 16 * 128])
    c_rowmask = din("c_rowmask", [128, 16]); c_seqsel = din("c_seqsel", [128, 16])
    scr = nc.dram_tensor("scr_attn", [16, 8, 2, 4, 64], BF16).ap()

    y_p = dout("y_p", [NT * 128, D]); y_s = dout("y_s", [128, D])
    kw_p = dout("kw_p", [128, 128]); vw_p = dout("vw_p", [128, 128]); gl_p = dout("gl_p", [128, 2, 128])
    kw_s = dout("kw_s", [16, 128, 128]); vw_s = dout("vw_s", [16, 128, 128]); gl_s = dout("gl_s", [16, 4, 64, 128])

    h2T = T.sb("h2T", [128, 8, (NT + 1) * 128], BF16)
    xbuf = [T.sb(f"xbuf{i}", [128, D], F32) for i in range(3)]
    st = T.sb("st", [128, 16], F32)
    stb = T.sb("stb", [128, 16], F32)
    tmp = T.sb("tmp", [128, D], F32)
    epsb = T.sb("epsb", [128, 1], F32)
    esA = contextlib.ExitStack()
    _sb = T.sb
    A = lambda name, shape, dt: _sb(name, shape, dt, scope=esA)
    Win = A("Win", [128, 8, INW], BF16)
    Wout = A("Wout", [128, 8, D], BF16)
    idf = A("idf", [128, 128], F32); idb = A("idb", [128, 128], BF16)
    ltri = A("ltri", [128, 128], F32); utri = A("utri", [128, 128], F32)
    ltri_s = A("ltri_s", [128, 128], F32); utri_s = A("utri_s", [128, 128], F32)
    cm = A("cm", [128, 128], F32); cm_s = A("cm_s", [128, 128], F32)
    maskb = A("maskb", [128, 256], BF16); mask0b = A("mask0b", [128, 256], BF16)
    gpre = A("gpre", [128, 8], F32); gfpre = A("gfpre", [128, 8], F32)
    gpost = A("gpost", [128, D], F32); ggla = A("ggla", [128, 512], F32)
    sink = A("sink", [128, 8], F32)
    wgk_sb = A("wgk_sb", [32, 256], F32)
    ones16 = A("ones16", [128, 2], F32)
    glrT = A("glrT", [32, 128], F32)
    S = A("S", [128, 2, 128], F32); Sb = A("Sb", [128, 2, 128], BF16)
    csb = [A(f"csb{i}", [128, 64], F32) for i in range(2)]
    hb = A("hb", [128, D], BF16)
    hT = A("hT", [128, 8, 128], BF16)
    qk = A("qk", [128, 10, 64], F32); qkr = A("qkr", [128, 10, 64], F32); qkb = A("qkb", [128, 10, 64], BF16)
    rt = [A(f"rt{i}", [128, 10, 32], F32) for i in range(2)]
    vf = A("vf", [128, 128], F32)
    vb = [A(f"vb{i}", [128, 128], BF16) for i in range(3)]
    kT = [A(f"kT{i}", [128, 128], BF16) for i in range(3)]
    qT = A("qT", [128, 4, 128], BF16)
    Pm = A("Pm", [128, 8, 256], BF16)
    PT = A("PT", [128, 16, 128], BF16)
    ast = A("ast", [128, 48], F32)
    mix = A("mix", [128, D], BF16)
    mixT = A("mixT", [128, 8, 128], BF16)
    glr_sb = A("glr_sb", [128, 16], F32)
    gk_sb = A("gk_sb", [128, 256], F32); gq_sb = A("gq_sb", [128, 256], F32)
    az = A("az", [128, 256], F32); ez = A("ez", [128, 256], F32); la = A("la", [128, 256], F32)
    eB = A("eB", [128, 256], F32); eNB = A("eNB", [128, 256], F32); eR = A("eR", [128, 256], F32)
    ablk = A("ablk", [128, 2], F32)
    qd = A("qd", [128, 256], BF16); ki = A("ki", [128, 256], BF16); kd = A("kd", [128, 256], BF16)
    qdT = A("qdT", [128, 2, 128], BF16)
    qdTz = A("qdTz", [128, 4, 128], BF16); kiTz = A("kiTz", [128, 4, 128], BF16)
    ATm = A("ATm", [128, 4, 128], BF16)
    gvb = A("gvb", [128, 512], BF16)
    sil = A("sil", [128, 512], F32); t1 = A("t1", [128, 512], F32)
    gst = A("gst", [128, 8], F32)
    esS = contextlib.ExitStack()
    SA = lambda name, shape, dt: _sb(name, shape, dt, scope=esS)
    cvb = SA("cvb", [128, 16, 128], BF16)
    kTcz = SA("kTcz", [128, 2, 16, 128], BF16); kTnz = SA("kTnz", [128, 2, 128], BF16)
    qTs = SA("qTs", [128, 16, 4, 8], BF16)
    vn = SA("vn", [8, 16, 128], BF16)
    masks = SA("masks", [128, 136], BF16); sink_s = SA("sink_s", [32, 2], F32)
    Pn = SA("Pn", [32, 8, 8], BF16); PTn = SA("PTn", [8, 8, 32], BF16)
    attn_s = SA("attn_s", [32, 2, 16, 64], BF16)
    sst = SA("sst", [32, 64], F32)
    bmask = SA("bmask", [128, 16, 128], BF16); rowmask = SA("rowmask", [128, 16], F32); seqsel = SA("seqsel", [128, 16], F32)
    ablk_s = SA("ablk_s", [128, 16, 2], F32)
    qdTm = [SA(f"qdTm{i}", [128, 16, 128], BF16) for i in range(2)]
    S0F = [SA(f"S0f{i}", [128, 2, 2, 128], F32) for i in range(2)]
    S0B = [SA(f"S0blk{i}", [128, 2, 2, 256], BF16) for i in range(2)]
    ckb_v = Pm.t[:].rearrange("p a (c k) -> p (a c) k", c=2)
    kdm_v = PT.t[:, 8:16, :].rearrange("p a k -> p (a k)").rearrange("p (b c) -> p b c", b=4)

    PU = [T.ps(f"PU{i}", [128, 1024], F32) for i in range(3)]
    PTb = [T.ps(f"PTb{i}", [128, 1024], BF16) for i in range(2)]
    cnt = {"u": 0, "t": 0}

    def pu(avoid=None, small=False):
        if T.seg is not None and T.seg[0] == "F":
            return PU[0]
        if T.seg is not None and T.seg[0] == "B":
            return PU[1 + T.seg[1] % 2]
        pool_ = (PU + cnt.get("extra_small", [])) if small else PU
        while True:
            cnt["u"] += 1
            u = pool_[cnt["u"] % len(pool_)]
            if u is not avoid:
                return u

    def ptb():
        if cnt.get("force_t") is not None:
            return PTb[cnt["force_t"]]
        if T.seg is not None and T.seg[0] == "F":
            return PTb[0]
        if T.seg is not None and T.seg[0] == "B":
            return PTb[1]
        cnt["t"] += 1
        return PTb[cnt["t"] % 2]

    for (sbt, dr) in ((idf, c_idf), (ltri, c_ltri), (utri, c_utri), (ltri_s, c_ltri_s), (utri_s, c_utri_s), (cm, c_cm),
                      (cm_s, c_cm_s), (gpre, c_gpre), (gfpre, c_gfpre), (gpost, c_gpost),
                      (ggla, c_ggla), (sink, c_sink), (wgk_sb, wgk)):
        T.dma("sp", sbt[:], dr, w=[sbt], key="G_const")
    for (sbt, dr) in ((sink_s, c_sinks), (rowmask, c_rowmask), (seqsel, c_seqsel)):
        T.dma("sp", sbt[:], dr, w=[sbt], key="G_const")
    for (sbt, dr) in ((idb, c_idf), (maskb, c_mask), (mask0b, c_mask0), (masks, c_masks)):
        T.dma("pool", sbt[:], dr, w=[sbt], key="G_constb")
    T.dve(lambda e: e.memset(ones16[:], 1.0 / 16.0), w=[ones16])
    T.dve(lambda e: e.memset(epsb[:], EPS), w=[epsb])
    T.dve(lambda e: e.memset(glrT[:], 1.0), w=[glrT])
    T.dve(lambda e: e.memset(S[:], 0.0), w=[S])
    T.dve(lambda e: e.memset(Sb[:], 0.0), w=[Sb])
    T.dve(lambda e: e.memset(qdTz[:], 0.0), w=[qdTz])
    T.dve(lambda e: e.memset(kiTz[:], 0.0), w=[kiTz])
    T.dve(lambda e: e.memset(kTcz[:], 0.0), w=[kTcz])
    T.dve(lambda e: e.memset(kTnz[:], 0.0), w=[kTnz])
    for sb_ in S0B:
        T.dve(lambda e, sb_=sb_: e.memset(sb_[:], 0.0), w=[sb_])
    for k in range(8):
        rows = slice(k * 128, (k + 1) * 128)
        T.dma("pool", Win[:, k, 0:1280], w_in[rows, 0:1280], w=[Win], key="G_win")
        T.dma("pool", Win[:, k, 1280:1296], w_in[rows, 2304:2320], w=[Win], key="G_win")
        T.dma("pool", Win[:, k, 1296:2320], w_in[rows, 1280:2304], w=[Win], key="G_win")
    for k in range(8):
        T.dma("pool", Wout[:, k, :], w_out[k * 128:(k + 1) * 128, :], w=[Wout], key="G_wout")
    T.dma("pool", ckb_v, ck.rearrange("b j c -> j b c"), w=[Pm], key="G_cache")
    T.dma("pool", cvb[:], cv.rearrange("b j c -> j b c"), w=[cvb], key="G_cache")
    T.dma("pool", bmask[:].rearrange("p a t -> p (a t)"), c_bmask, w=[bmask], key="G_cache")

    T.dma("sp", kw_s[:, 0:120, :], ck[:, 8:128, :], key="o_kws")
    T.dma("sp", vw_s[:, 0:120, :], cv[:, 8:128, :], key="o_vws")

    def rstd_from_ss(ss_ap, n, out_ap, rbufs, wbufs):
        T.act(lambda e: e.activation(out=out_ap, in_=ss_ap, func=AF.Ln, scale=1.0 / n, bias=epsb[:, 0:1]), r=list(rbufs) + [epsb], w=wbufs)
        T.act(lambda e: e.activation(out=out_ap, in_=out_ap, func=AF.Exp, scale=-0.5), r=wbufs, w=wbufs)

    def front(x_dram, slot, gcol, dst, dst_ap):
        xt = xbuf[slot]
        T.dma("sp", xt[:], x_dram, w=[xt], key=f"x{slot}")
        norm_T(xt, gcol, dst, dst_ap)

    def norm_T(xt, gcol, dst, dst_ap, tail=False):
        sx = stb if tail else st
        c0 = 4 if tail else 0
        if tail and isinstance(hb, Rot):
            hx = hb.bufs[(ROT["p"] + 1) % len(hb.bufs)]
        else:
            hx = hb.cur if isinstance(hb, Rot) else hb
        if isinstance(sx, Rot):
            sx = sx.cur
        T.act(lambda e: e.activation(out=hx[:], in_=xt[:], func=AF.Square, accum_out=sx[:, c0:c0 + 1]), r=[xt], w=[sx, hx])
        rstd_from_ss(sx[:, c0:c0 + 1], D, sx[:, c0 + 1:c0 + 2], [sx], [sx])
        T.dve(lambda e: e.tensor_scalar(out=hx[:], in0=xt[:], scalar1=sx[:, c0 + 1:c0 + 2], scalar2=None, op0=ALU.mult), r=[xt, sx], w=[hx])
        pt = ptb()
        for k in range(8):
            T.pe(lambda e, k=k: e.transpose(out=pt[:, k * 128:(k + 1) * 128], in_=hx[:, k * 128:(k + 1) * 128], identity=idb[:]),
                 r=[hx, idb], w=[pt])
        T.dve(lambda e: e.tensor_tensor(out=dst_ap, in0=pt[:].rearrange("p (k t) -> p k t", k=8),
                                        in1=gcol[:].unsqueeze(2).to_broadcast([128, 8, 128]), op=ALU.mult),
              r=[pt, gcol], w=[dst])

    def proj(ps_ap, c0, n, ps):
        for k in range(8):
            T.pe(lambda e, k=k: e.matmul(ps_ap, lhsT=hT[:, k, :], rhs=Win[:, k, c0:c0 + n], start=(k == 0), stop=(k == 7)),
                 r=[hT, Win], w=[ps])

    def rope_kv(psB, cst, kslot, write_kT=True):
        T.act(lambda e: e.copy(out=qk[:, 8:10, :], in_=psB[:, 0:128].rearrange("p (h d) -> p h d", h=2)), r=[psB], w=[qk])
        T.act(lambda e: e.copy(out=vf[:], in_=psB[:, 128:256]), r=[psB], w=[vf])
        T.pool(lambda e: e.tensor_copy(out=vb[kslot][:], in_=vf[:]), r=[vf], w=[vb[kslot]])

    def rope(cst, h0, h1):
        n = h1 - h0
        cos = cst[:, 0:32].unsqueeze(1).to_broadcast([128, n, 32])
        sin = cst[:, 32:64].unsqueeze(1).to_broadcast([128, n, 32])
        x1, x2 = qk[:, h0:h1, 0:32], qk[:, h0:h1, 32:64]
        T.dve(lambda e: e.tensor_tensor(out=rt[0][:, h0:h1, :], in0=x1, in1=cos, op=ALU.mult), r=[qk, cst], w=[rt[0]])
        T.pool(lambda e: e.tensor_tensor(out=rt[1][:, h0:h1, :], in0=x2, in1=sin, op=ALU.mult), r=[qk, cst], w=[rt[1]])
        T.dve(lambda e: e.tensor_tensor(out=qkr[:, h0:h1, 0:32], in0=rt[0][:, h0:h1, :], in1=rt[1][:, h0:h1, :], op=ALU.subtract),
              r=[rt[0], rt[1]], w=[qkr])
        T.dve(lambda e: e.tensor_tensor(out=rt[0][:, h0:h1, :], in0=x2, in1=cos, op=ALU.mult), r=[qk, cst], w=[rt[0]])
        T.pool(lambda e: e.tensor_tensor(out=rt[1][:, h0:h1, :], in0=x1, in1=sin, op=ALU.mult), r=[qk, cst], w=[rt[1]])
        T.dve(lambda e: e.tensor_tensor(out=qkr[:, h0:h1, 32:64], in0=rt[0][:, h0:h1, :], in1=rt[1][:, h0:h1, :], op=ALU.add),
              r=[rt[0], rt[1]], w=[qkr])
        if h0 == 0:
            for g in range(2):
                T.dve(lambda e, g=g: e.tensor_copy(out=qkb[:, g:8:2, :], in_=qkr[:, 4 * g:4 * g + 4, :]), r=[qkr], w=[qkb])
        T.dve(lambda e: e.tensor_copy(out=qkb[:, 8:10, :], in_=qkr[:, 8:10, :]), r=[qkr], w=[qkb])

    def k_transpose(kslot):
        pt = ptb()
        T.pe(lambda e: e.transpose(out=pt[:, 0:128], in_=qkb[:, 8:10, :].rearrange("p h d -> p (h d)"), identity=idb[:]),
             r=[qkb, idb], w=[pt])
        T.act(lambda e: e.copy(out=kT[kslot][:], in_=pt[:, 0:128]), r=[pt], w=[kT[kslot]])

    def evac_gate(psC):
        T.act(lambda e: e.copy(out=glr_sb[:], in_=psC[:, 256:272]), r=[psC], w=[glr_sb])
        T.dve(lambda e: e.tensor_copy(out=gk_sb[:], in_=psC[:, 0:256]), r=[psC], w=[gk_sb])

    def gate_common(ltm, utm, full):
        px = pu()
        T.pe(lambda e: e.matmul(px[0:16, 0:128], lhsT=glr_sb[:], rhs=idf[:], start=True, stop=True), r=[glr_sb, idf], w=[px])
        T.act(lambda e: e.copy(out=glrT[0:16, :], in_=px[0:16, 0:128]), r=[px], w=[glrT])
        T.pe(lambda e: e.matmul(px[:, 512:768], lhsT=glrT[:], rhs=wgk_sb[:], start=True, stop=True), r=[glrT, wgk_sb], w=[px])
        z = px[:, 512:768]
        T.act(lambda e: e.activation(out=az[:], in_=z, func=AF.Abs), r=[px], w=[az])
        T.act(lambda e: e.activation(out=ez[:], in_=az[:], func=AF.Exp, scale=-1.0), r=[az], w=[ez])
        T.act(lambda e: e.activation(out=ez[:], in_=ez[:], func=AF.Ln, bias=1.0), r=[ez], w=[ez])
        T.dve(lambda e: e.tensor_single_scalar(out=az[:], in_=z, scalar=0.0, op=ALU.min), r=[px], w=[az])
        T.dve(lambda e: e.tensor_tensor(out=la[:], in0=az[:], in1=ez[:], op=ALU.subtract), r=[az, ez], w=[la])
        pb = pu()
        if full:
            T.pe(lambda e: e.matmul(pb[:, 0:256], lhsT=ltm[:], rhs=la[:], start=True, stop=True), r=[ltm, la], w=[pb])
        T.pe(lambda e: e.matmul(pb[:, 256:512], lhsT=utm[:], rhs=la[:], start=True, stop=True), r=[utm, la], w=[pb])
        if full:
            T.act(lambda e: e.activation(out=eB[:], in_=pb[:, 0:256], func=AF.Exp), r=[pb], w=[eB])
            T.act(lambda e: e.activation(out=eNB[:], in_=pb[:, 0:256], func=AF.Exp, scale=-1.0), r=[pb], w=[eNB])
        T.act(lambda e: e.activation(out=eR[:], in_=pb[:, 256:512], func=AF.Exp), r=[pb], w=[eR])
        T.dve(lambda e: e.tensor_tensor(out=kd[:], in0=gk_sb[:], in1=eR[:], op=ALU.mult), r=[gk_sb, eR], w=[kd])

    def ablk_prompt():
        pa = pu()
        for p in range(2):
            T.pe(lambda e, p=p: e.matmul(pa[:, 2 * p:2 * p + 2], lhsT=la[:, p * 128:(p + 1) * 128], rhs=ones16[:], start=True, stop=True),
                 r=[la, ones16], w=[pa])
        T.act(lambda e: e.activation(out=ablk[:], in_=pa[:, 0:4:2], func=AF.Exp), r=[pa], w=[ablk])

    def state_update_prompt():
        for p in range(2):
            pss = pu()
            T.pe(lambda e, p=p, pss=pss: e.matmul(pss[:, 0:256], lhsT=kd[:, p * 128:(p + 1) * 128], rhs=gvb[:, p * 256:(p + 1) * 256],
                                         start=True, stop=True), r=[kd, gvb], w=[pss])
            for hp in range(2):
                rows = slice(hp * 64, (hp + 1) * 64)
                T.dve(lambda e, p=p, rows=rows, hp=hp, pss=pss: e.scalar_tensor_tensor(
                    out=S[rows, p, :], in0=S[rows, p, :], scalar=ablk[rows, p:p + 1], in1=pss[rows, hp * 128:(hp + 1) * 128],
                    op0=ALU.mult, op1=ALU.add), r=[S, ablk, pss], w=[S])
        T.pool(lambda e: e.tensor_copy(out=Sb[:], in_=S[:]), r=[S], w=[Sb])

    def hist_front(i):
        last = (i == NH - 1)
        front(xh[i * 128:(i + 1) * 128, :], i % 2, gpre, hT, hT[:])
        psC = pu()
        proj(psC[:, 0:272], CGK, 272, psC)
        evac_gate(psC)
        psD = pu()
        proj(psD[:, 0:512], CGV, 512, psD)
        T.act(lambda e, psD=psD: e.copy(out=gvb[:], in_=psD[:, 0:512]), r=[psD], w=[gvb])
        if last:
            T.dma("sp", csb[1][:], cs_h, w=[csb[1]], key="cs1")
            psB = pu()
            proj(psB[:, 0:256], CK, 256, psB)
            rope_kv(psB, csb[1], 2)
            rope(csb[1], 8, 10)
            k_transpose(2)

    def hist_back(i):
        gate_common(ltri, utri, full=False)
        ablk_prompt()
        state_update_prompt()

    def gla_intra_and_out(cmask, inter_fn):
        T.dve(lambda e: e.scalar_tensor_tensor(out=qd[:], in0=gq_sb[:], scalar=0.125, in1=eB[:], op0=ALU.mult, op1=ALU.mult),
              r=[gq_sb, eB], w=[qd])
        T.dve(lambda e: e.tensor_tensor(out=ki[:], in0=gk_sb[:], in1=eNB[:], op=ALU.mult), r=[gk_sb, eNB], w=[ki])
        pt = ptb()
        for p in range(2):
            T.pe(lambda e, p=p: e.transpose(out=pt[:, p * 128:(p + 1) * 128], in_=qd[:, p * 128:(p + 1) * 128], identity=idb[:]),
                 r=[qd, idb], w=[pt])
            T.pe(lambda e, p=p: e.transpose(out=pt[:, 256 + p * 128:256 + (p + 1) * 128], in_=ki[:, p * 128:(p + 1) * 128], identity=idb[:]),
                 r=[ki, idb], w=[pt])
        T.act(lambda e: e.copy(out=qdT[:].rearrange("p a t -> p (a t)"), in_=pt[:, 0:256]), r=[pt], w=[qdT])
        for hp in range(2):
            rows = slice(hp * 64, (hp + 1) * 64)
            T.act(lambda e, rows=rows, hp=hp: e.copy(out=qdTz[rows, hp:4:2, :], in_=pt[rows, 0:256].rearrange("p (a t) -> p a t", a=2)),
                  r=[pt], w=[qdTz])
            T.act(lambda e, rows=rows, hp=hp: e.copy(out=kiTz[rows, hp:4:2, :], in_=pt[rows, 256:512].rearrange("p (a t) -> p a t", a=2)),
                  r=[pt], w=[kiTz])
        pat = pu()
        for h in range(4):
            p = h // 2
            T.pe(lambda e, h=h, p=p: e.matmul(pat[:, h * 128:(h + 1) * 128], lhsT=kiTz[:, h, :], rhs=qdT[:, p, :],
                                             start=True, stop=True), r=[kiTz, qdT], w=[pat])
        T.dve(lambda e: e.tensor_tensor(out=ATm[:], in0=pat[:, 0:512].rearrange("p (h t) -> p h t", h=4),
                                        in1=cmask[:].unsqueeze(1).to_broadcast([128, 4, 128]), op=ALU.mult), r=[pat, cmask], w=[ATm])
        po = pu()
        if inter_fn is None:
            sample_inter_and_state(po)
            for h in range(4):
                c0 = (h // 2) * 512 + (h % 2) * 128
                T.pe(lambda e, h=h, c0=c0: e.matmul(po[:, c0:c0 + 128], lhsT=ATm[:, h, :], rhs=gvb[:, h * 128:(h + 1) * 128],
                                                   start=False, stop=(h % 2 == 1)), r=[ATm, gvb], w=[po])
            return po
        for h in range(4):
            T.pe(lambda e, h=h: e.matmul(po[:, h * 128:(h + 1) * 128], lhsT=ATm[:, h, :], rhs=gvb[:, h * 128:(h + 1) * 128],
                                         start=True, stop=False), r=[ATm, gvb], w=[po])
            inter_fn(po, h)
        return po

    def sample_inter_and_state(po):
        pa = pu(po)
        for p in range(2):
            T.pe(lambda e, p=p: e.matmul(pa[:, p * 16:(p + 1) * 16], lhsT=la[:, p * 128:(p + 1) * 128], rhs=seqsel[:], start=True, stop=True),
                 r=[la, seqsel], w=[pa])
        T.act(lambda e: e.activation(out=ablk_s[:].rearrange("q b p -> q p b"), in_=pa[:, 0:32].rearrange("q (p b) -> q p b", p=2),
                                     func=AF.Exp), r=[pa], w=[ablk_s])
        for p in range(2):
            T.dve(lambda e, p=p: e.tensor_tensor(out=qdTm[p][:], in0=qdT[:, p, :].unsqueeze(1).to_broadcast([128, 16, 128]),
                                                 in1=bmask[:], op=ALU.mult), r=[qdT, bmask], w=[qdTm[p]])
        sg_v = sg.rearrange("b (p hp) k v -> hp k b p v", hp=2)
        gl_v = gl_s.rearrange("b (p hp) k v -> hp k b p v", hp=2)
        for o in range(8):
            sf, sb_ = S0F[o % 2], S0B[o % 2]
            ko = (o % 2) * 2
            for hp in range(2):
                T.dma("sp", sf[hp * 64:(hp + 1) * 64, :, :, :], sg_v[hp][:, o * 2:(o + 1) * 2], w=[sf], key=f"s0_{hp}_{o % 2}")
            for hp in range(2):
                rows = slice(hp * 64, (hp + 1) * 64)
                T.pool(lambda e, rows=rows, hp=hp, sf=sf, sb_=sb_: e.tensor_copy(
                    out=sb_[rows, :, :, hp * 128:(hp + 1) * 128].rearrange("k b p v -> k (b p) v"),
                    in_=sf[rows, :, :, :].rearrange("k b p v -> k (b p) v")), r=[sf], w=[sb_])
            for bb in range(2):
                b = o * 2 + bb
                for p in range(2):
                    T.pe(lambda e, b=b, bb=bb, p=p, sb_=sb_: e.matmul(po[:, p * 512:p * 512 + 256], lhsT=qdTm[p][:, b, :], rhs=sb_[:, bb, p, :],
                                                                     start=(b == 0), stop=False), r=[qdTm[p], sb_], w=[po])
            for bb in range(2):
                b = o * 2 + bb
                T.dve(lambda e, b=b, bb=bb, ko=ko: e.tensor_scalar(out=kdm_v[:, ko + bb, :], in0=kd[:], scalar1=rowmask[:, b:b + 1], scalar2=None,
                                                                   op0=ALU.mult), r=[kd, rowmask], w=[PT])
            T.dve(lambda e, o=o, sf=sf: e.tensor_tensor(out=sf[:].rearrange("k b p v -> k (b p) v"), in0=sf[:].rearrange("k b p v -> k (b p) v"),
                                                        in1=ablk_s[:, o * 2:(o + 1) * 2, :].rearrange("k b p -> k (b p)").unsqueeze(2).to_broadcast([128, 4, 128]),
                                                        op=ALU.mult), r=[sf, ablk_s], w=[sf])
            pss = pu(po)
            for bb in range(2):
                for p in range(2):
                    c0 = (bb * 2 + p) * 256
                    T.pe(lambda e, bb=bb, p=p, c0=c0, pss=pss, ko=ko: e.matmul(pss[:, c0:c0 + 256], lhsT=kdm_v[:, ko + bb, p * 128:(p + 1) * 128],
                                                                              rhs=gvb[:, p * 256:(p + 1) * 256], start=True, stop=True),
                         r=[PT, gvb], w=[pss])
            for hp in range(2):
                rows = slice(hp * 64, (hp + 1) * 64)
                T.dve(lambda e, rows=rows, hp=hp, pss=pss, sf=sf: e.tensor_tensor(
                    out=sf[rows, :, :, :].rearrange("k b p v -> k (b p) v"),
                    in0=sf[rows, :, :, :].rearrange("k b p v -> k (b p) v"),
                    in1=pss[rows, :].rearrange("k (c v) -> k c v", c=4)[:, :, hp * 128:(hp + 1) * 128], op=ALU.add),
                    r=[sf, pss], w=[sf])
            for hp in range(2):
                T.dma("sp", gl_v[hp][:, o * 2:(o + 1) * 2], sf[hp * 64:(hp + 1) * 64, :, :, :], r=[sf], key=f"s0o_{hp}_{o % 2}")

    def gate_out_prep(psE):
        T.act(lambda e: e.activation(out=sil[:], in_=psE[:, 0:512], func=AF.Silu), r=[psE], w=[sil])
        T.pool(lambda e: e.tensor_tensor(out=t1[:], in0=sil[:], in1=ggla[:], op=ALU.mult), r=[sil, ggla], w=[t1])

    def gla_finish(po, banked=False):
        hc = (lambda h: (h // 2) * 512 + (h % 2) * 128) if banked else (lambda h: h * 128)
        for h in range(4):
            T.act(lambda e, h=h: e.activation(out=mix[:, 512 + h * 128:512 + (h + 1) * 128], in_=po[:, hc(h):hc(h) + 128], func=AF.Square,
                                              accum_out=gst[:, h:h + 1]), r=[po], w=[gst, mix])
        rstd_from_ss(gst[:, 0:4], 128, gst[:, 4:8], [gst], [gst])
        for h in range(4):
            T.dve(lambda e, h=h: e.scalar_tensor_tensor(out=mix[:, 512 + h * 128:512 + (h + 1) * 128], in0=po[:, hc(h):hc(h) + 128],
                                                        scalar=gst[:, 4 + h:5 + h], in1=t1[:, h * 128:(h + 1) * 128],
                                                        op0=ALU.mult, op1=ALU.mult), r=[po, gst, t1], w=[mix])

    def attention_prompt(cur, prev, mk):
        ptq = ptb()
        for i in range(4):
            T.pe(lambda e, i=i: e.transpose(out=ptq[:, i * 128:(i + 1) * 128], in_=qkb[:, 2 * i:2 * i + 2, :].rearrange("p h d -> p (h d)"), identity=idb[:]),
                 r=[qkb, idb], w=[ptq])
        T.act(lambda e: e.copy(out=qT[:].rearrange("p a t -> p (a t)"), in_=ptq[:, 0:512]), r=[ptq], w=[qT])
        for g in range(2):
            pss = pu()
            rows = slice(g * 64, (g + 1) * 64)
            for i in range(4):
                for blk, kslot in ((0, prev), (1, cur)):
                    T.pe(lambda e, i=i, rows=rows, pss=pss, blk=blk, kslot=kslot: e.matmul(
                        pss[:, i * 256 + blk * 128:i * 256 + (blk + 1) * 128], lhsT=qT[rows, i, :], rhs=kT[kslot][rows, :],
                        start=True, stop=False), r=[qT, kT[kslot]], w=[pss])
                    T.pe(lambda e, i=i, pss=pss, blk=blk: e.matmul(
                        pss[:, i * 256 + blk * 128:i * 256 + (blk + 1) * 128], lhsT=idb[:], rhs=mk[:, blk * 128:(blk + 1) * 128],
                        start=False, stop=True), r=[idb, mk], w=[pss])
            T.dve(lambda e, g=g, pss=pss: e.tensor_reduce(out=ast[:, g * 4:(g + 1) * 4], in_=pss[:].rearrange("p (i k) -> p i k", i=4),
                                                         axis=AX.X, op=ALU.max), r=[pss], w=[ast])
            T.dve(lambda e, g=g: e.scalar_tensor_tensor(out=ast[:, 8 + g * 4:8 + (g + 1) * 4], in0=ast[:, g * 4:(g + 1) * 4], scalar=0.125,
                                                        in1=sink[:, g * 4:(g + 1) * 4], op0=ALU.mult, op1=ALU.max), r=[ast, sink], w=[ast])
            T.dve(lambda e, g=g: e.tensor_scalar(out=ast[:, 16 + g * 4:16 + (g + 1) * 4], in0=ast[:, 8 + g * 4:8 + (g + 1) * 4],
                                                 scalar1=-1.0, scalar2=None, op0=ALU.mult), r=[ast], w=[ast])
            for i in range(4):
                h = g * 4 + i
                T.act(lambda e, i=i, h=h, pss=pss: e.activation(out=Pm[:, h, :], in_=pss[:, i * 256:(i + 1) * 256], func=AF.Exp, scale=0.125,
                                                                bias=ast[:, 16 + h:17 + h], accum_out=ast[:, 24 + h:25 + h]),
                      r=[pss, ast], w=[Pm, ast])
        T.dve(lambda e: e.tensor_tensor(out=ast[:, 32:40], in0=sink[:], in1=ast[:, 8:16], op=ALU.subtract), r=[sink, ast], w=[ast])
        T.act(lambda e: e.activation(out=ast[:, 32:40], in_=ast[:, 32:40], func=AF.Exp), r=[ast], w=[ast])
        T.dve(lambda e: e.tensor_tensor(out=ast[:, 40:48], in0=ast[:, 32:40], in1=ast[:, 24:32], op=ALU.add), r=[ast], w=[ast])
        T.dve(lambda e: e.reciprocal(out=ast[:, 40:48], in_=ast[:, 40:48]), r=[ast], w=[ast])
        for half in range(2):
            pt = ptb()
            for j in range(8):
                idx = half * 8 + j
                h, blk = idx // 2, idx % 2
                T.pe(lambda e, j=j, h=h, blk=blk, pt=pt: e.transpose(out=pt[:, j * 128:(j + 1) * 128], in_=Pm[:, h, blk * 128:(blk + 1) * 128],
                                                                    identity=idb[:]), r=[Pm, idb], w=[pt])
            if half == 0:
                T.act(lambda e, pt=pt: e.copy(out=PT[:, 0:8, :].rearrange("p a t -> p (a t)"), in_=pt[:]), r=[pt], w=[PT])
            else:
                T.dve(lambda e, pt=pt: e.tensor_copy(out=PT[:, 8:16, :].rearrange("p a t -> p (a t)"), in_=pt[:]), r=[pt], w=[PT])
        po = pu()
        for h in range(8):
            g = h // 4
            T.pe(lambda e, h=h, g=g: e.matmul(po[:, h * 64:(h + 1) * 64], lhsT=PT[:, 2 * h, :], rhs=vb[prev][:, g * 64:(g + 1) * 64],
                                              start=True, stop=False), r=[PT, vb[prev]], w=[po])
            T.pe(lambda e, h=h, g=g: e.matmul(po[:, h * 64:(h + 1) * 64], lhsT=PT[:, 2 * h + 1, :], rhs=vb[cur][:, g * 64:(g + 1) * 64],
                                              start=False, stop=True), r=[PT, vb[cur]], w=[po])
        T.dve(lambda e: e.tensor_tensor(out=mix[:, 0:512].rearrange("p (h d) -> p h d", h=8), in0=po[:, 0:512].rearrange("p (h d) -> p h d", h=8),
                                        in1=ast[:, 40:48].unsqueeze(2).to_broadcast([128, 8, 64]), op=ALU.mult), r=[po, ast], w=[mix])

    def sample_cache_prep():
        for half in range(2):
            pt = ptb()
            for j in range(8):
                b = half * 8 + j
                T.pe(lambda e, j=j, b=b, pt=pt: e.transpose(out=pt[:, j * 128:(j + 1) * 128], in_=ckb_v[:, b, :], identity=idb[:]),
                     r=[Pm, idb], w=[pt])
            for g in range(2):
                rows = slice(g * 64, (g + 1) * 64)
                T.act(lambda e, g=g, rows=rows, half=half, pt=pt: e.copy(out=kTcz[rows, g, half * 8:(half + 1) * 8, :],
                                                                         in_=pt[rows, :].rearrange("p (b j) -> p b j", b=8)), r=[pt], w=[kTcz])

    vwbuf = Buf("vw_dram", None)

    def attention_sample(cur):
        ptq = ptb()
        for i in range(4):
            T.pe(lambda e, i=i: e.transpose(out=ptq[:, i * 128:(i + 1) * 128], in_=qkb[:, 2 * i:2 * i + 2, :].rearrange("p h d -> p (h d)"), identity=idb[:]),
                 r=[qkb, idb], w=[ptq])
        for i in range(4):
            T.act(lambda e, i=i: e.copy(out=qTs[:, :, i, :], in_=ptq[:, i * 128:(i + 1) * 128].rearrange("p (b t) -> p b t", b=16)),
                  r=[ptq], w=[qTs])
        for g in range(2):
            rows = slice(g * 64, (g + 1) * 64)
            T.act(lambda e, g=g, rows=rows: e.copy(out=kTnz[rows, g, :], in_=kT[cur][rows, :]), r=[kT[cur]], w=[kTnz])
        for b in range(16):
            T.dma("sp", kw_s[b, 120:128, :], qkr[b * 8:(b + 1) * 8, 8:10, :].rearrange("p h d -> p (h d)"), r=[qkr], key="o_kwsn")
            T.dma("sp", vw_s[b, 120:128, :], vf[b * 8:(b + 1) * 8, :], r=[vf], w=[vwbuf], key="G_vwsn")
        T.dma("pool", vn[:], vw_s[:, 120:128, :].rearrange("b t c -> t b c"), r=[vwbuf], w=[vn], key="vnload")
        for g in range(2):
            for hb in range(2):
                Sc = pu()
                Sx = pu()
                for bl in range(8):
                    b = hb * 8 + bl
                    T.pe(lambda e, b=b, bl=bl, g=g, Sc=Sc: e.matmul(Sc[0:32, bl * 128:(bl + 1) * 128], lhsT=qTs[:, b, :, :].rearrange("p i t -> p (i t)"),
                                                                   rhs=kTcz[:, g, b, :], start=True, stop=False), r=[qTs, kTcz], w=[Sc])
                    T.pe(lambda e, bl=bl, Sc=Sc: e.matmul(Sc[0:32, bl * 128:(bl + 1) * 128], lhsT=idb[:, 0:32], rhs=masks[:, 0:128],
                                                         start=False, stop=True), r=[idb, masks], w=[Sc])
                for bl in range(8):
                    b = hb * 8 + bl
                    T.pe(lambda e, b=b, bl=bl, g=g, Sx=Sx: e.matmul(Sx[0:32, bl * 8:(bl + 1) * 8], lhsT=qTs[:, b, :, :].rearrange("p i t -> p (i t)"),
                                                                   rhs=kTnz[:, g, b * 8:(b + 1) * 8], start=True, stop=False), r=[qTs, kTnz], w=[Sx])
                    T.pe(lambda e, bl=bl, Sx=Sx: e.matmul(Sx[0:32, bl * 8:(bl + 1) * 8], lhsT=idb[:, 0:32], rhs=masks[:, 128:136],
                                                         start=False, stop=True), r=[idb, masks], w=[Sx])
                T.dve(lambda e, Sc=Sc: e.tensor_reduce(out=sst[:, 0:8], in_=Sc[0:32, :].rearrange("p (b k) -> p b k", b=8), axis=AX.X, op=ALU.max),
                      r=[Sc], w=[sst])
                T.dve(lambda e, Sx=Sx: e.tensor_reduce(out=sst[:, 8:16], in_=Sx[0:32, 0:64].rearrange("p (b k) -> p b k", b=8), axis=AX.X, op=ALU.max),
                      r=[Sx], w=[sst])
                T.dve(lambda e: e.tensor_tensor(out=sst[:, 0:8], in0=sst[:, 0:8], in1=sst[:, 8:16], op=ALU.max), r=[sst], w=[sst])
                T.dve(lambda e, g=g: e.scalar_tensor_tensor(out=sst[:, 16:24], in0=sst[:, 0:8], scalar=0.125, in1=sink_s[:, g:g + 1].to_broadcast([32, 8]),
                                                            op0=ALU.mult, op1=ALU.max), r=[sst, sink_s], w=[sst])
                T.dve(lambda e: e.tensor_scalar(out=sst[:, 24:32], in0=sst[:, 16:24], scalar1=-1.0, scalar2=None, op0=ALU.mult), r=[sst], w=[sst])
                for bl in range(8):
                    T.act(lambda e, bl=bl, Sc=Sc: e.activation(out=Pm[0:32, bl // 2, (bl % 2) * 128:(bl % 2 + 1) * 128], in_=Sc[0:32, bl * 128:(bl + 1) * 128],
                                                               func=AF.Exp, scale=0.125, bias=sst[:, 24 + bl:25 + bl], accum_out=sst[:, 32 + bl:33 + bl]),
                          r=[Sc, sst], w=[Pm, sst])
                for bl in range(8):
                    T.act(lambda e, bl=bl, Sx=Sx: e.activation(out=Pn[:, bl, :], in_=Sx[0:32, bl * 8:(bl + 1) * 8],
                                                               func=AF.Exp, scale=0.125, bias=sst[:, 24 + bl:25 + bl], accum_out=sst[:, 40 + bl:41 + bl]),
                          r=[Sx, sst], w=[Pn, sst])
                T.dve(lambda e, g=g: e.tensor_tensor(out=sst[:, 48:56], in0=sink_s[:, g:g + 1].to_broadcast([32, 8]), in1=sst[:, 16:24], op=ALU.subtract),
                      r=[sst, sink_s], w=[sst])
                T.act(lambda e: e.activation(out=sst[:, 48:56], in_=sst[:, 48:56], func=AF.Exp), r=[sst], w=[sst])
                T.dve(lambda e: e.tensor_tensor(out=sst[:, 56:64], in0=sst[:, 32:40], in1=sst[:, 40:48], op=ALU.add), r=[sst], w=[sst])
                T.dve(lambda e: e.tensor_tensor(out=sst[:, 56:64], in0=sst[:, 56:64], in1=sst[:, 48:56], op=ALU.add), r=[sst], w=[sst])
                T.dve(lambda e: e.reciprocal(out=sst[:, 56:64], in_=sst[:, 56:64]), r=[sst], w=[sst])
                Pc = Pm[0:32, 0:4, :].rearrange("p a (c k) -> p (a c) k", c=2)
                T.dve(lambda e, Pc=Pc: e.tensor_tensor(out=Pc, in0=Pc, in1=sst[:, 56:64].unsqueeze(2).to_broadcast([32, 8, 128]), op=ALU.mult),
                      r=[Pm, sst], w=[Pm])
                T.dve(lambda e: e.tensor_tensor(out=Pn[:], in0=Pn[:], in1=sst[:, 56:64].unsqueeze(2).to_broadcast([32, 8, 8]), op=ALU.mult),
                      r=[Pn, sst], w=[Pn])
                ptc = ptb()
                for bl in range(8):
                    T.pe(lambda e, bl=bl, ptc=ptc: e.transpose(out=ptc[:, bl * 32:(bl + 1) * 32], in_=Pm[0:32, bl // 2, (bl % 2) * 128:(bl % 2 + 1) * 128],
                                                              identity=idb[0:32, 0:32]), r=[Pm, idb], w=[ptc])
                for bl in range(8):
                    T.pe(lambda e, bl=bl, ptc=ptc: e.transpose(out=ptc[0:8, 256 + bl * 32:256 + (bl + 1) * 32], in_=Pn[:, bl, :],
                                                              identity=idb[0:32, 0:32]), r=[Pn, idb], w=[ptc])
                T.act(lambda e, ptc=ptc: e.copy(out=PT[:, 0:2, :].rearrange("p a (c m) -> p (a c) m", c=4), in_=ptc[:, 0:256].rearrange("p (b m) -> p b m", b=8)),
                      r=[ptc], w=[PT])
                T.act(lambda e, ptc=ptc: e.copy(out=PTn[:], in_=ptc[0:8, 256:512].rearrange("p (b m) -> p b m", b=8)), r=[ptc], w=[PTn])
                po = pu()
                for bl in range(8):
                    b = hb * 8 + bl
                    T.pe(lambda e, b=b, bl=bl, g=g, po=po: e.matmul(po[0:32, bl * 64:(bl + 1) * 64], lhsT=PT[:, bl // 4, (bl % 4) * 32:(bl % 4 + 1) * 32],
                                                                   rhs=cvb[:, b, g * 64:(g + 1) * 64], start=True, stop=False), r=[PT, cvb], w=[po])
                    T.pe(lambda e, b=b, bl=bl, g=g, po=po: e.matmul(po[0:32, bl * 64:(bl + 1) * 64], lhsT=PTn[:, bl, :],
                                                                   rhs=vn[:, b, g * 64:(g + 1) * 64], start=False, stop=True), r=[PTn, vn], w=[po])
                T.act(lambda e, g=g, hb=hb, po=po: e.copy(out=attn_s[:, g, hb * 8:(hb + 1) * 8, :], in_=po[0:32, 0:512].rearrange("p (b d) -> p b d", b=8)),
                      r=[po], w=[attn_s])
        scrbuf = Buf("scr_dram", None)
        for i in range(4):
            T.dma("sp", scr[:, :, :, i, :].rearrange("b t g d -> t g b d"), attn_s[i * 8:(i + 1) * 8, :, :, :], r=[attn_s], w=[scrbuf], key="G_scrw")
        T.dma("sp", mix[:, 0:512], scr.rearrange("b t g i d -> (b t) (g i d)"), r=[scrbuf], w=[mix], key="scrr")

    def post_norm_residual(ps, xt, grep):
        sx = stb.cur if isinstance(stb, Rot) else stb
        T.act(lambda e: e.activation(out=tmp[:], in_=ps[:], func=AF.Square, accum_out=sx[:, 2:3]), r=[ps], w=[sx, tmp])
        rstd_from_ss(sx[:, 2:3], D, sx[:, 3:4], [sx], [sx])
        T.dve(lambda e: e.scalar_tensor_tensor(out=tmp[:], in0=ps[:], scalar=sx[:, 3:4], in1=grep[:], op0=ALU.mult, op1=ALU.mult),
              r=[ps, sx, grep], w=[tmp])
        T.dve(lambda e: e.tensor_tensor(out=xt[:], in0=tmp[:], in1=xt[:], op=ALU.add), r=[tmp, xt], w=[xt])

    def inter_prompt(po, h):
        p = h // 2
        T.pe(lambda e: e.matmul(po[:, h * 128:(h + 1) * 128], lhsT=qdTz[:, h, :], rhs=Sb[:, p, :], start=False, stop=True),
             r=[qdTz, Sb], w=[po])

    ybuf = [Buf(f"ydram{i}", None) for i in range(NT + 1)]
    T.persist.extend(ybuf)

    def main_front(i, is_sample):
        slot = i % 3
        xsrc = xs if is_sample else xp[i * 128:(i + 1) * 128, :]
        cssrc = cs_s if is_sample else cs_p[i * 128:(i + 1) * 128, :]
        cst = csb[i % 2]
        T.dma("sp", cst[:], cssrc, w=[cst], key=f"cs{i % 2}")
        front(xsrc, slot, gpre, hT, hT[:])
        cur = i % 3
        psA = pu()
        proj(psA[:, 0:512], CQ, 512, psA)
        T.act(lambda e: e.copy(out=qk[:, 0:8, :], in_=psA[:, 0:512].rearrange("p (h d) -> p h d", h=8)), r=[psA], w=[qk])
        psB = pu()
        proj(psB[:, 0:512], CK, 512, psB)
        rope_kv(psB, cst, cur)
        T.dve(lambda e: e.tensor_copy(out=gq_sb[:], in_=psB[:, 256:512]), r=[psB], w=[gq_sb])
        rope(cst, 0, 10)
        k_transpose(cur)
        psC = pu()
        proj(psC[:, 0:272], CGK, 272, psC)
        proj(psC[:, 512:1024], CGV, 512, psC)
        T.act(lambda e: e.copy(out=gvb[:], in_=psC[:, 512:1024]), r=[psC], w=[gvb])
        evac_gate(psC)
        psE = pu()
        proj(psE[:, 0:512], CGR, 512, psE)
        gate_out_prep(psE)
        if is_sample:
            gate_common(ltri_s, utri_s, full=True)
        else:
            gate_common(ltri, utri, full=True)
            ablk_prompt()

    def main_back(i, is_sample):
        slot = i % 3
        xt = xbuf[slot]
        cur, prev = i % 3, (i - 1) % 3
        if not is_sample:
            attention_prompt(cur, prev, mask0b if i == 0 else maskb)
            po = gla_intra_and_out(cm, inter_prompt)
            gla_finish(po)
            state_update_prompt()
        else:
            attention_sample(cur)
            po = gla_intra_and_out(cm_s, None)
            gla_finish(po, banked=True)
        if (not is_sample) and i == NT - 1:
            T.dma("sp", kw_p, qkr[:, 8:10, :].rearrange("p h d -> p (h d)"), r=[qkr], key="o_kw")
            T.dma("sp", vw_p, vf[:], r=[vf], key="o_vw")
            T.dma("sp", gl_p, S[:], r=[S], key="o_gl")
        if T.seg is not None and T.seg[0] == "B":
            cnt["force_t"] = 0
        pt = ptb()
        for k in range(8):
            T.pe(lambda e, k=k: e.transpose(out=pt[:, k * 128:(k + 1) * 128], in_=mix[:, k * 128:(k + 1) * 128], identity=idb[:]),
                 r=[mix, idb], w=[pt])
        T.act(lambda e: e.copy(out=mixT[:].rearrange("p a t -> p (a t)"), in_=pt[:]), r=[pt], w=[mixT])
        pm = pu()
        for n in range(2):
            for k in range(8):
                T.pe(lambda e, n=n, k=k: e.matmul(pm[:, n * 512:(n + 1) * 512], lhsT=mixT[:, k, :], rhs=Wout[:, k, n * 512:(n + 1) * 512],
                                                  start=(k == 0), stop=(k == 7)), r=[mixT, Wout], w=[pm])
        post_norm_residual(pm, xt, gpost)
        ydst = y_s if is_sample else y_p[i * 128:(i + 1) * 128, :]
        T.dma("sp", ydst, xt[:], r=[xt], w=[ybuf[i]], key=f"x1o{slot}")
        norm_T(xt, gfpre, h2T, h2T[:, :, i * 128:(i + 1) * 128], tail=True)
        cnt["force_t"] = None

    ROT["p"] = 0
    if not _CACHE.get('no_sample'):
        T.seg = ("S", 0)
        sample_cache_prep()
        main_front(NT, True)
        main_back(NT, True)
        T.seg = None
        bar0 = T.seal()
        T.emit()
        T.pending_bar = {e: bar0 for e in T.engs}
    esS.close()
    esA2 = contextlib.ExitStack()
    A2 = lambda name, shape, dt: _sb(name + "_b", shape, dt, scope=esA2)
    def dup(bf):
        return Rot([bf, A2(bf.name, list(bf.t.shape), bf.t.dtype)])
    hb = dup(hb); hT = dup(hT); st = dup(st); stb = dup(stb); qk = dup(qk); qkr = dup(qkr); qkb = dup(qkb); rt = [dup(x) for x in rt]
    vf = dup(vf); qT = dup(qT); ast = dup(ast); mix = dup(mix); mixT = dup(mixT)
    glr_sb = dup(glr_sb); gk_sb = dup(gk_sb); gq_sb = dup(gq_sb); glrT = dup(glrT)
    az = dup(az); ez = dup(ez); la = dup(la); eB = dup(eB); eNB = dup(eNB); eR = dup(eR); ablk = dup(ablk)
    qd = dup(qd); ki = dup(ki); kd = dup(kd); qdT = dup(qdT); qdTz = dup(qdTz); kiTz = dup(kiTz); ATm = dup(ATm)
    gvb = dup(gvb); sil = dup(sil); t1 = dup(t1); gst = dup(gst)
    ROT["p"] = 1
    T.dve(lambda e: e.memset(glrT[:], 1.0), w=[glrT])
    T.dve(lambda e: e.memset(qdTz[:], 0.0), w=[qdTz])
    T.dve(lambda e: e.memset(kiTz[:], 0.0), w=[kiTz])
    def stage(kind, n, par, fn, *args):
        T.seg = (kind, n)
        ROT["p"] = par
        fn(*args)
        T.seg = None

    stage("F", 0, 0, hist_front, 0)
    for i in range(NH):
        if i + 1 < NH:
            stage("F", i + 1, i + 1, hist_front, i + 1)
        else:
            stage("F", NH, NH, main_front, 0, False)
        stage("B", i, i, hist_back, i)
    for i in range(NT):
        if i + 1 < NT:
            stage("F", NH + i + 1, NH + i + 1, main_front, i + 1, False)
        stage("B", NH + i, NH + i, main_back, i, False)
    ROT["p"] = 0
    bar = T.seal()
    T.emit()
    if _CACHE.get('stop_after_A'):
        T.emit(final=True)
        return nc, es
    esA2.close()
    esA.close()
    st = st.bufs[0]
    stb = stb.bufs[0]
    ROT["p"] = 0

    class _F32View:
        def __init__(self, t):
            self.t = t
        def __getitem__(self, k):
            return self.t[:].bitcast(F32)[k]
    cnt["extra_small"] = []
    for pb_ in PTb:
        vb_ = Buf(pb_.name + "_f32", _F32View(pb_.t))
        vb_.psum = True
        T.persist.append(vb_)
        cnt["extra_small"].append(vb_)

    Wup = [T.sb(f"Wup{j}", [128, 8, 512], BF16) for j in range(8)]
    Wdn = [T.sb(f"Wdn{j}", [128, 4, D], BF16) for j in range(8)]
    u2T = T.sb("u2T", [128, 32, 256], BF16)
    ur = [T.sb(f"ur{i}", [128, 512], F32) for i in range(2)]
    gfpost = T.sb("gfpost", [128, D], F32)
    T.dma("sp", gfpost[:], c_gfpost, w=[gfpost], key="G_gfp", extra=bar)
    for j in range(8):
        T.dma("pool", Wup[j][:], w_up[:, j * 512:(j + 1) * 512].rearrange("(k p) f -> p k f", p=128), w=[Wup[j]], key=f"wup{j}",
              extra=(bar if j == 0 else ()))
    for j in range(8):
        T.dma("pool", Wdn[j][:], w_down[j * 512:(j + 1) * 512, :].rearrange("(c p) n -> p c n", p=128), w=[Wdn[j]], key=f"wdn{j}")
    ntiles = NT + 1
    t0 = 0
    nev = 0
    while t0 < ntiles:
        nt = min(2, ntiles - t0)
        ntok = nt * 128
        for c2 in range(16):
            pu_ = pu(small=True)
            for cc in range(2):
                c = c2 * 2 + cc
                for k in range(8):
                    T.pe(lambda e, c=c, cc=cc, k=k, pu_=pu_, t0=t0, ntok=ntok: e.matmul(pu_[:, cc * 256:cc * 256 + ntok], lhsT=Wup[c // 4][:, k, (c % 4) * 128:(c % 4 + 1) * 128],
                                                                     rhs=h2T[:, k, t0 * 128:t0 * 128 + ntok], start=(k == 0), stop=(k == 7)),
                         r=[Wup[c // 4], h2T], w=[pu_])
            urb = ur[nev % 2]
            nev += 1
            src = pu_[:, 0:512].rearrange("p (c t) -> p c t", c=2)[:, :, 0:ntok]
            dstv = urb[:].rearrange("p (c t) -> p c t", c=2)[:, :, 0:ntok]
            T.act(lambda e, src=src, dstv=dstv: e.activation(out=dstv, in_=src, func=AF.Relu), r=[pu_], w=[urb])
            T.dve(lambda e, dstv=dstv, c2=c2, ntok=ntok: e.tensor_tensor(out=u2T[:, c2 * 2:c2 * 2 + 2, 0:ntok], in0=dstv, in1=dstv, op=ALU.mult),
                   r=[urb], w=[u2T])
        for tl in range(nt):
            ti = t0 + tl
            slot = ti % 2
            xt = xbuf[slot]
            ysrc = y_s if ti == NT else y_p[ti * 128:(ti + 1) * 128, :]
            T.dma("sp", xt[:], ysrc, r=[ybuf[ti]], w=[xt], key=f"x{slot}")
            pf = pu()
            for n in range(2):
                for c in range(32):
                    T.pe(lambda e, n=n, c=c, tl=tl, pf=pf: e.matmul(pf[:, n * 512:(n + 1) * 512], lhsT=u2T[:, c, tl * 128:(tl + 1) * 128],
                                                                   rhs=Wdn[c // 4][:, c % 4, n * 512:(n + 1) * 512], start=(c == 0), stop=(c == 31)),
                         r=[u2T, Wdn[c // 4]], w=[pf])
            post_norm_residual(pf, xt, gfpost)
            T.dma("sp", ysrc, xt[:], r=[xt], w=[ybuf[ti]], key=f"yo{slot}")
        t0 += nt

    T.emit(final=True)
    return nc, es


_CACHE = {}


def _consts():
    idf = np.eye(128, dtype=np.float32)
    s = np.arange(128)
    ltri = (s[:, None] <= s[None, :]).astype(np.float32) / 16.0
    utri = (s[:, None] > s[None, :]).astype(np.float32) / 16.0
    same = (s[:, None] // 8) == (s[None, :] // 8)
    ltri_s = ltri * same
    utri_s = utri * same
    cm = (s[:, None] <= s[None, :]).astype(np.float32)
    cm_s = cm * same
    q = np.arange(128)[:, None]
    kk = np.arange(256)[None, :]
    rel = (q + 128) - kk
    mask = np.where((rel >= 0) & (rel < 128), 0.0, NEG).astype(np.float32)
    mask0 = mask.copy()
    mask0[:, :128] = NEG
    r = np.arange(128)
    t_of = (r % 8)[:, None]
    masks = np.zeros((128, 136), np.float32)
    jj = np.arange(128)[None, :]
    masks[:32, :128] = np.where(jj >= t_of[:32] + 1, 0.0, NEG)
    masks[:32, 128:] = np.where(np.arange(8)[None, :] <= t_of[:32], 0.0, NEG)
    rowmask = ((r[:, None] // 8) == np.arange(16)[None, :]).astype(np.float32)
    bm = ((np.arange(128)[None, :] // 8) == np.arange(16)[:, None]).astype(np.float32).reshape(1, 2048)
    extra = dict(c_masks=masks, c_rowmask=rowmask, c_seqsel=rowmask / 16.0, c_bmask=np.ascontiguousarray(np.broadcast_to(bm, (128, 2048))))
    return dict(**extra, c_idf=idf, c_ltri=ltri, c_utri=utri, c_ltri_s=ltri_s.astype(np.float32), c_utri_s=utri_s.astype(np.float32),
                c_cm=cm, c_cm_s=cm_s.astype(np.float32), c_mask=mask, c_mask0=mask0)


def _rope_table(pos):
    half = 32
    inv = (10000.0 ** (-np.arange(half, dtype=np.float32) / half)).astype(np.float32)
    ang = pos.astype(np.float32)[:, None] * inv[None, :]
    return np.concatenate([np.cos(ang), np.sin(ang)], axis=1).astype(np.float32)


def kernel(x_prompt, x_sample, cache_k, cache_v, state_gla, w_in, w_gk2, b_gk, g_gla, sinks,
           w_out, g_mix_pre, g_mix_post, g_ffn_pre, g_ffn_post, w_up, w_down):
    f = lambda a: np.ascontiguousarray(np.asarray(a, dtype=np.float32))
    x_prompt, x_sample, cache_k, cache_v, state_gla = map(f, (x_prompt, x_sample, cache_k, cache_v, state_gla))
    if "nc" not in _CACHE:
        _CACHE["nc"] = build_program()
    nc, _es = _CACHE["nc"]
    consts = _consts()
    wgk = np.zeros((32, 256), np.float32)
    wgk[0:16] = f(w_gk2)[0]
    wgk[16] = f(b_gk)[0]
    rep = lambda v, n=128: np.ascontiguousarray(np.broadcast_to(f(v).reshape(1, -1), (n, f(v).size)))
    shared = dict(
        w_in=f(w_in)[0], w_out=f(w_out)[0], w_up=f(w_up)[0], w_down=f(w_down)[0], wgk=wgk,
        c_gpre=np.ascontiguousarray(f(g_mix_pre)[0].reshape(8, 128).T), c_gfpre=np.ascontiguousarray(f(g_ffn_pre)[0].reshape(8, 128).T),
        c_gpost=rep(g_mix_post[0]), c_gfpost=rep(g_ffn_post[0]), c_ggla=rep(np.tile(f(g_gla)[0], 4)), c_sink=rep(sinks[0]),
        cs_s=_rope_table(16384 + (np.arange(128) % 8)),
        c_sinks=np.ascontiguousarray(f(sinks)[0].reshape(2, 4).T[:, None, :].repeat(8, axis=1).reshape(32, 2)), **consts)
    in_maps = []
    for c in range(NCORES):
        b, j = c // 4, c % 4
        start = j * NT * 128
        xh = np.zeros((NH * 128, D), np.float32)
        if start > 0:
            hist = x_prompt[b, max(0, start - NH * 128):start]
            xh[NH * 128 - hist.shape[0]:] = hist
        m = dict(shared)
        m.update(
            xp=x_prompt[b, start:start + NT * 128], xh=xh, xs=x_sample[16 * c:16 * (c + 1)].reshape(128, D),
            ck=cache_k[0, 16 * c:16 * (c + 1)].reshape(16, 128, 128), cv=cache_v[0, 16 * c:16 * (c + 1)].reshape(16, 128, 128),
            sg=state_gla[0, 16 * c:16 * (c + 1)],
            cs_p=_rope_table(start + np.arange(NT * 128)), cs_h=_rope_table(np.maximum(start - 128 + np.arange(128), 0)),
        )
        if j != 0:
            m["c_mask0"] = consts["c_mask"]
        in_maps.append({k: np.ascontiguousarray(v, dtype=np.float32) for k, v in m.items()})
    if _CACHE.get("prep_only"):
        return nc, in_maps
    res = run_bass_kernel_spmd(nc, in_maps, core_ids=list(range(NCORES)))
    R = res.results
    if NT != 16:
        return R
    y_prompt = np.stack([np.concatenate([R[b * 4 + j]["y_p"] for j in range(4)], axis=0) for b in range(2)])
    y_sample = np.concatenate([R[c]["y_s"] for c in range(NCORES)], axis=0).reshape(128, 8, D)
    kwp = np.stack([R[b * 4 + 3]["kw_p"].reshape(128, 2, 64) for b in range(2)])[None]
    vwp = np.stack([R[b * 4 + 3]["vw_p"].reshape(128, 2, 64) for b in range(2)])[None]
    def gl(a):
        a = a.reshape(2, 64, 2, 128)
        return np.ascontiguousarray(a.transpose(2, 0, 1, 3)).reshape(4, 64, 128)
    glp = np.stack([gl(R[b * 4 + 3]["gl_p"]) for b in range(2)])[None]
    kws = np.concatenate([R[c]["kw_s"].reshape(16, 128, 2, 64) for c in range(NCORES)], axis=0)[None]
    vws = np.concatenate([R[c]["vw_s"].reshape(16, 128, 2, 64) for c in range(NCORES)], axis=0)[None]
    gls = np.concatenate([R[c]["gl_s"] for c in range(NCORES)], axis=0)[None]
    return (y_prompt.astype(np.float32), y_sample.astype(np.float32), kwp.astype(np.float32), vwp.astype(np.float32),
            glp.astype(np.float32), kws.astype(np.float32), vws.astype(np.float32), gls.astype(np.float32))
```

```python
import contextlib
import sys as _sys
import numpy as np
import concourse.bass as bass
import concourse.mybir as mybir
from concourse.bass_utils import run_bass_kernel_spmd

F32 = mybir.dt.float32
BF16 = mybir.dt.bfloat16
AF = mybir.ActivationFunctionType
ALU = mybir.AluOpType
AX = mybir.AxisListType

D = 1024
NCORES = 8
NT = 16
NH = 48
INW = 2320
DFF = 4096
EPS = 1e-6
NEG = -30000.0
CQ, CK, CV, CGQ, CGK, CGLR, CGV, CGR = 0, 512, 640, 768, 1024, 1280, 1296, 1808


class Buf:
    def __init__(self, name, t):
        self.name, self.t = name, t
        self.w = None
        self.r = []
        self.psum = False

    def __getitem__(self, k):
        return self.t[k]


ROT = {"p": 0}


class Rot:
    def __init__(self, bufs):
        self.bufs = list(bufs)

    @property
    def cur(self):
        return self.bufs[ROT["p"] % len(self.bufs)]

    def __getitem__(self, k):
        return self.cur.t[k]

    t = property(lambda s: s.cur.t)
    psum = property(lambda s: s.cur.psum)
    name = property(lambda s: s.cur.name)
    w = property(lambda s: s.cur.w, lambda s, v: setattr(s.cur, "w", v))
    r = property(lambda s: s.cur.r, lambda s, v: setattr(s.cur, "r", v))


class Inst:
    __slots__ = ("eng", "fn", "deps", "ticket", "key", "needs", "rot", "seg", "odeps", "src", "keep", "edeps")

    def __init__(self, eng, fn, key=None):
        self.eng, self.fn, self.key = eng, fn, key
        self.rot = ROT["p"]
        self.keep = False
        self.edeps = None
        self.seg = None
        self.odeps = []
        self.deps = []
        self.ticket = None
        self.needs = False


class Tracker:
    def __init__(self, nc, es):
        self.nc, self.es = nc, es
        self.engs = {"pe": nc.tensor, "act": nc.scalar, "dve": nc.vector, "pool": nc.gpsimd, "sp": nc.sync}
        self.order = []
        self.pos = 0
        self.esem = None
        self.ksem, self.kcnt, self.ecnt = {}, {}, {e: 0 for e in self.engs}
        self.waited = {e: {} for e in self.engs}
        self.persist = []
        self.allbufs = []
        self.pending_bar = {}
        self.seg = None
        self.groups = {}

    def sb(self, name, shape, dt, scope=None):
        b = Buf(name, (scope or self.es).enter_context(self.nc.sbuf_tensor(name, list(shape), dt)))
        self.allbufs.append(b)
        if scope is None:
            self.persist.append(b)
        return b

    def ps(self, name, shape, dt):
        b = Buf(name, self.es.enter_context(self.nc.psum_tensor(name, list(shape), dt)))
        b.psum = True
        self.persist.append(b)
        return b

    def op(self, eng, fn, r=(), w=(), key=None, extra=()):
        ins = Inst(eng, fn, key)
        f = _sys._getframe(1)
        while f.f_code.co_name in ("pe", "act", "dve", "pool", "dma", "op"):
            f = f.f_back
        ins.src = "%s:%d" % (f.f_code.co_name, f.f_lineno)
        if _CACHE.get("trace_lines"):
            f = _sys._getframe(1)
            while f.f_code.co_name in ("pe", "act", "dve", "pool", "dma", "op"):
                f = f.f_back
            _CACHE.setdefault("lines", []).append((len(self.order), eng, f.f_code.co_name, f.f_lineno))
        deps = list(extra) + list(self.pending_bar.pop(eng, ()))
        for b in r:
            if b.w is not None:
                deps.append(b.w)
            if b.psum:
                deps.extend(x for x in b.r if x.eng != eng)
        for b in w:
            if b.w is not None:
                deps.append(b.w)
            deps.extend(b.r)
        seen = set()
        ins.seg = self.seg
        for d in deps:
            if id(d) in seen or d is ins:
                continue
            seen.add(id(d))
            ins.odeps.append(d)
            if d.key is not None and d.key.startswith("G_") and d.key != key:
                ins.odeps.extend(self.groups.get(d.key, ()))
            if d.key is None and d.eng == "pe" and eng == "pe" and key is None:
                continue
            if key is not None and d.key == key and key.startswith("G_"):
                continue
            ins.deps.append(d)
            d.needs = True
        for b in r:
            b.r.append(ins)
        for b in w:
            b.w = ins
            b.r = []
        if key is not None and key.startswith("G_"):
            self.groups.setdefault(key, []).append(ins)
        self.order.append(ins)
        return ins

    def pe(self, fn, r=(), w=()):
        return self.op("pe", fn, r, w)

    def act(self, fn, r=(), w=()):
        return self.op("act", fn, r, w)

    def dve(self, fn, r=(), w=()):
        return self.op("dve", fn, r, w)

    def pool(self, fn, r=(), w=()):
        return self.op("pool", fn, r, w)

    def dma(self, q, out, in_, r=(), w=(), key=None, extra=()):
        return self.op(q, lambda e: e.dma_start(out=out, in_=in_), r, w, key=key, extra=extra)

    def seal(self):
        bar = []
        last = {}
        for ins in self.order:
            last[ins.eng] = ins
            if ins.key is not None:
                bar.append(ins)
        for ins in last.values():
            if ins.key is None:
                ins.needs = True
                ins.keep = True
                bar.append(ins)
        for b in self.persist + self.allbufs:
            for ins in ([b.w] if b.w is not None else []) + list(b.r):
                ins.needs = True
                ins.keep = True
        return bar

    @staticmethod
    def _zipmerge(F, B):
        inF = {id(x): k for k, x in enumerate(F)}
        out, pf, pb = [], 0, 0
        while pf < len(F) or pb < len(B):
            take_b = pb < len(B) and (pf >= len(F) or pb * len(F) <= pf * len(B))
            if take_b:
                ins = B[pb]
                need = max((inF[id(d)] for d in ins.odeps if id(d) in inF), default=-1)
                while pf <= need:
                    out.append(F[pf]); pf += 1
                out.append(ins); pb += 1
            else:
                out.append(F[pf]); pf += 1
        return out

    COST = {"pe": 0.20, "act": 0.40, "dve": 0.40, "pool": 0.80, "sp": 0.08}
    FCOST = {
        ("pe", "proj"): 0.35, ("pe", "norm_T"): 0.14, ("pe", "main_back"): 0.30, ("pe", "attention_prompt"): 0.15,
        ("pe", "gla_intra_and_out"): 0.15, ("pe", "inter_prompt"): 0.15, ("pe", "gate_common"): 0.50, ("pe", "ablk_prompt"): 0.20,
        ("pe", "state_update_prompt"): 0.45, ("pe", "k_transpose"): 0.14,
        ("act", "norm_T"): 0.75, ("act", "rstd_from_ss"): 0.25, ("act", "main_front"): 0.40, ("act", "rope_kv"): 0.30,
        ("act", "evac_gate"): 0.30, ("act", "gate_out_prep"): 0.60, ("act", "attention_prompt"): 0.45, ("act", "gate_common"): 0.35,
        ("act", "gla_intra_and_out"): 0.35, ("act", "gla_finish"): 0.30, ("act", "post_norm_residual"): 0.75, ("act", "main_back"): 0.60,
        ("act", "hist_front"): 0.40,
        ("dve", "norm_T"): 0.70, ("dve", "rope"): 0.35, ("dve", "attention_prompt"): 0.45, ("dve", "gate_common"): 0.35,
        ("dve", "gla_intra_and_out"): 0.50, ("dve", "gla_finish"): 0.35, ("dve", "state_update_prompt"): 0.33,
        ("dve", "post_norm_residual"): 1.10, ("dve", "evac_gate"): 0.40, ("dve", "main_front"): 0.40,
        ("pool", "rope"): 0.60, ("dve", "post_norm_residual"): 1.10, ("pool", "gate_out_prep"): 1.10, ("pool", "state_update_prompt"): 0.80,
        ("pool", "rope_kv"): 0.40,
    }

    def _schedule(self, todo):
        import heapq
        idx = {id(x): k for k, x in enumerate(todo)}
        n = len(todo)
        ndep = [0] * n
        users = [[] for _ in range(n)]
        for k, ins in enumerate(todo):
            ds = {idx[id(d)] for d in ins.odeps if id(d) in idx}
            ndep[k] = len(ds)
            for j in ds:
                users[j].append(k)
        fin = [0.0] * n
        ready_t = [0.0] * n
        crit = [None] * n
        elast = {e: None for e in self.engs}
        efree = {e: 0.0 for e in self.engs}
        ready = {e: [] for e in self.engs}
        for k in range(n):
            if ndep[k] == 0:
                heapq.heappush(ready[todo[k].eng], k)
        out = []
        LOOK = 24
        while len(out) < n:
            best = None
            for e, hp in ready.items():
                if not hp:
                    continue
                cands = heapq.nsmallest(LOOK, hp)
                for k in cands:
                    st = max(efree[e], ready_t[k])
                    key = (st, k)
                    if best is None or key < best[0]:
                        best = (key, e, k)
            (st, _), e, k = best
            ready[e].remove(k)
            heapq.heapify(ready[e])
            ins = todo[k]
            is_dma = ins.key is not None
            dur = (0.08 if is_dma else self.FCOST.get((e, ins.src.split(":")[0]), self.COST[e]))
            if efree[e] > ready_t[k] and elast[e] is not None:
                crit[k] = ("eng", elast[e])
            elast[e] = k
            efree[e] = st + dur
            fin[k] = st + (3.0 if is_dma else dur)
            out.append(ins)
            for u in users[k]:
                lat = 0.1 if (todo[u].eng == e and not is_dma) else 0.8
                if fin[k] + lat > ready_t[u]:
                    ready_t[u] = fin[k] + lat
                    if crit[u] is None or crit[u][0] != "eng":
                        crit[u] = ("dep", k)
                ndep[u] -= 1
                if ndep[u] == 0:
                    heapq.heappush(ready[todo[u].eng], u)
        if _CACHE.get("crit_seg"):
            tgt = _CACHE["crit_seg"]
            ks = [k for k, ins in enumerate(todo) if ins.seg == tgt]
            k = max(ks, key=lambda q: fin[q]) if ks else None
            lines = {t[0]: t for t in _CACHE.get("lines", [])}
            base = self.pos
            prevdesc = None
            hops = 0
            while k is not None and hops < 4000:
                ins = todo[k]
                li = lines.get(base + self.order[self.pos:].index(ins)) if False else None
                desc = (ins.seg, ins.eng, getattr(ins, "src", None), crit[k][0] if crit[k] else None)
                if desc != prevdesc:
                    print("crit: t=%.1f" % fin[k], desc)
                    prevdesc = desc
                k = crit[k][1] if crit[k] else None
                hops += 1
                if ins.seg is not None and ins.seg[1] < tgt[1] - 1:
                    break
        if _CACHE.get("sched_report"):
            last = {}
            for k, ins in enumerate(todo):
                if ins.seg is not None:
                    last[ins.seg] = max(last.get(ins.seg, 0.0), fin[k])
            prev = 0.0
            for sg in sorted(last, key=lambda x: (x[1], x[0])):
                if sg[0] == "B":
                    print("sched est: seg", sg, "done at %.1f us (+%.1f)" % (last[sg], last[sg] - prev))
                    prev = last[sg]
        return out

    def _reorder(self, todo):
        runs = []
        for ins in todo:
            if runs and runs[-1][0] == ins.seg:
                runs[-1][1].append(ins)
            else:
                runs.append((ins.seg, [ins]))
        out, k = [], 0
        while k < len(runs):
            lab, lst = runs[k]
            if lab is not None and lab[0] == "F" and k + 1 < len(runs) and runs[k + 1][0] is not None and runs[k + 1][0][0] == "B":
                out.extend(self._zipmerge(lst, runs[k + 1][1]))
                k += 2
            else:
                out.extend(lst)
                k += 1
        assert len(out) == len(todo)
        return out

    def emit(self, final=False):
        nc, es = self.nc, self.es
        if self.esem is None:
            self.esem = {e: es.enter_context(nc.semaphore("se_" + e)) for e in self.engs}
        esem, ksem, kcnt, ecnt, waited = self.esem, self.ksem, self.kcnt, self.ecnt, self.waited
        todo = self.order[self.pos:]
        if _CACHE.get('zipmerge'):
            todo = self._reorder(todo)
        elif not _CACHE.get('no_sched'):
            todo = self._schedule(todo)
        if _CACHE.get('maxinst'):
            todo = todo[:_CACHE['maxinst']]
        epos = {id(x): k for k, x in enumerate(todo)}
        for ins in todo:
            if ins.key is None:
                ins.needs = ins.keep
        for ins in todo:
            latest = {}
            eff = []
            for d in ins.deps:
                if d.key is not None or id(d) not in epos:
                    eff.append(d)
                    continue
                c = latest.get(d.eng)
                if c is None or epos[id(d)] > epos[id(c)]:
                    latest[d.eng] = d
            for d in latest.values():
                d.needs = True
                eff.append(d)
            ins.edeps = eff
        for ins in todo:
            if ins.key is not None:
                if ins.key not in ksem:
                    ksem[ins.key] = es.enter_context(nc.semaphore("sk_" + ins.key))
                    kcnt[ins.key] = 0
                kcnt[ins.key] += 16
                ins.ticket = kcnt[ins.key]
            elif ins.needs:
                ecnt[ins.eng] += 1
                ins.ticket = ecnt[ins.eng]
        for ins in todo:
            eng = self.engs[ins.eng]
            wl = {}
            for d in (ins.edeps if ins.edeps is not None else ins.deps):
                if d.key is not None:
                    grp = d.key.startswith("G_")
                    s, v = ksem[d.key], (kcnt[d.key] if grp else d.ticket)
                else:
                    s, v = esem[d.eng], d.ticket
                assert v is not None, (ins.eng, d.eng)
                if wl.get(s.name, (None, 0))[1] < v:
                    wl[s.name] = (s, v)
            for nm, (s, v) in wl.items():
                if waited[ins.eng].get(nm, 0) >= v:
                    continue
                waited[ins.eng][nm] = v
                eng.wait_ge(s, v)
            ROT["p"] = ins.rot
            bi = ins.fn(eng)
            if ins.key is not None:
                bi.then_inc(ksem[ins.key], 16)
            elif ins.needs:
                bi.then_inc(esem[ins.eng], 1)
        self.pos = len(self.order)
        if final:
            for k, s in ksem.items():
                nc.sync.wait_ge(s, kcnt[k])
            print("bass program: insts", len(self.order), "sems", len(ksem) + 5, "eng tickets", ecnt)


def build_program():
    nc = bass.Bass("TRN2", target_bir_lowering=False)
    es = contextlib.ExitStack()
    T = Tracker(nc, es)

    def din(name, shape):
        return nc.dram_tensor(name, list(shape), F32, kind="ExternalInput").ap()

    def dout(name, shape):
        return nc.dram_tensor(name, list(shape), F32, kind="ExternalOutput").ap()

    xp = din("xp", [NT * 128, D]); xh = din("xh", [NH * 128, D]); xs = din("xs", [128, D])
    ck = din("ck", [16, 128, 128]); cv = din("cv", [16, 128, 128]); sg = din("sg", [16, 4, 64, 128])
    w_in = din("w_in", [D, INW]); w_out = din("w_out", [D, D]); w_up = din("w_up", [D, DFF]); w_down = din("w_down", [DFF, D])
    wgk = din("wgk", [32, 256])
    cs_p = din("cs_p", [NT * 128, 64]); cs_h = din("cs_h", [128, 64]); cs_s = din("cs_s", [128, 64])
    c_idf = din("c_idf", [128, 128]); c_ltri = din("c_ltri", [128, 128]); c_utri = din("c_utri", [128, 128])
    c_ltri_s = din("c_ltri_s", [128, 128]); c_utri_s = din("c_utri_s", [128, 128])
    c_cm = din("c_cm", [128, 128]); c_cm_s = din("c_cm_s", [128, 128])
    c_mask = din("c_mask", [128, 256]); c_mask0 = din("c_mask0", [128, 256])
    c_gpre = din("c_gpre", [128, 8]); c_gfpre = din("c_gfpre", [128, 8])
    c_gpost = din("c_gpost", [128, D]); c_gfpost = din("c_gfpost", [128, D]); c_ggla = din("c_ggla", [128, 512])
    c_sink = din("c_sink", [128, 8])
    c_masks = din("c_masks", [128, 136]); c_sinks = din("c_sinks", [32, 2]); c_bmask = din("c_bmask", [128, 16 * 128])
    c_rowmask = din("c_rowmask", [128, 16]); c_seqsel = din("c_seqsel", [128, 16])
    scr = nc.dram_tensor("scr_attn", [16, 8, 2, 4, 64], BF16).ap()

    y_p = dout("y_p", [NT * 128, D]); y_s = dout("y_s", [128, D])
    kw_p = dout("kw_p", [128, 128]); vw_p = dout("vw_p", [128, 128]); gl_p = dout("gl_p", [128, 2, 128])
    kw_s = dout("kw_s", [16, 128, 128]); vw_s = dout("vw_s", [16, 128, 128]); gl_s = dout("gl_s", [16, 4, 64, 128])

    h2T = T.sb("h2T", [128, 8, (NT + 1) * 128], BF16)
    xbuf = [T.sb(f"xbuf{i}", [128, D], F32) for i in range(3)]
    st = T.sb("st", [128, 16], F32)
    stb = T.sb("stb", [128, 16], F32)
    tmp = T.sb("tmp", [128, D], F32)
    epsb = T.sb("epsb", [128, 1], F32)
    esA = contextlib.ExitStack()
    _sb = T.sb
    A = lambda name, shape, dt: _sb(name, shape, dt, scope=esA)
    Win = A("Win", [128, 8, INW], BF16)
    Wout = A("Wout", [128, 8, D], BF16)
    idf = A("idf", [128, 128], F32); idb = A("idb", [128, 128], BF16)
    ltri = A("ltri", [128, 128], F32); utri = A("utri", [128, 128], F32)
    ltri_s = A("ltri_s", [128, 128], F32); utri_s = A("utri_s", [128, 128], F32)
    cm = A("cm", [128, 128], F32); cm_s = A("cm_s", [128, 128], F32)
    maskb = A("maskb", [128, 256], BF16); mask0b = A("mask0b", [128, 256], BF16)
    gpre = A("gpre", [128, 8], F32); gfpre = A("gfpre", [128, 8], F32)
    gpost = A("gpost", [128, D], F32); ggla = A("ggla", [128, 512], F32)
    sink = A("sink", [128, 8], F32)
    wgk_sb = A("wgk_sb", [32, 256], F32)
    ones16 = A("ones16", [128, 2], F32)
    glrT = A("glrT", [32, 128], F32)
    S = A("S", [128, 2, 128], F32); Sb = A("Sb", [128, 2, 128], BF16)
    csb = [A(f"csb{i}", [128, 64], F32) for i in range(2)]
    hb = A("hb", [128, D], BF16)
    hT = A("hT", [128, 8, 128], BF16)
    qk = A("qk", [128, 10, 64], F32); qkr = A("qkr", [128, 10, 64], F32); qkb = A("qkb", [128, 10, 64], BF16)
    rt = [A(f"rt{i}", [128, 10, 32], F32) for i in range(2)]
    vf = A("vf", [128, 128], F32)
    vb = [A(f"vb{i}", [128, 128], BF16) for i in range(3)]
    kT = [A(f"kT{i}", [128, 128], BF16) for i in range(3)]
    qT = A("qT", [128, 4, 128], BF16)
    Pm = A("Pm", [128, 8, 256], BF16)
    PT = A("PT", [128, 16, 128], BF16)
    ast = A("ast", [128, 48], F32)
    mix = A("mix", [128, D], BF16)
    mixT = A("mixT", [128, 8, 128], BF16)
    glr_sb = A("glr_sb", [128, 16], F32)
    gk_sb = A("gk_sb", [128, 256], F32); gq_sb = A("gq_sb", [128, 256], F32)
    az = A("az", [128, 256], F32); ez = A("ez", [128, 256], F32); la = A("la", [128, 256], F32)
    eB = A("eB", [128, 256], F32); eNB = A("eNB", [128, 256], F32); eR = A("eR", [128, 256], F32)
    ablk = A("ablk", [128, 2], F32)
    qd = A("qd", [128, 256], BF16); ki = A("ki", [128, 256], BF16); kd = A("kd", [128, 256], BF16)
    qdT = A("qdT", [128, 2, 128], BF16)
    qdTz = A("qdTz", [128, 4, 128], BF16); kiTz = A("kiTz", [128, 4, 128], BF16)
    ATm = A("ATm", [128, 4, 128], BF16)
    gvb = A("gvb", [128, 512], BF16)
    sil = A("sil", [128, 512], F32); t1 = A("t1", [128, 512], F32)
    gst = A("gst", [128, 8], F32)
    esS = contextlib.ExitStack()
    SA = lambda name, shape, dt: _sb(name, shape, dt, scope=esS)
    cvb = SA("cvb", [128, 16, 128], BF16)
    kTcz = SA("kTcz", [128, 2, 16, 128], BF16); kTnz = SA("kTnz", [128, 2, 128], BF16)
    qTs = SA("qTs", [128, 16, 4, 8], BF16)
    vn = SA("vn", [8, 16, 128], BF16)
    masks = SA("masks", [128, 136], BF16); sink_s = SA("sink_s", [32, 2], F32)
    Pn = SA("Pn", [32, 8, 8], BF16); PTn = SA("PTn", [8, 8, 32], BF16)
    attn_s = SA("attn_s", [32, 2, 16, 64], BF16)
    sst = SA("sst", [32, 64], F32)
    bmask = SA("bmask", [128, 16, 128], BF16); rowmask = SA("rowmask", [128, 16], F32); seqsel = SA("seqsel", [128, 16], F32)
    ablk_s = SA("ablk_s", [128, 16, 2], F32)
    qdTm = [SA(f"qdTm{i}", [128, 16, 128], BF16) for i in range(2)]
    S0F = [SA(f"S0f{i}", [128, 2, 2, 128], F32) for i in range(2)]
    S0B = [SA(f"S0blk{i}", [128, 2, 2, 256], BF16) for i in range(2)]
    ckb_v = Pm.t[:].rearrange("p a (c k) -> p (a c) k", c=2)
    kdm_v = PT.t[:, 8:16, :].rearrange("p a k -> p (a k)").rearrange("p (b c) -> p b c", b=4)

    PU = [T.ps(f"PU{i}", [128, 1024], F32) for i in range(3)]
    PTb = [T.ps(f"PTb{i}", [128, 1024], BF16) for i in range(2)]
    cnt = {"u": 0, "t": 0}

    def pu(avoid=None, small=False):
        if T.seg is not None and T.seg[0] == "F":
            return PU[0]
        if T.seg is not None and T.seg[0] == "B":
            return PU[1 + T.seg[1] % 2]
        pool_ = (PU + cnt.get("extra_small", [])) if small else PU
        while True:
            cnt["u"] += 1
            u = pool_[cnt["u"] % len(pool_)]
            if u is not avoid:
                return u

    def ptb():
        if cnt.get("force_t") is not None:
            return PTb[cnt["force_t"]]
        if T.seg is not None and T.seg[0] == "F":
            return PTb[0]
        if T.seg is not None and T.seg[0] == "B":
            return PTb[1]
        cnt["t"] += 1
        return PTb[cnt["t"] % 2]

    for (sbt, dr) in ((idf, c_idf), (ltri, c_ltri), (utri, c_utri), (ltri_s, c_ltri_s), (utri_s, c_utri_s), (cm, c_cm),
                      (cm_s, c_cm_s), (gpre, c_gpre), (gfpre, c_gfpre), (gpost, c_gpost),
                      (ggla, c_ggla), (sink, c_sink), (wgk_sb, wgk)):
        T.dma("sp", sbt[:], dr, w=[sbt], key="G_const")
    for (sbt, dr) in ((sink_s, c_sinks), (rowmask, c_rowmask), (seqsel, c_seqsel)):
        T.dma("sp", sbt[:], dr, w=[sbt], key="G_const")
    for (sbt, dr) in ((idb, c_idf), (maskb, c_mask), (mask0b, c_mask0), (masks, c_masks)):
        T.dma("pool", sbt[:], dr, w=[sbt], key="G_constb")
    T.dma("pool", bmask[:].rearrange("p a t -> p (a t)"), c_bmask, w=[bmask], key="G_constb")
    T.dma("pool", ckb_v, ck.rearrange("b j c -> j b c"), w=[Pm], key="G_constb")
    T.dma("pool", cvb[:], cv.rearrange("b j c -> j b c"), w=[cvb], key="G_constb")
    T.dve(lambda e: e.memset(ones16[:], 1.0 / 16.0), w=[ones16])
    T.dve(lambda e: e.memset(epsb[:], EPS), w=[epsb])
    T.dve(lambda e: e.memset(glrT[:], 1.0), w=[glrT])
    T.dve(lambda e: e.memset(S[:], 0.0), w=[S])
    T.dve(lambda e: e.memset(Sb[:], 0.0), w=[Sb])
    T.dve(lambda e: e.memset(qdTz[:], 0.0), w=[qdTz])
    T.dve(lambda e: e.memset(kiTz[:], 0.0), w=[kiTz])
    T.dve(lambda e: e.memset(kTcz[:], 0.0), w=[kTcz])
    T.dve(lambda e: e.memset(kTnz[:], 0.0), w=[kTnz])
    for sb_ in S0B:
        T.dve(lambda e, sb_=sb_: e.memset(sb_[:], 0.0), w=[sb_])
    for k in range(8):
        rows = slice(k * 128, (k + 1) * 128)
        T.dma("pool", Win[:, k, 0:1280], w_in[rows, 0:1280], w=[Win], key="G_win")
        T.dma("pool", Win[:, k, 1280:1296], w_in[rows, 2304:2320], w=[Win], key="G_win")
        T.dma("pool", Win[:, k, 1296:2320], w_in[rows, 1280:2304], w=[Win], key="G_win")
    for k in range(8):
        T.dma("pool", Wout[:, k, :], w_out[k * 128:(k + 1) * 128, :], w=[Wout], key="G_wout")

    T.dma("sp", kw_s[:, 0:120, :], ck[:, 8:128, :], key="o_kws")
    T.dma("sp", vw_s[:, 0:120, :], cv[:, 8:128, :], key="o_vws")

    def rstd_from_ss(ss_ap, n, out_ap, rbufs, wbufs):
        T.act(lambda e: e.activation(out=out_ap, in_=ss_ap, func=AF.Ln, scale=1.0 / n, bias=epsb[:, 0:1]), r=list(rbufs) + [epsb], w=wbufs)
        T.act(lambda e: e.activation(out=out_ap, in_=out_ap, func=AF.Exp, scale=-0.5), r=wbufs, w=wbufs)

    def front(x_dram, slot, gcol, dst, dst_ap):
        xt = xbuf[slot]
        T.dma("sp", xt[:], x_dram, w=[xt], key=f"x{slot}")
        norm_T(xt, gcol, dst, dst_ap)

    def norm_T(xt, gcol, dst, dst_ap, tail=False):
        sx = stb if tail else st
        c0 = 4 if tail else 0
        if tail and isinstance(hb, Rot):
            hx = hb.bufs[(ROT["p"] + 1) % len(hb.bufs)]
        else:
            hx = hb.cur if isinstance(hb, Rot) else hb
        if isinstance(sx, Rot):
            sx = sx.cur
        T.act(lambda e: e.activation(out=hx[:], in_=xt[:], func=AF.Square, accum_out=sx[:, c0:c0 + 1]), r=[xt], w=[sx, hx])
        rstd_from_ss(sx[:, c0:c0 + 1], D, sx[:, c0 + 1:c0 + 2], [sx], [sx])
        T.dve(lambda e: e.tensor_scalar(out=hx[:], in0=xt[:], scalar1=sx[:, c0 + 1:c0 + 2], scalar2=None, op0=ALU.mult), r=[xt, sx], w=[hx])
        pt = ptb()
        for k in range(8):
            T.pe(lambda e, k=k: e.transpose(out=pt[:, k * 128:(k + 1) * 128], in_=hx[:, k * 128:(k + 1) * 128], identity=idb[:]),
                 r=[hx, idb], w=[pt])
        T.dve(lambda e: e.tensor_tensor(out=dst_ap, in0=pt[:].rearrange("p (k t) -> p k t", k=8),
                                        in1=gcol[:].unsqueeze(2).to_broadcast([128, 8, 128]), op=ALU.mult),
              r=[pt, gcol], w=[dst])

    def proj(ps_ap, c0, n, ps):
        for k in range(8):
            T.pe(lambda e, k=k: e.matmul(ps_ap, lhsT=hT[:, k, :], rhs=Win[:, k, c0:c0 + n], start=(k == 0), stop=(k == 7)),
                 r=[hT, Win], w=[ps])

    def rope_kv(psB, cst, kslot, write_kT=True):
        T.act(lambda e: e.copy(out=qk[:, 8:10, :], in_=psB[:, 0:128].rearrange("p (h d) -> p h d", h=2)), r=[psB], w=[qk])
        T.act(lambda e: e.copy(out=vf[:], in_=psB[:, 128:256]), r=[psB], w=[vf])
        T.pool(lambda e: e.tensor_copy(out=vb[kslot][:], in_=vf[:]), r=[vf], w=[vb[kslot]])

    def rope(cst, h0, h1):
        n = h1 - h0
        cos = cst[:, 0:32].unsqueeze(1).to_broadcast([128, n, 32])
        sin = cst[:, 32:64].unsqueeze(1).to_broadcast([128, n, 32])
        x1, x2 = qk[:, h0:h1, 0:32], qk[:, h0:h1, 32:64]
        T.dve(lambda e: e.tensor_tensor(out=rt[0][:, h0:h1, :], in0=x1, in1=cos, op=ALU.mult), r=[qk, cst], w=[rt[0]])
        T.pool(lambda e: e.tensor_tensor(out=rt[1][:, h0:h1, :], in0=x2, in1=sin, op=ALU.mult), r=[qk, cst], w=[rt[1]])
        T.dve(lambda e: e.tensor_tensor(out=qkr[:, h0:h1, 0:32], in0=rt[0][:, h0:h1, :], in1=rt[1][:, h0:h1, :], op=ALU.subtract),
              r=[rt[0], rt[1]], w=[qkr])
        T.dve(lambda e: e.tensor_tensor(out=rt[0][:, h0:h1, :], in0=x2, in1=cos, op=ALU.mult), r=[qk, cst], w=[rt[0]])
        T.pool(lambda e: e.tensor_tensor(out=rt[1][:, h0:h1, :], in0=x1, in1=sin, op=ALU.mult), r=[qk, cst], w=[rt[1]])
        T.dve(lambda e: e.tensor_tensor(out=qkr[:, h0:h1, 32:64], in0=rt[0][:, h0:h1, :], in1=rt[1][:, h0:h1, :], op=ALU.add),
              r=[rt[0], rt[1]], w=[qkr])
        if h0 == 0:
            for g in range(2):
                T.dve(lambda e, g=g: e.tensor_copy(out=qkb[:, g:8:2, :], in_=qkr[:, 4 * g:4 * g + 4, :]), r=[qkr], w=[qkb])
        T.dve(lambda e: e.tensor_copy(out=qkb[:, 8:10, :], in_=qkr[:, 8:10, :]), r=[qkr], w=[qkb])

    def k_transpose(kslot):
        pt = ptb()
        T.pe(lambda e: e.transpose(out=pt[:, 0:128], in_=qkb[:, 8:10, :].rearrange("p h d -> p (h d)"), identity=idb[:]),
             r=[qkb, idb], w=[pt])
        T.act(lambda e: e.copy(out=kT[kslot][:], in_=pt[:, 0:128]), r=[pt], w=[kT[kslot]])

    def evac_gate(psC):
        T.act(lambda e: e.copy(out=glr_sb[:], in_=psC[:, 256:272]), r=[psC], w=[glr_sb])
        T.dve(lambda e: e.tensor_copy(out=gk_sb[:], in_=psC[:, 0:256]), r=[psC], w=[gk_sb])

    def gate_common(ltm, utm, full):
        px = pu()
        T.pe(lambda e: e.matmul(px[0:16, 0:128], lhsT=glr_sb[:], rhs=idf[:], start=True, stop=True), r=[glr_sb, idf], w=[px])
        T.act(lambda e: e.copy(out=glrT[0:16, :], in_=px[0:16, 0:128]), r=[px], w=[glrT])
        T.pe(lambda e: e.matmul(px[:, 512:768], lhsT=glrT[:], rhs=wgk_sb[:], start=True, stop=True), r=[glrT, wgk_sb], w=[px])
        z = px[:, 512:768]
        T.act(lambda e: e.activation(out=az[:], in_=z, func=AF.Abs), r=[px], w=[az])
        T.act(lambda e: e.activation(out=ez[:], in_=az[:], func=AF.Exp, scale=-1.0), r=[az], w=[ez])
        T.act(lambda e: e.activation(out=ez[:], in_=ez[:], func=AF.Ln, bias=1.0), r=[ez], w=[ez])
        T.dve(lambda e: e.tensor_single_scalar(out=az[:], in_=z, scalar=0.0, op=ALU.min), r=[px], w=[az])
        T.dve(lambda e: e.tensor_tensor(out=la[:], in0=az[:], in1=ez[:], op=ALU.subtract), r=[az, ez], w=[la])
        pb = pu()
        if full:
            T.pe(lambda e: e.matmul(pb[:, 0:256], lhsT=ltm[:], rhs=la[:], start=True, stop=True), r=[ltm, la], w=[pb])
        T.pe(lambda e: e.matmul(pb[:, 256:512], lhsT=utm[:], rhs=la[:], start=True, stop=True), r=[utm, la], w=[pb])
        if full:
            T.act(lambda e: e.activation(out=eB[:], in_=pb[:, 0:256], func=AF.Exp), r=[pb], w=[eB])
            T.act(lambda e: e.activation(out=eNB[:], in_=pb[:, 0:256], func=AF.Exp, scale=-1.0), r=[pb], w=[eNB])
        T.act(lambda e: e.activation(out=eR[:], in_=pb[:, 256:512], func=AF.Exp), r=[pb], w=[eR])
        T.dve(lambda e: e.tensor_tensor(out=kd[:], in0=gk_sb[:], in1=eR[:], op=ALU.mult), r=[gk_sb, eR], w=[kd])

    def ablk_prompt():
        pa = pu()
        for p in range(2):
            T.pe(lambda e, p=p: e.matmul(pa[:, 2 * p:2 * p + 2], lhsT=la[:, p * 128:(p + 1) * 128], rhs=ones16[:], start=True, stop=True),
                 r=[la, ones16], w=[pa])
        T.act(lambda e: e.activation(out=ablk[:], in_=pa[:, 0:4:2], func=AF.Exp), r=[pa], w=[ablk])

    def state_update_prompt():
        for p in range(2):
            pss = pu()
            T.pe(lambda e, p=p, pss=pss: e.matmul(pss[:, 0:256], lhsT=kd[:, p * 128:(p + 1) * 128], rhs=gvb[:, p * 256:(p + 1) * 256],
                                         start=True, stop=True), r=[kd, gvb], w=[pss])
            for hp in range(2):
                rows = slice(hp * 64, (hp + 1) * 64)
                T.dve(lambda e, p=p, rows=rows, hp=hp, pss=pss: e.scalar_tensor_tensor(
                    out=S[rows, p, :], in0=S[rows, p, :], scalar=ablk[rows, p:p + 1], in1=pss[rows, hp * 128:(hp + 1) * 128],
                    op0=ALU.mult, op1=ALU.add), r=[S, ablk, pss], w=[S])
        T.pool(lambda e: e.tensor_copy(out=Sb[:], in_=S[:]), r=[S], w=[Sb])

    def hist_front(i):
        last = (i == NH - 1)
        front(xh[i * 128:(i + 1) * 128, :], i % 2, gpre, hT, hT[:])
        psC = pu()
        proj(psC[:, 0:272], CGK, 272, psC)
        evac_gate(psC)
        psD = pu()
        proj(psD[:, 0:512], CGV, 512, psD)
        T.act(lambda e, psD=psD: e.copy(out=gvb[:], in_=psD[:, 0:512]), r=[psD], w=[gvb])
        if last:
            T.dma("sp", csb[1][:], cs_h, w=[csb[1]], key="cs1")
            psB = pu()
            proj(psB[:, 0:256], CK, 256, psB)
            rope_kv(psB, csb[1], 2)
            rope(csb[1], 8, 10)
            k_transpose(2)

    def hist_back(i):
        gate_common(ltri, utri, full=False)
        ablk_prompt()
        state_update_prompt()

    def gla_intra_and_out(cmask, inter_fn):
        T.dve(lambda e: e.scalar_tensor_tensor(out=qd[:], in0=gq_sb[:], scalar=0.125, in1=eB[:], op0=ALU.mult, op1=ALU.mult),
              r=[gq_sb, eB], w=[qd])
        T.dve(lambda e: e.tensor_tensor(out=ki[:], in0=gk_sb[:], in1=eNB[:], op=ALU.mult), r=[gk_sb, eNB], w=[ki])
        pt = ptb()
        for p in range(2):
            T.pe(lambda e, p=p: e.transpose(out=pt[:, p * 128:(p + 1) * 128], in_=qd[:, p * 128:(p + 1) * 128], identity=idb[:]),
                 r=[qd, idb], w=[pt])
            T.pe(lambda e, p=p: e.transpose(out=pt[:, 256 + p * 128:256 + (p + 1) * 128], in_=ki[:, p * 128:(p + 1) * 128], identity=idb[:]),
                 r=[ki, idb], w=[pt])
        T.act(lambda e: e.copy(out=qdT[:].rearrange("p a t -> p (a t)"), in_=pt[:, 0:256]), r=[pt], w=[qdT])
        for hp in range(2):
            rows = slice(hp * 64, (hp + 1) * 64)
            T.act(lambda e, rows=rows, hp=hp: e.copy(out=qdTz[rows, hp:4:2, :], in_=pt[rows, 0:256].rearrange("p (a t) -> p a t", a=2)),
                  r=[pt], w=[qdTz])
            T.act(lambda e, rows=rows, hp=hp: e.copy(out=kiTz[rows, hp:4:2, :], in_=pt[rows, 256:512].rearrange("p (a t) -> p a t", a=2)),
                  r=[pt], w=[kiTz])
        pat = pu()
        for h in range(4):
            p = h // 2
            T.pe(lambda e, h=h, p=p: e.matmul(pat[:, h * 128:(h + 1) * 128], lhsT=kiTz[:, h, :], rhs=qdT[:, p, :],
                                             start=True, stop=True), r=[kiTz, qdT], w=[pat])
        T.dve(lambda e: e.tensor_tensor(out=ATm[:], in0=pat[:, 0:512].rearrange("p (h t) -> p h t", h=4),
                                        in1=cmask[:].unsqueeze(1).to_broadcast([128, 4, 128]), op=ALU.mult), r=[pat, cmask], w=[ATm])
        po = pu()
        if inter_fn is None:
            sample_inter_and_state(po)
            for h in range(4):
                c0 = (h // 2) * 512 + (h % 2) * 128
                T.pe(lambda e, h=h, c0=c0: e.matmul(po[:, c0:c0 + 128], lhsT=ATm[:, h, :], rhs=gvb[:, h * 128:(h + 1) * 128],
                                                   start=False, stop=(h % 2 == 1)), r=[ATm, gvb], w=[po])
            return po
        for h in range(4):
            T.pe(lambda e, h=h: e.matmul(po[:, h * 128:(h + 1) * 128], lhsT=ATm[:, h, :], rhs=gvb[:, h * 128:(h + 1) * 128],
                                         start=True, stop=False), r=[ATm, gvb], w=[po])
            inter_fn(po, h)
        return po

    def sample_inter_and_state(po):
        pa = pu(po)
        for p in range(2):
            T.pe(lambda e, p=p: e.matmul(pa[:, p * 16:(p + 1) * 16], lhsT=la[:, p * 128:(p + 1) * 128], rhs=seqsel[:], start=True, stop=True),
                 r=[la, seqsel], w=[pa])
        T.act(lambda e: e.activation(out=ablk_s[:].rearrange("q b p -> q p b"), in_=pa[:, 0:32].rearrange("q (p b) -> q p b", p=2),
                                     func=AF.Exp), r=[pa], w=[ablk_s])
        for p in range(2):
            T.dve(lambda e, p=p: e.tensor_tensor(out=qdTm[p][:], in0=qdT[:, p, :].unsqueeze(1).to_broadcast([128, 16, 128]),
                                                 in1=bmask[:], op=ALU.mult), r=[qdT, bmask], w=[qdTm[p]])
        sg_v = sg.rearrange("b (p hp) k v -> hp k b p v", hp=2)
        gl_v = gl_s.rearrange("b (p hp) k v -> hp k b p v", hp=2)
        for o in range(8):
            sf, sb_ = S0F[o % 2], S0B[o % 2]
            ko = (o % 2) * 2
            for hp in range(2):
                T.dma("sp", sf[hp * 64:(hp + 1) * 64, :, :, :], sg_v[hp][:, o * 2:(o + 1) * 2], w=[sf], key=f"s0_{hp}_{o % 2}")
            for hp in range(2):
                rows = slice(hp * 64, (hp + 1) * 64)
                T.pool(lambda e, rows=rows, hp=hp, sf=sf, sb_=sb_: e.tensor_copy(
                    out=sb_[rows, :, :, hp * 128:(hp + 1) * 128].rearrange("k b p v -> k (b p) v"),
                    in_=sf[rows, :, :, :].rearrange("k b p v -> k (b p) v")), r=[sf], w=[sb_])
            for bb in range(2):
                b = o * 2 + bb
                for p in range(2):
                    T.pe(lambda e, b=b, bb=bb, p=p, sb_=sb_: e.matmul(po[:, p * 512:p * 512 + 256], lhsT=qdTm[p][:, b, :], rhs=sb_[:, bb, p, :],
                                                                     start=(b == 0), stop=False), r=[qdTm[p], sb_], w=[po])
            for bb in range(2):
                b = o * 2 + bb
                T.dve(lambda e, b=b, bb=bb, ko=ko: e.tensor_scalar(out=kdm_v[:, ko + bb, :], in0=kd[:], scalar1=rowmask[:, b:b + 1], scalar2=None,
                                                                   op0=ALU.mult), r=[kd, rowmask], w=[PT])
            T.dve(lambda e, o=o, sf=sf: e.tensor_tensor(out=sf[:].rearrange("k b p v -> k (b p) v"), in0=sf[:].rearrange("k b p v -> k (b p) v"),
                                                        in1=ablk_s[:, o * 2:(o + 1) * 2, :].rearrange("k b p -> k (b p)").unsqueeze(2).to_broadcast([128, 4, 128]),
                                                        op=ALU.mult), r=[sf, ablk_s], w=[sf])
            pss = pu(po)
            for bb in range(2):
                for p in range(2):
                    c0 = (bb * 2 + p) * 256
                    T.pe(lambda e, bb=bb, p=p, c0=c0, pss=pss, ko=ko: e.matmul(pss[:, c0:c0 + 256], lhsT=kdm_v[:, ko + bb, p * 128:(p + 1) * 128],
                                                                              rhs=gvb[:, p * 256:(p + 1) * 256], start=True, stop=True),
                         r=[PT, gvb], w=[pss])
            for hp in range(2):
                rows = slice(hp * 64, (hp + 1) * 64)
                T.dve(lambda e, rows=rows, hp=hp, pss=pss, sf=sf: e.tensor_tensor(
                    out=sf[rows, :, :, :].rearrange("k b p v -> k (b p) v"),
                    in0=sf[rows, :, :, :].rearrange("k b p v -> k (b p) v"),
                    in1=pss[rows, :].rearrange("k (c v) -> k c v", c=4)[:, :, hp * 128:(hp + 1) * 128], op=ALU.add),
                    r=[sf, pss], w=[sf])
            for hp in range(2):
                T.dma("sp", gl_v[hp][:, o * 2:(o + 1) * 2], sf[hp * 64:(hp + 1) * 64, :, :, :], r=[sf], key=f"s0o_{hp}_{o % 2}")

    def gate_out_prep(psE):
        T.act(lambda e: e.activation(out=sil[:], in_=psE[:, 0:512], func=AF.Silu), r=[psE], w=[sil])
        T.pool(lambda e: e.tensor_tensor(out=t1[:], in0=sil[:], in1=ggla[:], op=ALU.mult), r=[sil, ggla], w=[t1])

    def gla_finish(po, banked=False):
        hc = (lambda h: (h // 2) * 512 + (h % 2) * 128) if banked else (lambda h: h * 128)
        for h in range(4):
            T.act(lambda e, h=h: e.activation(out=mix[:, 512 + h * 128:512 + (h + 1) * 128], in_=po[:, hc(h):hc(h) + 128], func=AF.Square,
                                              accum_out=gst[:, h:h + 1]), r=[po], w=[gst, mix])
        rstd_from_ss(gst[:, 0:4], 128, gst[:, 4:8], [gst], [gst])
        for h in range(4):
            T.dve(lambda e, h=h: e.scalar_tensor_tensor(out=mix[:, 512 + h * 128:512 + (h + 1) * 128], in0=po[:, hc(h):hc(h) + 128],
                                                        scalar=gst[:, 4 + h:5 + h], in1=t1[:, h * 128:(h + 1) * 128],
                                                        op0=ALU.mult, op1=ALU.mult), r=[po, gst, t1], w=[mix])

    def attention_prompt(cur, prev, mk):
        ptq = ptb()
        for i in range(4):
            T.pe(lambda e, i=i: e.transpose(out=ptq[:, i * 128:(i + 1) * 128], in_=qkb[:, 2 * i:2 * i + 2, :].rearrange("p h d -> p (h d)"), identity=idb[:]),
                 r=[qkb, idb], w=[ptq])
        T.act(lambda e: e.copy(out=qT[:].rearrange("p a t -> p (a t)"), in_=ptq[:, 0:512]), r=[ptq], w=[qT])
        for g in range(2):
            pss = pu()
            rows = slice(g * 64, (g + 1) * 64)
            for i in range(4):
                for blk, kslot in ((0, prev), (1, cur)):
                    T.pe(lambda e, i=i, rows=rows, pss=pss, blk=blk, kslot=kslot: e.matmul(
                        pss[:, i * 256 + blk * 128:i * 256 + (blk + 1) * 128], lhsT=qT[rows, i, :], rhs=kT[kslot][rows, :],
                        start=True, stop=False), r=[qT, kT[kslot]], w=[pss])
                    T.pe(lambda e, i=i, pss=pss, blk=blk: e.matmul(
                        pss[:, i * 256 + blk * 128:i * 256 + (blk + 1) * 128], lhsT=idb[:], rhs=mk[:, blk * 128:(blk + 1) * 128],
                        start=False, stop=True), r=[idb, mk], w=[pss])
            T.dve(lambda e, g=g, pss=pss: e.tensor_reduce(out=ast[:, g * 4:(g + 1) * 4], in_=pss[:].rearrange("p (i k) -> p i k", i=4),
                                                         axis=AX.X, op=ALU.max), r=[pss], w=[ast])
            T.dve(lambda e, g=g: e.scalar_tensor_tensor(out=ast[:, 8 + g * 4:8 + (g + 1) * 4], in0=ast[:, g * 4:(g + 1) * 4], scalar=0.125,
                                                        in1=sink[:, g * 4:(g + 1) * 4], op0=ALU.mult, op1=ALU.max), r=[ast, sink], w=[ast])
            T.dve(lambda e, g=g: e.tensor_scalar(out=ast[:, 16 + g * 4:16 + (g + 1) * 4], in0=ast[:, 8 + g * 4:8 + (g + 1) * 4],
                                                 scalar1=-1.0, scalar2=None, op0=ALU.mult), r=[ast], w=[ast])
            for i in range(4):
                h = g * 4 + i
                T.act(lambda e, i=i, h=h, pss=pss: e.activation(out=Pm[:, h, :], in_=pss[:, i * 256:(i + 1) * 256], func=AF.Exp, scale=0.125,
                                                                bias=ast[:, 16 + h:17 + h], accum_out=ast[:, 24 + h:25 + h]),
                      r=[pss, ast], w=[Pm, ast])
        T.dve(lambda e: e.tensor_tensor(out=ast[:, 32:40], in0=sink[:], in1=ast[:, 8:16], op=ALU.subtract), r=[sink, ast], w=[ast])
        T.act(lambda e: e.activation(out=ast[:, 32:40], in_=ast[:, 32:40], func=AF.Exp), r=[ast], w=[ast])
        T.dve(lambda e: e.tensor_tensor(out=ast[:, 40:48], in0=ast[:, 32:40], in1=ast[:, 24:32], op=ALU.add), r=[ast], w=[ast])
        T.dve(lambda e: e.reciprocal(out=ast[:, 40:48], in_=ast[:, 40:48]), r=[ast], w=[ast])
        for half in range(2):
            pt = ptb()
            for j in range(8):
                idx = half * 8 + j
                h, blk = idx // 2, idx % 2
                T.pe(lambda e, j=j, h=h, blk=blk, pt=pt: e.transpose(out=pt[:, j * 128:(j + 1) * 128], in_=Pm[:, h, blk * 128:(blk + 1) * 128],
                                                                    identity=idb[:]), r=[Pm, idb], w=[pt])
            if half == 0:
                T.act(lambda e, pt=pt: e.copy(out=PT[:, 0:8, :].rearrange("p a t -> p (a t)"), in_=pt[:]), r=[pt], w=[PT])
            else:
                T.dve(lambda e, pt=pt: e.tensor_copy(out=PT[:, 8:16, :].rearrange("p a t -> p (a t)"), in_=pt[:]), r=[pt], w=[PT])
        po = pu()
        for h in range(8):
            g = h // 4
            T.pe(lambda e, h=h, g=g: e.matmul(po[:, h * 64:(h + 1) * 64], lhsT=PT[:, 2 * h, :], rhs=vb[prev][:, g * 64:(g + 1) * 64],
                                              start=True, stop=False), r=[PT, vb[prev]], w=[po])
            T.pe(lambda e, h=h, g=g: e.matmul(po[:, h * 64:(h + 1) * 64], lhsT=PT[:, 2 * h + 1, :], rhs=vb[cur][:, g * 64:(g + 1) * 64],
                                              start=False, stop=True), r=[PT, vb[cur]], w=[po])
        T.dve(lambda e: e.tensor_tensor(out=mix[:, 0:512].rearrange("p (h d) -> p h d", h=8), in0=po[:, 0:512].rearrange("p (h d) -> p h d", h=8),
                                        in1=ast[:, 40:48].unsqueeze(2).to_broadcast([128, 8, 64]), op=ALU.mult), r=[po, ast], w=[mix])

    def sample_cache_prep():
        for half in range(2):
            pt = ptb()
            for j in range(8):
                b = half * 8 + j
                T.pe(lambda e, j=j, b=b, pt=pt: e.transpose(out=pt[:, j * 128:(j + 1) * 128], in_=ckb_v[:, b, :], identity=idb[:]),
                     r=[Pm, idb], w=[pt])
            for g in range(2):
                rows = slice(g * 64, (g + 1) * 64)
                T.act(lambda e, g=g, rows=rows, half=half, pt=pt: e.copy(out=kTcz[rows, g, half * 8:(half + 1) * 8, :],
                                                                         in_=pt[rows, :].rearrange("p (b j) -> p b j", b=8)), r=[pt], w=[kTcz])

    vwbuf = Buf("vw_dram", None)

    def attention_sample(cur):
        ptq = ptb()
        for i in range(4):
            T.pe(lambda e, i=i: e.transpose(out=ptq[:, i * 128:(i + 1) * 128], in_=qkb[:, 2 * i:2 * i + 2, :].rearrange("p h d -> p (h d)"), identity=idb[:]),
                 r=[qkb, idb], w=[ptq])
        for i in range(4):
            T.act(lambda e, i=i: e.copy(out=qTs[:, :, i, :], in_=ptq[:, i * 128:(i + 1) * 128].rearrange("p (b t) -> p b t", b=16)),
                  r=[ptq], w=[qTs])
        for g in range(2):
            rows = slice(g * 64, (g + 1) * 64)
            T.act(lambda e, g=g, rows=rows: e.copy(out=kTnz[rows, g, :], in_=kT[cur][rows, :]), r=[kT[cur]], w=[kTnz])
        for b in range(16):
            T.dma("sp", kw_s[b, 120:128, :], qkr[b * 8:(b + 1) * 8, 8:10, :].rearrange("p h d -> p (h d)"), r=[qkr], key="o_kwsn")
            T.dma("sp", vw_s[b, 120:128, :], vf[b * 8:(b + 1) * 8, :], r=[vf], w=[vwbuf], key="G_vwsn")
        T.dma("pool", vn[:], vw_s[:, 120:128, :].rearrange("b t c -> t b c"), r=[vwbuf], w=[vn], key="vnload")
        for g in range(2):
            for hb in range(2):
                Sc = pu()
                Sx = pu()
                for bl in range(8):
                    b = hb * 8 + bl
                    T.pe(lambda e, b=b, bl=bl, g=g, Sc=Sc: e.matmul(Sc[0:32, bl * 128:(bl + 1) * 128], lhsT=qTs[:, b, :, :].rearrange("p i t -> p (i t)"),
                                                                   rhs=kTcz[:, g, b, :], start=True, stop=False), r=[qTs, kTcz], w=[Sc])
                    T.pe(lambda e, bl=bl, Sc=Sc: e.matmul(Sc[0:32, bl * 128:(bl + 1) * 128], lhsT=idb[:, 0:32], rhs=masks[:, 0:128],
                                                         start=False, stop=True), r=[idb, masks], w=[Sc])
                for bl in range(8):
                    b = hb * 8 + bl
                    T.pe(lambda e, b=b, bl=bl, g=g, Sx=Sx: e.matmul(Sx[0:32, bl * 8:(bl + 1) * 8], lhsT=qTs[:, b, :, :].rearrange("p i t -> p (i t)"),
                                                                   rhs=kTnz[:, g, b * 8:(b + 1) * 8], start=True, stop=False), r=[qTs, kTnz], w=[Sx])
                    T.pe(lambda e, bl=bl, Sx=Sx: e.matmul(Sx[0:32, bl * 8:(bl + 1) * 8], lhsT=idb[:, 0:32], rhs=masks[:, 128:136],
                                                         start=False, stop=True), r=[idb, masks], w=[Sx])
                T.dve(lambda e, Sc=Sc: e.tensor_reduce(out=sst[:, 0:8], in_=Sc[0:32, :].rearrange("p (b k) -> p b k", b=8), axis=AX.X, op=ALU.max),
                      r=[Sc], w=[sst])
                T.dve(lambda e, Sx=Sx: e.tensor_reduce(out=sst[:, 8:16], in_=Sx[0:32, 0:64].rearrange("p (b k) -> p b k", b=8), axis=AX.X, op=ALU.max),
                      r=[Sx], w=[sst])
                T.dve(lambda e: e.tensor_tensor(out=sst[:, 0:8], in0=sst[:, 0:8], in1=sst[:, 8:16], op=ALU.max), r=[sst], w=[sst])
                T.dve(lambda e, g=g: e.scalar_tensor_tensor(out=sst[:, 16:24], in0=sst[:, 0:8], scalar=0.125, in1=sink_s[:, g:g + 1].to_broadcast([32, 8]),
                                                            op0=ALU.mult, op1=ALU.max), r=[sst, sink_s], w=[sst])
                T.dve(lambda e: e.tensor_scalar(out=sst[:, 24:32], in0=sst[:, 16:24], scalar1=-1.0, scalar2=None, op0=ALU.mult), r=[sst], w=[sst])
                for bl in range(8):
                    T.act(lambda e, bl=bl, Sc=Sc: e.activation(out=Pm[0:32, bl // 2, (bl % 2) * 128:(bl % 2 + 1) * 128], in_=Sc[0:32, bl * 128:(bl + 1) * 128],
                                                               func=AF.Exp, scale=0.125, bias=sst[:, 24 + bl:25 + bl], accum_out=sst[:, 32 + bl:33 + bl]),
                          r=[Sc, sst], w=[Pm, sst])
                for bl in range(8):
                    T.act(lambda e, bl=bl, Sx=Sx: e.activation(out=Pn[:, bl, :], in_=Sx[0:32, bl * 8:(bl + 1) * 8],
                                                               func=AF.Exp, scale=0.125, bias=sst[:, 24 + bl:25 + bl], accum_out=sst[:, 40 + bl:41 + bl]),
                          r=[Sx, sst], w=[Pn, sst])
                T.dve(lambda e, g=g: e.tensor_tensor(out=sst[:, 48:56], in0=sink_s[:, g:g + 1].to_broadcast([32, 8]), in1=sst[:, 16:24], op=ALU.subtract),
                      r=[sst, sink_s], w=[sst])
                T.act(lambda e: e.activation(out=sst[:, 48:56], in_=sst[:, 48:56], func=AF.Exp), r=[sst], w=[sst])
                T.dve(lambda e: e.tensor_tensor(out=sst[:, 56:64], in0=sst[:, 32:40], in1=sst[:, 40:48], op=ALU.add), r=[sst], w=[sst])
                T.dve(lambda e: e.tensor_tensor(out=sst[:, 56:64], in0=sst[:, 56:64], in1=sst[:, 48:56], op=ALU.add), r=[sst], w=[sst])
                T.dve(lambda e: e.reciprocal(out=sst[:, 56:64], in_=sst[:, 56:64]), r=[sst], w=[sst])
                Pc = Pm[0:32, 0:4, :].rearrange("p a (c k) -> p (a c) k", c=2)
                T.dve(lambda e, Pc=Pc: e.tensor_tensor(out=Pc, in0=Pc, in1=sst[:, 56:64].unsqueeze(2).to_broadcast([32, 8, 128]), op=ALU.mult),
                      r=[Pm, sst], w=[Pm])
                T.dve(lambda e: e.tensor_tensor(out=Pn[:], in0=Pn[:], in1=sst[:, 56:64].unsqueeze(2).to_broadcast([32, 8, 8]), op=ALU.mult),
                      r=[Pn, sst], w=[Pn])
                ptc = ptb()
                for bl in range(8):
                    T.pe(lambda e, bl=bl, ptc=ptc: e.transpose(out=ptc[:, bl * 32:(bl + 1) * 32], in_=Pm[0:32, bl // 2, (bl % 2) * 128:(bl % 2 + 1) * 128],
                                                              identity=idb[0:32, 0:32]), r=[Pm, idb], w=[ptc])
                for bl in range(8):
                    T.pe(lambda e, bl=bl, ptc=ptc: e.transpose(out=ptc[0:8, 256 + bl * 32:256 + (bl + 1) * 32], in_=Pn[:, bl, :],
                                                              identity=idb[0:32, 0:32]), r=[Pn, idb], w=[ptc])
                T.act(lambda e, ptc=ptc: e.copy(out=PT[:, 0:2, :].rearrange("p a (c m) -> p (a c) m", c=4), in_=ptc[:, 0:256].rearrange("p (b m) -> p b m", b=8)),
                      r=[ptc], w=[PT])
                T.act(lambda e, ptc=ptc: e.copy(out=PTn[:], in_=ptc[0:8, 256:512].rearrange("p (b m) -> p b m", b=8)), r=[ptc], w=[PTn])
                po = pu()
                for bl in range(8):
                    b = hb * 8 + bl
                    T.pe(lambda e, b=b, bl=bl, g=g, po=po: e.matmul(po[0:32, bl * 64:(bl + 1) * 64], lhsT=PT[:, bl // 4, (bl % 4) * 32:(bl % 4 + 1) * 32],
                                                                   rhs=cvb[:, b, g * 64:(g + 1) * 64], start=True, stop=False), r=[PT, cvb], w=[po])
                    T.pe(lambda e, b=b, bl=bl, g=g, po=po: e.matmul(po[0:32, bl * 64:(bl + 1) * 64], lhsT=PTn[:, bl, :],
                                                                   rhs=vn[:, b, g * 64:(g + 1) * 64], start=False, stop=True), r=[PTn, vn], w=[po])
                T.act(lambda e, g=g, hb=hb, po=po: e.copy(out=attn_s[:, g, hb * 8:(hb + 1) * 8, :], in_=po[0:32, 0:512].rearrange("p (b d) -> p b d", b=8)),
                      r=[po], w=[attn_s])
        scrbuf = Buf("scr_dram", None)
        for i in range(4):
            T.dma("sp", scr[:, :, :, i, :].rearrange("b t g d -> t g b d"), attn_s[i * 8:(i + 1) * 8, :, :, :], r=[attn_s], w=[scrbuf], key="G_scrw")
        T.dma("sp", mix[:, 0:512], scr.rearrange("b t g i d -> (b t) (g i d)"), r=[scrbuf], w=[mix], key="scrr")

    def post_norm_residual(ps, xt, grep):
        sx = stb.cur if isinstance(stb, Rot) else stb
        T.act(lambda e: e.activation(out=tmp[:], in_=ps[:], func=AF.Square, accum_out=sx[:, 2:3]), r=[ps], w=[sx, tmp])
        rstd_from_ss(sx[:, 2:3], D, sx[:, 3:4], [sx], [sx])
        T.dve(lambda e: e.scalar_tensor_tensor(out=tmp[:], in0=ps[:], scalar=sx[:, 3:4], in1=grep[:], op0=ALU.mult, op1=ALU.mult),
              r=[ps, sx, grep], w=[tmp])
        T.dve(lambda e: e.tensor_tensor(out=xt[:], in0=tmp[:], in1=xt[:], op=ALU.add), r=[tmp, xt], w=[xt])

    def inter_prompt(po, h):
        p = h // 2
        T.pe(lambda e: e.matmul(po[:, h * 128:(h + 1) * 128], lhsT=qdTz[:, h, :], rhs=Sb[:, p, :], start=False, stop=True),
             r=[qdTz, Sb], w=[po])

    ybuf = [Buf(f"ydram{i}", None) for i in range(NT + 1)]
    T.persist.extend(ybuf)

    def main_front(i, is_sample):
        slot = i % 3
        xsrc = xs if is_sample else xp[i * 128:(i + 1) * 128, :]
        cssrc = cs_s if is_sample else cs_p[i * 128:(i + 1) * 128, :]
        cst = csb[i % 2]
        T.dma("sp", cst[:], cssrc, w=[cst], key=f"cs{i % 2}")
        front(xsrc, slot, gpre, hT, hT[:])
        cur = i % 3
        psA = pu()
        proj(psA[:, 0:512], CQ, 512, psA)
        T.act(lambda e: e.copy(out=qk[:, 0:8, :], in_=psA[:, 0:512].rearrange("p (h d) -> p h d", h=8)), r=[psA], w=[qk])
        psB = pu()
        proj(psB[:, 0:512], CK, 512, psB)
        rope_kv(psB, cst, cur)
        T.dve(lambda e: e.tensor_copy(out=gq_sb[:], in_=psB[:, 256:512]), r=[psB], w=[gq_sb])
        rope(cst, 0, 10)
        k_transpose(cur)
        psC = pu()
        proj(psC[:, 0:272], CGK, 272, psC)
        proj(psC[:, 512:1024], CGV, 512, psC)
        T.act(lambda e: e.copy(out=gvb[:], in_=psC[:, 512:1024]), r=[psC], w=[gvb])
        evac_gate(psC)
        psE = pu()
        proj(psE[:, 0:512], CGR, 512, psE)
        gate_out_prep(psE)
        if is_sample:
            gate_common(ltri_s, utri_s, full=True)
        else:
            gate_common(ltri, utri, full=True)
            ablk_prompt()

    def main_back(i, is_sample):
        slot = i % 3
        xt = xbuf[slot]
        cur, prev = i % 3, (i - 1) % 3
        if not is_sample:
            attention_prompt(cur, prev, mask0b if i == 0 else maskb)
            po = gla_intra_and_out(cm, inter_prompt)
            gla_finish(po)
            state_update_prompt()
        else:
            attention_sample(cur)
            po = gla_intra_and_out(cm_s, None)
            gla_finish(po, banked=True)
        if (not is_sample) and i == NT - 1:
            T.dma("sp", kw_p, qkr[:, 8:10, :].rearrange("p h d -> p (h d)"), r=[qkr], key="o_kw")
            T.dma("sp", vw_p, vf[:], r=[vf], key="o_vw")
            T.dma("sp", gl_p, S[:], r=[S], key="o_gl")
        if T.seg is not None and T.seg[0] == "B":
            cnt["force_t"] = 0
        pt = ptb()
        for k in range(8):
            T.pe(lambda e, k=k: e.transpose(out=pt[:, k * 128:(k + 1) * 128], in_=mix[:, k * 128:(k + 1) * 128], identity=idb[:]),
                 r=[mix, idb], w=[pt])
        T.act(lambda e: e.copy(out=mixT[:].rearrange("p a t -> p (a t)"), in_=pt[:]), r=[pt], w=[mixT])
        pm = pu()
        for n in range(2):
            for k in range(8):
                T.pe(lambda e, n=n, k=k: e.matmul(pm[:, n * 512:(n + 1) * 512], lhsT=mixT[:, k, :], rhs=Wout[:, k, n * 512:(n + 1) * 512],
                                                  start=(k == 0), stop=(k == 7)), r=[mixT, Wout], w=[pm])
        post_norm_residual(pm, xt, gpost)
        ydst = y_s if is_sample else y_p[i * 128:(i + 1) * 128, :]
        T.dma("sp", ydst, xt[:], r=[xt], w=[ybuf[i]], key=f"x1o{slot}")
        norm_T(xt, gfpre, h2T, h2T[:, :, i * 128:(i + 1) * 128], tail=True)
        cnt["force_t"] = None

    ROT["p"] = 0
    if not _CACHE.get('no_sample'):
        T.seg = ("S", 0)
        sample_cache_prep()
        main_front(NT, True)
        main_back(NT, True)
        T.seg = None
        bar0 = T.seal()
        T.emit()
        T.pending_bar = {e: bar0 for e in T.engs}
    esS.close()
    esA2 = contextlib.ExitStack()
    A2 = lambda name, shape, dt: _sb(name + "_b", shape, dt, scope=esA2)
    def dup(bf):
        return Rot([bf, A2(bf.name, list(bf.t.shape), bf.t.dtype)])
    hb = dup(hb); hT = dup(hT); st = dup(st); stb = dup(stb); qk = dup(qk); qkr = dup(qkr); qkb = dup(qkb); rt = [dup(x) for x in rt]
    vf = dup(vf); qT = dup(qT); ast = dup(ast); mix = dup(mix); mixT = dup(mixT)
    glr_sb = dup(glr_sb); gk_sb = dup(gk_sb); gq_sb = dup(gq_sb); glrT = dup(glrT)
    az = dup(az); ez = dup(ez); la = dup(la); eB = dup(eB); eNB = dup(eNB); eR = dup(eR); ablk = dup(ablk)
    qd = dup(qd); ki = dup(ki); kd = dup(kd); qdT = dup(qdT); qdTz = dup(qdTz); kiTz = dup(kiTz); ATm = dup(ATm)
    gvb = dup(gvb); sil = dup(sil); t1 = dup(t1); gst = dup(gst)
    ROT["p"] = 1
    T.dve(lambda e: e.memset(glrT[:], 1.0), w=[glrT])
    T.dve(lambda e: e.memset(qdTz[:], 0.0), w=[qdTz])
    T.dve(lambda e: e.memset(kiTz[:], 0.0), w=[kiTz])
    def stage(kind, n, par, fn, *args):
        T.seg = (kind, n)
        ROT["p"] = par
        fn(*args)
        T.seg = None

    stage("F", 0, 0, hist_front, 0)
    for i in range(NH):
        if i + 1 < NH:
            stage("F", i + 1, i + 1, hist_front, i + 1)
        else:
            stage("F", NH, NH, main_front, 0, False)
        stage("B", i, i, hist_back, i)
    for i in range(NT):
        if i + 1 < NT:
            stage("F", NH + i + 1, NH + i + 1, main_front, i + 1, False)
        stage("B", NH + i, NH + i, main_back, i, False)
    ROT["p"] = 0
    bar = T.seal()
    T.emit()
    if _CACHE.get('stop_after_A'):
        T.emit(final=True)
        return nc, es
    esA2.close()
    esA.close()
    st = st.bufs[0]
    stb = stb.bufs[0]
    ROT["p"] = 0

    class _F32View:
        def __init__(self, t):
            self.t = t
        def __getitem__(self, k):
            return self.t[:].bitcast(F32)[k]
    cnt["extra_small"] = []
    for pb_ in PTb:
        vb_ = Buf(pb_.name + "_f32", _F32View(pb_.t))
        vb_.psum = True
        T.persist.append(vb_)
        cnt["extra_small"].append(vb_)

    Wup = [T.sb(f"Wup{j}", [128, 8, 512], BF16) for j in range(8)]
    Wdn = [T.sb(f"Wdn{j}", [128, 4, D], BF16) for j in range(8)]
    u2T = T.sb("u2T", [128, 32, 256], BF16)
    ur = [T.sb(f"ur{i}", [128, 512], F32) for i in range(2)]
    gfpost = T.sb("gfpost", [128, D], F32)
    T.dma("sp", gfpost[:], c_gfpost, w=[gfpost], key="G_gfp", extra=bar)
    for j in range(8):
        T.dma("pool", Wup[j][:], w_up[:, j * 512:(j + 1) * 512].rearrange("(k p) f -> p k f", p=128), w=[Wup[j]], key=f"wup{j}",
              extra=(bar if j == 0 else ()))
    for j in range(8):
        T.dma("pool", Wdn[j][:], w_down[j * 512:(j + 1) * 512, :].rearrange("(c p) n -> p c n", p=128), w=[Wdn[j]], key=f"wdn{j}")
    ntiles = NT + 1
    t0 = 0
    nev = 0
    while t0 < ntiles:
        nt = min(2, ntiles - t0)
        ntok = nt * 128
        for c2 in range(16):
            pu_ = pu(small=True)
            for cc in range(2):
                c = c2 * 2 + cc
                for k in range(8):
                    T.pe(lambda e, c=c, cc=cc, k=k, pu_=pu_, t0=t0, ntok=ntok: e.matmul(pu_[:, cc * 256:cc * 256 + ntok], lhsT=Wup[c // 4][:, k, (c % 4) * 128:(c % 4 + 1) * 128],
                                                                     rhs=h2T[:, k, t0 * 128:t0 * 128 + ntok], start=(k == 0), stop=(k == 7)),
                         r=[Wup[c // 4], h2T], w=[pu_])
            urb = ur[nev % 2]
            nev += 1
            src = pu_[:, 0:512].rearrange("p (c t) -> p c t", c=2)[:, :, 0:ntok]
            dstv = urb[:].rearrange("p (c t) -> p c t", c=2)[:, :, 0:ntok]
            T.act(lambda e, src=src, dstv=dstv: e.activation(out=dstv, in_=src, func=AF.Relu), r=[pu_], w=[urb])
            T.dve(lambda e, dstv=dstv, c2=c2, ntok=ntok: e.tensor_tensor(out=u2T[:, c2 * 2:c2 * 2 + 2, 0:ntok], in0=dstv, in1=dstv, op=ALU.mult),
                   r=[urb], w=[u2T])
        for tl in range(nt):
            ti = t0 + tl
            slot = ti % 2
            xt = xbuf[slot]
            ysrc = y_s if ti == NT else y_p[ti * 128:(ti + 1) * 128, :]
            T.dma("sp", xt[:], ysrc, r=[ybuf[ti]], w=[xt], key=f"x{slot}")
            pf = pu()
            for n in range(2):
                for c in range(32):
                    T.pe(lambda e, n=n, c=c, tl=tl, pf=pf: e.matmul(pf[:, n * 512:(n + 1) * 512], lhsT=u2T[:, c, tl * 128:(tl + 1) * 128],
                                                                   rhs=Wdn[c // 4][:, c % 4, n * 512:(n + 1) * 512], start=(c == 0), stop=(c == 31)),
                         r=[u2T, Wdn[c // 4]], w=[pf])
            post_norm_residual(pf, xt, gfpost)
            T.dma("sp", ysrc, xt[:], r=[xt], w=[ybuf[ti]], key=f"yo{slot}")
        t0 += nt

    T.emit(final=True)
    return nc, es


_CACHE = {}


def _consts():
    idf = np.eye(128, dtype=np.float32)
    s = np.arange(128)
    ltri = (s[:, None] <= s[None, :]).astype(np.float32) / 16.0
    utri = (s[:, None] > s[None, :]).astype(np.float32) / 16.0
    same = (s[:, None] // 8) == (s[None, :] // 8)
    ltri_s = ltri * same
    utri_s = utri * same
    cm = (s[:, None] <= s[None, :]).astype(np.float32)
    cm_s = cm * same
    q = np.arange(128)[:, None]
    kk = np.arange(256)[None, :]
    rel = (q + 128) - kk
    mask = np.where((rel >= 0) & (rel < 128), 0.0, NEG).astype(np.float32)
    mask0 = mask.copy()
    mask0[:, :128] = NEG
    r = np.arange(128)
    t_of = (r % 8)[:, None]
    masks = np.zeros((128, 136), np.float32)
    jj = np.arange(128)[None, :]
    masks[:32, :128] = np.where(jj >= t_of[:32] + 1, 0.0, NEG)
    masks[:32, 128:] = np.where(np.arange(8)[None, :] <= t_of[:32], 0.0, NEG)
    rowmask = ((r[:, None] // 8) == np.arange(16)[None, :]).astype(np.float32)
    bm = ((np.arange(128)[None, :] // 8) == np.arange(16)[:, None]).astype(np.float32).reshape(1, 2048)
    extra = dict(c_masks=masks, c_rowmask=rowmask, c_seqsel=rowmask / 16.0, c_bmask=np.ascontiguousarray(np.broadcast_to(bm, (128, 2048))))
    return dict(**extra, c_idf=idf, c_ltri=ltri, c_utri=utri, c_ltri_s=ltri_s.astype(np.float32), c_utri_s=utri_s.astype(np.float32),
                c_cm=cm, c_cm_s=cm_s.astype(np.float32), c_mask=mask, c_mask0=mask0)


def _rope_table(pos):
    half = 32
    inv = (10000.0 ** (-np.arange(half, dtype=np.float32) / half)).astype(np.float32)
    ang = pos.astype(np.float32)[:, None] * inv[None, :]
    return np.concatenate([np.cos(ang), np.sin(ang)], axis=1).astype(np.float32)


def kernel(x_prompt, x_sample, cache_k, cache_v, state_gla, w_in, w_gk2, b_gk, g_gla, sinks,
           w_out, g_mix_pre, g_mix_post, g_ffn_pre, g_ffn_post, w_up, w_down):
    f = lambda a: np.ascontiguousarray(np.asarray(a, dtype=np.float32))
    x_prompt, x_sample, cache_k, cache_v, state_gla = map(f, (x_prompt, x_sample, cache_k, cache_v, state_gla))
    if "nc" not in _CACHE:
        _CACHE["nc"] = build_program()
    nc, _es = _CACHE["nc"]
    consts = _consts()
    wgk = np.zeros((32, 256), np.float32)
    wgk[0:16] = f(w_gk2)[0]
    wgk[16] = f(b_gk)[0]
    rep = lambda v, n=128: np.ascontiguousarray(np.broadcast_to(f(v).reshape(1, -1), (n, f(v).size)))
    shared = dict(
        w_in=f(w_in)[0], w_out=f(w_out)[0], w_up=f(w_up)[0], w_down=f(w_down)[0], wgk=wgk,
        c_gpre=np.ascontiguousarray(f(g_mix_pre)[0].reshape(8, 128).T), c_gfpre=np.ascontiguousarray(f(g_ffn_pre)[0].reshape(8, 128).T),
        c_gpost=rep(g_mix_post[0]), c_gfpost=rep(g_ffn_post[0]), c_ggla=rep(np.tile(f(g_gla)[0], 4)), c_sink=rep(sinks[0]),
        cs_s=_rope_table(16384 + (np.arange(128) % 8)),
        c_sinks=np.ascontiguousarray(f(sinks)[0].reshape(2, 4).T[:, None, :].repeat(8, axis=1).reshape(32, 2)), **consts)
    in_maps = []
    for c in range(NCORES):
        b, j = c // 4, c % 4
        start = j * NT * 128
        xh = np.zeros((NH * 128, D), np.float32)
        if start > 0:
            hist = x_prompt[b, max(0, start - NH * 128):start]
            xh[NH * 128 - hist.shape[0]:] = hist
        m = dict(shared)
        m.update(
            xp=x_prompt[b, start:start + NT * 128], xh=xh, xs=x_sample[16 * c:16 * (c + 1)].reshape(128, D),
            ck=cache_k[0, 16 * c:16 * (c + 1)].reshape(16, 128, 128), cv=cache_v[0, 16 * c:16 * (c + 1)].reshape(16, 128, 128),
            sg=state_gla[0, 16 * c:16 * (c + 1)],
            cs_p=_rope_table(start + np.arange(NT * 128)), cs_h=_rope_table(np.maximum(start - 128 + np.arange(128), 0)),
        )
        if j != 0:
            m["c_mask0"] = consts["c_mask"]
        in_maps.append({k: np.ascontiguousarray(v, dtype=np.float32) for k, v in m.items()})
    if _CACHE.get("prep_only"):
        return nc, in_maps
    res = run_bass_kernel_spmd(nc, in_maps, core_ids=list(range(NCORES)))
    R = res.results
    if NT != 16:
        return R
    y_prompt = np.stack([np.concatenate([R[b * 4 + j]["y_p"] for j in range(4)], axis=0) for b in range(2)])
    y_sample = np.concatenate([R[c]["y_s"] for c in range(NCORES)], axis=0).reshape(128, 8, D)
    kwp = np.stack([R[b * 4 + 3]["kw_p"].reshape(128, 2, 64) for b in range(2)])[None]
    vwp = np.stack([R[b * 4 + 3]["vw_p"].reshape(128, 2, 64) for b in range(2)])[None]
    def gl(a):
        a = a.reshape(2, 64, 2, 128)
        return np.ascontiguousarray(a.transpose(2, 0, 1, 3)).reshape(4, 64, 128)
    glp = np.stack([gl(R[b * 4 + 3]["gl_p"]) for b in range(2)])[None]
    kws = np.concatenate([R[c]["kw_s"].reshape(16, 128, 2, 64) for c in range(NCORES)], axis=0)[None]
    vws = np.concatenate([R[c]["vw_s"].reshape(16, 128, 2, 64) for c in range(NCORES)], axis=0)[None]
    gls = np.concatenate([R[c]["gl_s"] for c in range(NCORES)], axis=0)[None]
    return (y_prompt.astype(np.float32), y_sample.astype(np.float32), kwp.astype(np.float32), vwp.astype(np.float32),
            glp.astype(np.float32), kws.astype(np.float32), vws.astype(np.float32), gls.astype(np.float32))
```
